# Optimizing a Trainium2 kernel written in Bass

```python
import math
import jax, jax.numpy as jnp
from jax import lax
import numpy as np

D_MODEL = 1024
BATCH = 32
SEQ = 2048
DEPTH = 1

HEAD_DIM = 64
SB_HEADS = (D_MODEL // 2) // HEAD_DIM
DIFF_HEADS = (D_MODEL // 2) // (2 * HEAD_DIM)
SB_WIDTH = SB_HEADS * HEAD_DIM
DIFF_WIDTH = DIFF_HEADS * 2 * HEAD_DIM
MIX_WIDTH = SB_WIDTH + DIFF_WIDTH
IN_WIDTH = 3 * MIX_WIDTH
D_FF = -(-8 * D_MODEL // (3 * 256)) * 256
Q_BLOCK = 128
ROPE_THETA = 10000.0
NORM_EPS = 1e-6
SUBLN_EPS = 1e-5

kernel_name = "hybrid_stickbreak_diffattn_swiglu"


def rmsnorm(x, g, eps=NORM_EPS):
    xf = x.astype(jnp.float32)
    y = xf * lax.rsqrt(jnp.mean(xf * xf, axis=-1, keepdims=True) + eps) * g.astype(jnp.float32)
    return y.astype(x.dtype)


def rope(x, pos):
    half = HEAD_DIM // 2
    inv_freq = ROPE_THETA ** (-jnp.arange(half, dtype=jnp.float32) / half)
    ang = pos.astype(jnp.float32)[:, None] * inv_freq[None, :]
    cos, sin = jnp.cos(ang), jnp.sin(ang)
    xf = x.astype(jnp.float32)
    x1, x2 = xf[..., :half], xf[..., half:]
    return jnp.concatenate([x1 * cos - x2 * sin, x1 * sin + x2 * cos], axis=-1).astype(x.dtype)


def stick_breaking_attention(q, k, v):
    S = q.shape[2]
    scale = HEAD_DIM ** -0.5
    outs = []
    for start in range(0, S, Q_BLOCK):
        end = start + Q_BLOCK
        z = jnp.einsum('bhqd,bhkd->bhqk', q[:, :, start:end], k[:, :, :end]).astype(jnp.float32) * scale
        past = jnp.arange(end)[None, :] < jnp.arange(start, end)[:, None]
        log_beta = jax.nn.log_sigmoid(z)
        log_1m = jnp.where(past, jax.nn.log_sigmoid(-z), 0.0)
        acc = lax.cumsum(log_1m, axis=3, reverse=True) - log_1m
        w = jnp.where(past, jnp.exp(log_beta + acc), 0.0)
        outs.append(jnp.einsum('bhqk,bhkd->bhqd', w.astype(v.dtype), v[:, :, :end]))
    return jnp.concatenate(outs, axis=2)


def differential_attention(q, k, v, lam):
    S = q.shape[3]
    scale = HEAD_DIM ** -0.5
    outs = []
    for start in range(0, S, Q_BLOCK):
        end = start + Q_BLOCK
        s = jnp.einsum('bhcqd,bhckd->bhcqk', q[:, :, :, start:end], k[:, :, :, :end]).astype(jnp.float32) * scale
        causal = jnp.arange(end)[None, :] <= jnp.arange(start, end)[:, None]
        p = jax.nn.softmax(jnp.where(causal, s, -jnp.inf), axis=-1)
        a = p[:, :, 0] - lam * p[:, :, 1]
        outs.append(jnp.einsum('bhqk,bhkd->bhqd', a.astype(v.dtype), v[:, :, :end]))
    return jnp.concatenate(outs, axis=2)


def setup_inputs(seed: int = 0) -> dict:
    key = jax.random.key(seed)
    ks = jax.random.split(key, 16)
    f32 = jnp.float32
    nrm = lambda k, shape, s: jax.random.normal(k, shape, f32) * s
    return {
        "x": nrm(ks[0], (BATCH, SEQ, D_MODEL), 1.0),
        "attn_norm_g": 1.0 + nrm(ks[1], (DEPTH, D_MODEL), 0.02),
        "w_in": nrm(ks[2], (DEPTH, D_MODEL, IN_WIDTH), D_MODEL ** -0.5),
        "diff_q_norm_g": 1.0 + nrm(ks[3], (DEPTH, HEAD_DIM), 0.02),
        "diff_k_norm_g": 1.0 + nrm(ks[4], (DEPTH, HEAD_DIM), 0.02),
        "lambda_q1": nrm(ks[5], (DEPTH, HEAD_DIM), 0.1),
        "lambda_k1": nrm(ks[6], (DEPTH, HEAD_DIM), 0.1),
        "lambda_q2": nrm(ks[7], (DEPTH, HEAD_DIM), 0.1),
        "lambda_k2": nrm(ks[8], (DEPTH, HEAD_DIM), 0.1),
        "diff_subln_g": 1.0 + nrm(ks[9], (DEPTH, 2 * HEAD_DIM), 0.02),
        "w_o": nrm(ks[10], (DEPTH, MIX_WIDTH, D_MODEL), MIX_WIDTH ** -0.5),
        "ffn_norm_g": 1.0 + nrm(ks[11], (DEPTH, D_MODEL), 0.02),
        "w_gate": nrm(ks[12], (DEPTH, D_MODEL, D_FF), D_MODEL ** -0.5),
        "w_up": nrm(ks[13], (DEPTH, D_MODEL, D_FF), D_MODEL ** -0.5),
        "w_down": nrm(ks[14], (DEPTH, D_FF, D_MODEL), D_FF ** -0.5),
    }


def reference(x, attn_norm_g, w_in, diff_q_norm_g, diff_k_norm_g, lambda_q1, lambda_k1,
              lambda_q2, lambda_k2, diff_subln_g, w_o, ffn_norm_g, w_gate, w_up, w_down):
    B, S, _ = x.shape
    pos = jnp.arange(S, dtype=jnp.int32)
    for layer in range(DEPTH):
        lambda_init = 0.8 - 0.6 * math.exp(-0.3 * layer)
        h = rmsnorm(x, attn_norm_g[layer])
        proj = h @ w_in[layer]
        sb_q, sb_k, sb_v, d_q, d_k, d_v = jnp.split(
            proj, [SB_WIDTH, 2 * SB_WIDTH, 3 * SB_WIDTH,
                   3 * SB_WIDTH + DIFF_WIDTH, 3 * SB_WIDTH + 2 * DIFF_WIDTH], axis=-1)

        to_heads = lambda t: t.reshape(B, S, SB_HEADS, HEAD_DIM).transpose(0, 2, 1, 3)
        sb_out = stick_breaking_attention(to_heads(sb_q), to_heads(sb_k), to_heads(sb_v))
        sb_out = sb_out.transpose(0, 2, 1, 3).reshape(B, S, SB_WIDTH)

        to_pair = lambda t: t.reshape(B, S, DIFF_HEADS, 2, HEAD_DIM).transpose(0, 2, 3, 1, 4)
        dq = rope(rmsnorm(to_pair(d_q), diff_q_norm_g[layer]), pos)
        dk = rope(rmsnorm(to_pair(d_k), diff_k_norm_g[layer]), pos)
        dv = d_v.reshape(B, S, DIFF_HEADS, 2 * HEAD_DIM).transpose(0, 2, 1, 3)
        lam = (jnp.exp(jnp.sum(lambda_q1[layer].astype(jnp.float32) * lambda_k1[layer].astype(jnp.float32)))
               - jnp.exp(jnp.sum(lambda_q2[layer].astype(jnp.float32) * lambda_k2[layer].astype(jnp.float32)))
               + lambda_init)
        d_out = differential_attention(dq, dk, dv, lam)
        d_out = rmsnorm(d_out, diff_subln_g[layer], SUBLN_EPS) * (1.0 - lambda_init)
        d_out = d_out.transpose(0, 2, 1, 3).reshape(B, S, DIFF_WIDTH)

        mix = jnp.concatenate([sb_out, d_out.astype(sb_out.dtype)], axis=-1)
        x = x + mix @ w_o[layer]

        h2 = rmsnorm(x, ffn_norm_g[layer])
        x = x + (jax.nn.silu(h2 @ w_gate[layer]) * (h2 @ w_up[layer])) @ w_down[layer]
    return x
```

```python
import math
from contextlib import ExitStack

import numpy as np
import concourse.bass as bass
import concourse.mybir as mybir
from concourse.bass_utils import run_bass_kernel_spmd

F32 = mybir.dt.float32
BF16 = mybir.dt.bfloat16
AF = mybir.ActivationFunctionType
ALU = mybir.AluOpType
AX = mybir.AxisListType

NCORES = 8
D = 1024
S = 2048
BATCH = 32
NSEQ = BATCH // NCORES
DFF = 2816
NF = DFF // 128
NFH = NF // 2
HD = 64
NT = S // 128
NJ = S // 512
TT = 1024
NEG = -30000.0
NDUMMY = 4
LAMBDA_INIT = 0.8 - 0.6 * math.exp(-0.3 * 0)

SP_AG, SP_FG, SP_GQ, SP_GQS, SP_GK, SP_GKS, SP_L = 0, 1024, 2048, 2112, 2176, 2240, 2304
SP_N = 2304 + 256


class Tok:
    __slots__ = ("sem", "val", "eng")

    def __init__(self, sem, val, eng):
        self.sem, self.val, self.eng = sem, val, eng


class Prog:
    def __init__(self, nc, es):
        self.nc = nc
        self.E = {"pe": nc.tensor, "act": nc.scalar, "dve": nc.vector, "pool": nc.gpsimd, "sp": nc.sync}
        self.sem = {e: es.enter_context(nc.semaphore("s_" + e)) for e in self.E}
        self.cnt = {e: 0 for e in self.E}
        self.waited = {e: {} for e in self.E}
        self.last_w = {}
        self.readers = {}
        self.pending = {e: [] for e in self.E}
        self.dma_sem = {}
        self.dma_cnt = {}
        self.es = es
        self.out_toks = []
        self.all_dma = []

    def _wait(self, e, toks):
        best = {}
        for t in toks:
            if t is None:
                continue
            assert t.val is not None, "dependency on unsignaled op"
            if t.val > best.get(t.sem, 0):
                best[t.sem] = t.val
        for sname, v in best.items():
            if v > self.waited[e].get(sname, 0):
                self.E[e].wait_ge(self._semh(sname), v)
                self.waited[e][sname] = v

    def _semh(self, sname):
        return self.sem[sname] if sname in self.sem else self.dma_sem[sname]

    def _deps(self, e, reads, writes, is_dma=False, slot=None):
        deps = []
        for k in reads:
            t = self.last_w.get(k)
            if t is not None:
                if not (t.eng == e and e == "pe"):
                    deps.append(t)
            if len(k) == 2 and k[0] == "b":
                for r in self.readers.get(k, ()):
                    if r.eng != e:
                        deps.append(r)
        for k in writes:
            t = self.last_w.get(k)
            same_ok = (not is_dma) and e != "pool"
            if t is not None and not (t.eng == e and same_ok) and not (is_dma and t.sem == slot):
                deps.append(t)
            for r in self.readers.get(k, ()):
                if r.eng == e and same_ok:
                    continue
                deps.append(r)
        return deps

    def op(self, e, fn, reads=(), writes=(), signal=True):
        self._wait(e, self._deps(e, reads, writes))
        inst = fn(self.E[e])
        tok = Tok(e, None, e)
        if signal:
            inst.then_inc(self.sem[e], 1)
            self.cnt[e] += 1
            tok.val = self.cnt[e]
            for p in self.pending[e]:
                p.val = self.cnt[e]
            self.pending[e] = []
        else:
            self.pending[e].append(tok)
        for k in reads:
            self.readers.setdefault(k, []).append(tok)
        for k in writes:
            self.last_w[k] = tok
            self.readers[k] = []
        return tok

    def dma(self, q, out, in_, reads=(), writes=(), slot="d", is_out=False):
        sname = "dma_" + slot
        if sname not in self.dma_sem:
            self.dma_sem[sname] = self.es.enter_context(self.nc.semaphore(sname))
            self.dma_cnt[sname] = 0
        self._wait(q, self._deps(q, reads, writes, is_dma=True, slot=sname))
        self.E[q].dma_start(out=out, in_=in_).then_inc(self.dma_sem[sname], 16)
        self.dma_cnt[sname] += 16
        tok = Tok(sname, self.dma_cnt[sname], None)
        for k in reads:
            self.readers.setdefault(k, []).append(tok)
        for k in writes:
            self.last_w[k] = tok
            self.readers[k] = []
        if is_out:
            self.out_toks.append(tok)
        self.all_dma.append(tok)
        return tok

    def barrier(self):
        for e in self.E:
            assert not self.pending[e], "unsignaled tail on " + e
        toks = [Tok(e, self.cnt[e], e) for e in self.E if self.cnt[e] > 0]
        toks += [Tok(s, c, None) for s, c in self.dma_cnt.items()]
        for e in self.E:
            self._wait(e, [t for t in toks if t.eng != e])

    def finish(self):
        self._wait("sp", self.out_toks)


class Arena:
    log = []

    def __init__(self, t, nelem):
        self.t, self.n, self.off = t, nelem, 0

    def reset(self, off=0):
        self.off = off

    def alloc(self, free_shape, dt):
        n = int(np.prod(free_shape))
        sz = n * (2 if dt == F32 else 1)
        self.off = (self.off + 15) // 16 * 16
        a = self.t[:, self.off:self.off + sz]
        Arena.log.append((self.off, sz, str(dt), tuple(free_shape)))
        self.off += sz
        assert self.off <= self.n, ("arena overflow", self.off, self.n)
        if dt == F32:
            a = a.bitcast(F32)
        if len(free_shape) == 2:
            a = a.rearrange("p (a b) -> p a b", a=free_shape[0])
        elif len(free_shape) == 3:
            a = a.rearrange("p (a b c) -> p a b c", a=free_shape[0], b=free_shape[1])
        elif len(free_shape) == 4:
            a = a.rearrange("p (a b c d) -> p a b c d", a=free_shape[0], b=free_shape[1], c=free_shape[2])
        return a


def build_program(nseq=NSEQ):
    nc = bass.Bass("TRN2", target_bir_lowering=False)
    ntok = nseq * S
    xd = nc.dram_tensor("x", [ntok, D], F32, kind="ExternalInput").ap()
    wing = nc.dram_tensor("wing", [8, 128, 8 * 384], F32, kind="ExternalInput").ap()
    wod = nc.dram_tensor("wo", [128, 8 * 1024], F32, kind="ExternalInput").ap()
    wgud = nc.dram_tensor("wgu", [NF, 128, 2 * 8 * 128], F32, kind="ExternalInput").ap()
    wdd = nc.dram_tensor("wd", [NF, 128, 1024], F32, kind="ExternalInput").ap()
    smalld = nc.dram_tensor("small", [128, SP_N], F32, kind="ExternalInput").ap()
    sublnd = nc.dram_tensor("subln", [128, 1], F32, kind="ExternalInput").ap()
    cmatd = nc.dram_tensor("cmat", [128, 9 * 128], F32, kind="ExternalInput").ap()
    roped = nc.dram_tensor("rope", [128, 2 * NT * 64], F32, kind="ExternalInput").ap()
    yd = nc.dram_tensor("y", [ntok, D], F32, kind="ExternalOutput").ap()
    scrd = nc.dram_tensor("bc_scratch", [2, 2, 512], F32, kind="Internal").ap()

    with ExitStack() as es:
        P = Prog(nc, es)
        sb = lambda name, shape, dt: es.enter_context(nc.sbuf_tensor(name, shape, dt))
        ps = es.enter_context(nc.psum_tensor("ps", [128, 8, 512], F32))
        wo = sb("wo_sb", [128, 8, 1024], BF16)
        mixT = sb("mixT", [128, 8, S], BF16)
        small = sb("small_sb", [128, 2048], F32)
        cmat = sb("cmat_sb", [128, 9, 128], BF16)
        rtab = sb("rtab", [128, NT, 2, 2, 64], F32)
        misc = sb("misc", [128, 16], F32)
        zero_bf = sb("zero_bf", [128, 512], BF16)
        subg = sb("subg", [128, 1], F32)
        ltmp = sb("ltmp", [128, 2, 64], F32)
        ARENA = (int(nc.sbuf_bytes_remaining) - 2048) // 64 * 32
        arena_t = sb("arena", [128, ARENA], BF16)
        AR = Arena(arena_t, ARENA)
        ropeT = AR.alloc([2, NT, 64], F32)
        small2 = AR.alloc([SP_N - 2048], F32)

        ident = cmat[:, 0, :]
        L1 = cmat[:, 1, :]
        L2 = cmat[:, 2, :]
        NEGS = cmat[:, 3, :]
        NEGI = cmat[:, 4, :]
        ONES = cmat[:, 5, :]
        C32 = cmat[:, 6, :]
        C128 = cmat[:, 7, :]
        M01 = cmat[:, 8, :]
        EPS6, EPS5, ONE, LAM, NLAM = (misc[:, i:i + 1] for i in range(5))
        ps_bf7 = ps[:, 7, :].bitcast(BF16)

        def bank(b):
            return "b%d" % b

        P.dma("pool", cmat[:].rearrange("p a b -> p (a b)"), cmatd, writes=["cmat"], slot="c0")
        P.dma("sp", small[:], smalld[:, 0:2048], writes=["small"], slot="c1")
        P.dma("sp", small2, smalld[:, 2048:SP_N], writes=["small2"], slot="c5")
        P.dma("sp", ropeT.rearrange("p a t d -> p (a t d)"), roped, writes=["rope"], slot="c2")
        P.dma("sp", subg[:], sublnd, writes=["subg"], slot="c3")
        P.dma("pool", wo[:].rearrange("p a b -> p (a b)"), wod, writes=["wo"], slot="c4")
        P.op("dve", lambda v: v.memset(misc[:, 0:1], 1e-6), writes=["misc"])
        P.op("dve", lambda v: v.memset(misc[:, 1:2], 1e-5), writes=["misc"])
        P.op("dve", lambda v: v.memset(misc[:, 2:3], 1.0), writes=["misc"])
        P.op("dve", lambda v: v.memset(zero_bf[:], 0.0), writes=["zero"])
        P.op("dve", lambda v: v.tensor_scalar(out=subg[:], in0=subg[:], scalar1=1.0 - LAMBDA_INIT, scalar2=None,
                                              op0=ALU.mult), reads=["subg"], writes=["subg"])
        lv = lambda i: small2[:, SP_L - 2048 + 64 * i: SP_L - 2048 + 64 * (i + 1)]
        P.op("dve", lambda v: v.tensor_tensor(out=ltmp[:, 0, :], in0=lv(0), in1=lv(1), op=ALU.mult),
             reads=["small2"], writes=["ltmp"])
        P.op("dve", lambda v: v.tensor_tensor(out=ltmp[:, 1, :], in0=lv(2), in1=lv(3), op=ALU.mult),
             reads=["small2"], writes=["ltmp"])
        P.op("dve", lambda v: v.tensor_reduce(out=misc[:, 8:10], in_=ltmp[:], axis=AX.X, op=ALU.add),
             reads=["ltmp"], writes=["misc"])
        P.op("act", lambda a: a.activation(out=misc[:, 10:12], in_=misc[:, 8:10], func=AF.Exp),
             reads=["misc"], writes=["misc2"])
        P.op("dve", lambda v: v.tensor_tensor(out=misc[:, 12:13], in0=misc[:, 10:11], in1=misc[:, 11:12],
                                              op=ALU.subtract), reads=["misc2"], writes=["misc3"])
        P.op("dve", lambda v: v.tensor_scalar(out=misc[:, 3:4], in0=misc[:, 12:13], scalar1=LAMBDA_INIT, scalar2=None,
                                              op0=ALU.add), reads=["misc3"], writes=["misc4"])
        P.op("dve", lambda v: v.tensor_scalar(out=misc[:, 4:5], in0=misc[:, 3:4], scalar1=-1.0, scalar2=None,
                                              op0=ALU.mult), reads=["misc4"], writes=["misc5"])
        for qk, (go, gso) in enumerate(((SP_GQ, SP_GQS), (SP_GK, SP_GKS))):
            gb = small2[:, go - 2048:go - 2048 + 64].unsqueeze(1).broadcast_to([128, NT, 64])
            gsb = small2[:, gso - 2048:gso - 2048 + 64].unsqueeze(1).broadcast_to([128, NT, 64])
            P.op("pool", lambda g, gb=gb, qk=qk: g.tensor_tensor(out=rtab[:, :, qk, 0, :], in0=ropeT[:, 0, :, :],
                                                                 in1=gb, op=ALU.mult),
                 reads=["small2", "rope"], writes=["rtab"])
            P.op("pool", lambda g, gsb=gsb, qk=qk: g.tensor_tensor(out=rtab[:, :, qk, 1, :], in0=ropeT[:, 1, :, :],
                                                                   in1=gsb, op=ALU.mult),
                 reads=["small2", "rope"], writes=["rtab"])

        P.barrier()

        def rstd_from_ss(ss_ap, out_ap, n, eps_ap, rk, wk):
            P.op("act", lambda a: a.activation(out=out_ap, in_=ss_ap, func=AF.Ln, bias=eps_ap, scale=1.0 / n),
                 reads=[rk, "misc"], writes=[wk])
            P.op("act", lambda a: a.activation(out=out_ap, in_=out_ap, func=AF.Exp, scale=-0.5),
                 reads=[wk], writes=[wk])

        for b in range(nseq):
            row0 = b * S
            AR.reset()
            hT = AR.alloc([8, S], BF16)
            stat = AR.alloc([NT, 4], F32)
            wgb = [AR.alloc([8, 384], BF16) for _ in range(2)]
            qkT = [AR.alloc([2, S], BF16) for _ in range(2)]
            vtok = [AR.alloc([NT, 128], BF16) for _ in range(2)]
            Eb = [AR.alloc([2, 512], F32) for _ in range(3)]
            xs = [e_.rearrange("p a b -> p (a b)") for e_ in Eb]
            SQb = [AR.alloc([2, 512], BF16) for _ in range(2)]
            xn = [q_.rearrange("p a b -> p (a b)") for q_ in SQb]
            Fb = [AR.alloc([2, 512], F32) for _ in range(2)]
            Wb = [AR.alloc([2, 512], BF16) for _ in range(3)]
            sqj = Wb[0].rearrange("p a b -> p (a b)")
            qk32 = [AR.alloc([256], F32) for _ in range(2)]
            sqs = [AR.alloc([256], F32) for _ in range(2)]
            rt1 = [AR.alloc([256], F32) for _ in range(4)]
            rt2 = AR.alloc([256], F32)
            qkb = [AR.alloc([256], BF16) for _ in range(4)]
            dstat = [AR.alloc([8], F32) for _ in range(4)]
            Os = [AR.alloc([2, 512], F32) for _ in range(2)]
            RSs = [AR.alloc([512], F32) for _ in range(2)]
            fin = [AR.alloc([512], F32) for _ in range(3)]
            hib = AR.alloc([512], BF16)
            lob = AR.alloc([512], BF16)
            pfx = "s%d_" % b
            K = lambda *a: pfx + "_".join(str(x) for x in a)

            def phaseA_evac(i):
                for c in range(8):
                    P.op("pe", lambda t, c=c, i=i: t.transpose(ps_bf7[:, c * 128:(c + 1) * 128],
                                                               xn[i % 2][:, c * 128:(c + 1) * 128], ident),
                         reads=[K("SQ", i % 2), "cmat"], writes=[bank(7)], signal=(c == 7))
                if i % 2 == 0:
                    P.op("act", lambda a, i=i: a.activation(
                        out=hT[:, :, i * 128:(i + 1) * 128],
                        in_=ps_bf7.rearrange("p (c n) -> p c n", c=8), func=AF.Copy),
                         reads=[bank(7)], writes=[K("hT", i)])
                else:
                    P.op("dve", lambda v, i=i: v.tensor_copy(
                        out=hT[:, :, i * 128:(i + 1) * 128],
                        in_=ps_bf7.rearrange("p (c n) -> p c n", c=8)),
                         reads=[bank(7)], writes=[K("hT", i)])

            for i in range(NT):
                P.dma("sp", xs[i % 2], xd[row0 + i * 128: row0 + (i + 1) * 128, :], writes=[K("E", i % 2)],
                      slot="xs%d" % (i % 2))
                P.op("act", lambda a, i=i: a.activation(out=sqj, in_=xs[i % 2], func=AF.Square,
                                                        accum_out=stat[:, i, 0:1]),
                     reads=[K("E", i % 2)], writes=[K("W", 0), K("stat", i)])
                rstd_from_ss(stat[:, i, 0:1], stat[:, i, 1:2], D, EPS6, K("stat", i), K("rstd", i))
                P.op("dve", lambda v, i=i: v.scalar_tensor_tensor(
                    out=xn[i % 2], in0=xs[i % 2], scalar=stat[:, i, 1:2], in1=small[:, SP_AG:SP_AG + 1024],
                    op0=ALU.mult, op1=ALU.mult),
                     reads=[K("E", i % 2), K("rstd", i), "small"], writes=[K("SQ", i % 2)])
                if i >= 1:
                    phaseA_evac(i - 1)
            phaseA_evac(NT - 1)

            def load_wg(g, par):
                P.dma("pool", wgb[par].rearrange("p a b -> p (a b)"), wing[g], writes=[K("wg", par)],
                      slot="wg%d" % par)

            b7 = {"req": False, "clean": True}

            def run_stages(n_items, stages):
                done = {st[0]: 0 for st in stages}
                b7w = [(st[0], st[3]) for st in stages if st[3]]
                while any(done[st[0]] < n_items for st in stages):
                    snap = dict(done)
                    for name, fn, prods, rd7, limits in stages:
                        i = done[name]
                        if i >= n_items or any(snap[p] <= i for p in prods):
                            continue
                        if any(i - done[c] >= d for c, d in limits):
                            continue
                        if rd7 and (b7["req"] or done[rd7] < i):
                            continue
                        fn(i)
                        done[name] += 1
                    b7["clean"] = all(done[r] == done[w] for w, r in b7w)
                    yield
                b7["clean"] = True

            def inproj_sb(g, par, atomic):
                wgk = K("wg", par)
                units = []
                for j in range(NJ):
                    units += [("qk", j, 0), ("qk", j, 1), ("v", j, 0)]

                def mm(u):
                    kind, j, which = units[u]
                    hk = [K("hT", 4 * j + t) for t in range(4)]
                    if kind == "qk":
                        for kc in range(8):
                            P.op("pe", lambda t: t.matmul(
                                ps[:, 7, :], wgb[par][:, kc, which * 128:(which + 1) * 128],
                                hT[:, kc, j * 512:(j + 1) * 512], start=(kc == 0), stop=(kc == 7)),
                                 reads=[wgk] + hk, writes=[bank(7)], signal=(kc == 7))
                    else:
                        for t4 in range(4):
                            for kc in range(8):
                                P.op("pe", lambda t: t.matmul(
                                    ps[:, 7, t4 * 128:(t4 + 1) * 128],
                                    hT[:, kc, (4 * j + t4) * 128:(4 * j + t4 + 1) * 128],
                                    wgb[par][:, kc, 256:384], start=(kc == 0), stop=(kc == 7)),
                                     reads=[wgk] + hk, writes=[bank(7)], signal=(kc == 7 and t4 == 3))

                def ev(u):
                    kind, j, which = units[u]
                    if kind == "qk":
                        P.op("dve", lambda v: v.tensor_copy(
                            out=qkT[par][:, which, j * 512:(j + 1) * 512], in_=ps[:, 7, :]),
                             reads=[bank(7)], writes=[K("qk", par, j)])
                    else:
                        P.op("dve", lambda v: v.tensor_copy(
                            out=vtok[par][:, 4 * j:4 * j + 4, :],
                            in_=ps[:, 7, :].rearrange("p (a n) -> p a n", a=4)),
                             reads=[bank(7)], writes=[K("v", par, j)])

                yield from run_stages(len(units), [("ev", ev, ["mm"], None, []), ("mm", mm, [], "ev", [])])

            def inproj_diff(g, par, atomic):
                wgk = K("wg", par)

                def st_mm(i):
                    for kc in range(8):
                        P.op("pe", lambda t: t.matmul(
                            ps[:, 7, 0:384], hT[:, kc, i * 128:(i + 1) * 128], wgb[par][:, kc, :],
                            start=(kc == 0), stop=(kc == 7)),
                             reads=[wgk, K("hT", i)], writes=[bank(7)], signal=(kc == 7))

                def st_ev(i):
                    k2 = i % 2
                    P.op("dve", lambda v: v.tensor_copy(out=qk32[k2], in_=ps[:, 7, 0:256]),
                         reads=[bank(7)], writes=[K("qk32", k2)])
                    P.op("dve", lambda v: v.tensor_copy(out=vtok[par][:, i, :], in_=ps[:, 7, 256:384]),
                         reads=[bank(7)], writes=[K("v", par, i // 4)])
                    P.op("dve", lambda v: v.tensor_tensor(out=sqs[k2], in0=qk32[k2], in1=qk32[k2], op=ALU.mult),
                         reads=[K("qk32", k2)], writes=[K("sqs", k2)])
                    k4 = i % 4
                    x4 = qk32[k2].rearrange("p (q m d) -> p q m d", q=2, m=2)
                    o1 = rt1[k4].rearrange("p (q m d) -> p q m d", q=2, m=2)
                    o2 = rt2.rearrange("p (q m d) -> p q m d", q=2, m=2)
                    tabA = rtab[:, i, :, 0, :].unsqueeze(2).broadcast_to([128, 2, 2, 64])
                    tabB_lo = rtab[:, i, :, 1, 0:32].unsqueeze(2).broadcast_to([128, 2, 2, 32])
                    tabB_hi = rtab[:, i, :, 1, 32:64].unsqueeze(2).broadcast_to([128, 2, 2, 32])
                    P.op("dve", lambda v: v.tensor_tensor(out=o1, in0=x4, in1=tabA, op=ALU.mult),
                         reads=[K("qk32", k2), "rtab"], writes=[K("rt1", k4)])
                    P.op("pool", lambda g_: g_.tensor_tensor(out=o2[:, :, :, 0:32], in0=x4[:, :, :, 32:64],
                                                             in1=tabB_lo, op=ALU.mult),
                         reads=[K("qk32", k2), "rtab"], writes=[K("rt2")])
                    P.op("pool", lambda g_: g_.tensor_tensor(out=o2[:, :, :, 32:64], in0=x4[:, :, :, 0:32],
                                                             in1=tabB_hi, op=ALU.mult),
                         reads=[K("qk32", k2), "rtab"], writes=[K("rt2")])

                def st_add(i):
                    k4 = i % 4
                    P.op("pool", lambda g_: g_.tensor_tensor(out=rt1[k4], in0=rt1[k4], in1=rt2, op=ALU.add),
                         reads=[K("rt1", k4), K("rt2")], writes=[K("rt1", k4)])

                def st_red(i):
                    k2, k4 = i % 2, i % 4
                    P.op("dve", lambda v: v.tensor_reduce(out=dstat[k4][:, 0:4],
                                                          in_=sqs[k2].rearrange("p (g d) -> p g d", g=4),
                                                          axis=AX.X, op=ALU.add),
                         reads=[K("sqs", k2)], writes=[K("dss", k4)])

                def st_rstd(i):
                    k4 = i % 4
                    rstd_from_ss(dstat[k4][:, 0:4], dstat[k4][:, 4:8], HD, EPS6, K("dss", k4), K("drs", k4))

                def st_fin(i):
                    k4 = i % 4
                    rb = dstat[k4][:, 4:8].unsqueeze(2).broadcast_to([128, 4, 64])
                    P.op("pool", lambda v: v.tensor_tensor(out=qkb[k4].rearrange("p (g d) -> p g d", g=4),
                                                           in0=rt1[k4].rearrange("p (g d) -> p g d", g=4),
                                                           in1=rb, op=ALU.mult),
                         reads=[K("rt1", k4), K("drs", k4)], writes=[K("qkb", k4)])

                def st_tr(i):
                    k4 = i % 4
                    for which in range(2):
                        P.op("pe", lambda t: t.transpose(
                            ps_bf7[:, 768 + which * 128: 768 + (which + 1) * 128],
                            qkb[k4][:, which * 128:(which + 1) * 128], ident),
                             reads=[K("qkb", k4), "cmat"], writes=[bank(7)], signal=(which == 1))

                def st_tev(i):
                    P.op("dve", lambda v: v.tensor_copy(
                        out=qkT[par][:, :, i * 128:(i + 1) * 128],
                        in_=ps_bf7[:, 768:1024].rearrange("p (a n) -> p a n", a=2)),
                         reads=[bank(7)], writes=[K("qk", par, i // 4)])

                yield from run_stages(NT, [
                    ("tev", st_tev, ["tr"], None, []),
                    ("tr", st_tr, ["w"], "tev", []),
                    ("w", lambda i: None, ["fin"], None, []),
                    ("fin", st_fin, ["rstd", "add"], None, [("tr", 4)]),
                    ("rstd", st_rstd, ["red"], None, []),
                    ("red", st_red, ["ev"], None, [("fin", 4)]),
                    ("add", st_add, ["ev"], None, []),
                    ("ev", st_ev, ["mm"], None, [("fin", 4), ("red", 2), ("add", 1)]),
                    ("mm", st_mm, [], "ev", []),
                ])

            bg = []

            def bg_step(n=1):
                for _ in range(n):
                    for gen in list(bg):
                        try:
                            next(gen)
                        except StopIteration:
                            bg.remove(gen)

            def bg_drain():
                while bg:
                    bg_step()

            inproj_gen = [None]
            finals = []
            os_busy = {0: False, 1: False}
            closing = [False]

            def final_worker():
                while True:
                    if finals:
                        yield from diff_final(*finals.pop(0))
                    elif closing[0]:
                        return
                    else:
                        yield

            ORDER = [0, 4, 1, 5, 2, 6, 3, 7]

            def throttle(gen, every):
                k = 0
                for _ in gen:
                    yield
                    k += 1
                    if k % every == 0:
                        yield

            def start_group(pos):
                g, par = ORDER[pos], pos % 2
                load_wg(g, par)
                during_diff = pos >= 1 and ORDER[pos - 1] >= 4
                inproj_gen[0] = inproj_sb(g, par, during_diff) if g < 4 else throttle(inproj_diff(g, par, True), 3)
                bg.append(inproj_gen[0])

            def drain_inproj():
                while inproj_gen[0] in bg:
                    bg_step()

            def sb_attention(g, par):
                q_ = qkT[par][:, 0, :]
                k_ = qkT[par][:, 1, :]
                steps = []
                for J in range(NJ):
                    nkb = 4 * J + 4
                    for kb in range(nkb - 1, -1, -1):
                        steps.append((J, kb, kb == nkb - 1, kb == 0))
                N = len(steps)

                def c0_of(J, kb):
                    return max(0, kb * 128 - 512 * J)

                def S1(i):
                    J, kb, first, last = steps[i]
                    c0 = c0_of(J, kb)
                    zb = 0
                    diag = kb >= 4 * J
                    rk = [K("qk", par, J), K("qk", par, kb // 4)]
                    for h in range(2):
                        lo, hi = h * 64, (h + 1) * 64
                        kT = k_[lo:hi, kb * 128:(kb + 1) * 128]
                        if diag:
                            P.op("pe", lambda t: t.matmul(ps[:, zb + h, c0:c0 + 128], kT,
                                                          q_[lo:hi, J * 512 + c0: J * 512 + c0 + 128],
                                                          start=True, stop=False),
                                 reads=rk, writes=[bank(zb + h)], signal=False)
                            P.op("pe", lambda t: t.matmul(ps[:, zb + h, c0:c0 + 128], ident, NEGS,
                                                          start=False, stop=True),
                                 reads=["cmat"], writes=[bank(zb + h)], signal=(h == 1 and c0 + 128 >= 512))
                            if c0 + 128 < 512:
                                P.op("pe", lambda t: t.matmul(ps[:, zb + h, c0 + 128:512], kT,
                                                              q_[lo:hi, J * 512 + c0 + 128: (J + 1) * 512],
                                                              start=True, stop=True),
                                     reads=rk, writes=[bank(zb + h)], signal=(h == 1))
                        else:
                            P.op("pe", lambda t: t.matmul(ps[:, zb + h, :], kT, q_[lo:hi, J * 512:(J + 1) * 512],
                                                          start=True, stop=True),
                                 reads=rk, writes=[bank(zb + h)], signal=(h == 1))

                def S2a(i):
                    J, kb, first, last = steps[i]
                    c0 = c0_of(J, kb)
                    zb = 0
                    P.op("act", lambda a: a.activation(out=Eb[i % 3][:, :, c0:], in_=ps[:, zb:zb + 2, c0:],
                                                       func=AF.Exp, scale=0.125),
                         reads=[bank(zb), bank(zb + 1)], writes=[K("E", i % 3)])

                def S2b(i):
                    J, kb, first, last = steps[i]
                    c0 = c0_of(J, kb)
                    P.op("act", lambda a: a.activation(out=SQb[i % 2][:, :, c0:], in_=Eb[i % 3][:, :, c0:],
                                                       func=AF.Ln, bias=ONE, scale=1.0),
                         reads=[K("E", i % 3), "misc"], writes=[K("SQ", i % 2)])

                def S3(i):
                    J, kb, first, last = steps[i]
                    c0 = c0_of(J, kb)
                    if first:
                        for h in range(2):
                            P.op("pe", lambda t: t.matmul(ps[:, 4 + h, :], L1, zero_bf[:], start=True, stop=True),
                                 reads=["cmat", "zero"], writes=[bank(4 + h)], signal=False)
                    for h in range(2):
                        P.op("pe", lambda t: t.matmul(ps[:, 4 + h, c0:], L1, SQb[i % 2][:, h, c0:],
                                                      start=False, stop=True, skip_group_check=True),
                             reads=["cmat", K("SQ", i % 2)], writes=[bank(4 + h)], signal=(h == 1))

                def S4(i):
                    J, kb, first, last = steps[i]
                    c0 = c0_of(J, kb)
                    P.op("act", lambda a: a.activation(out=Fb[i % 2][:, :, c0:], in_=ps[:, 4:6, c0:], func=AF.Exp,
                                                       scale=-1.0),
                         reads=[bank(4), bank(5)], writes=[K("F", i % 2)])

                def S5(i):
                    J, kb, first, last = steps[i]
                    c0 = c0_of(J, kb)
                    P.op("dve", lambda v: v.tensor_tensor(out=Wb[i % 3][:, :, c0:], in0=Eb[i % 3][:, :, c0:],
                                                          in1=Fb[i % 2][:, :, c0:], op=ALU.mult),
                         reads=[K("E", i % 3), K("F", i % 2)], writes=[K("W", i % 3)])

                def S6(i):
                    J, kb, first, last = steps[i]
                    c0 = c0_of(J, kb)
                    if last:
                        return
                    for h in range(2):
                        P.op("pe", lambda t: t.matmul(ps[:, 4 + h, c0:], L2, SQb[i % 2][:, h, c0:],
                                                      start=False, stop=True, skip_group_check=True),
                             reads=["cmat", K("SQ", i % 2)], writes=[bank(4 + h)], signal=(h == 1))

                def S7(i):
                    J, kb, first, last = steps[i]
                    c0 = c0_of(J, kb)
                    if first:
                        for h in range(2):
                            P.op("pe", lambda t: t.matmul(ps[h * 64:(h + 1) * 64, 6, :],
                                                          vtok[par][:, 0, h * 64:(h + 1) * 64], zero_bf[:],
                                                          start=True, stop=False),
                                 reads=["zero", K("v", par, 0)], writes=[bank(6)], signal=False)
                    for h in range(2):
                        P.op("pe", lambda t: t.matmul(ps[h * 64:(h + 1) * 64, 6, c0:],
                                                      vtok[par][:, kb, h * 64:(h + 1) * 64],
                                                      Wb[i % 3][:, h, c0:], start=False, stop=last),
                             reads=[K("v", par, kb // 4), K("W", i % 3)], writes=[bank(6)], signal=(h == 1))
                    if last:
                        P.op("dve", lambda v: v.tensor_copy(out=mixT[:, g, J * 512:(J + 1) * 512], in_=ps[:, 6, :]),
                             reads=[bank(6)], writes=[K("mix", g, J)])

                for n in range(-2, N + 1):
                    if 0 <= n + 2 < N:
                        S1(n + 2)
                    if 0 <= n < N:
                        S4(n)
                        S5(n)
                    if 0 <= n + 1 < N:
                        S2b(n + 1)
                    if 0 <= n + 2 < N:
                        S2a(n + 2)
                    if 0 <= n < N:
                        S6(n)
                    if 0 <= n + 1 < N:
                        S3(n + 1)
                    if 0 <= n - 1 < N:
                        S7(n - 1)
                    bg_step()
                    if inproj_gen[0] not in bg:
                        for _ in range(NDUMMY):
                            P.op("pe", lambda t: t.matmul(ps[:, 2, :], ident, zero_bf[:], start=True, stop=True),
                                 reads=["cmat", "zero"], writes=[bank(2)], signal=False)

            def diff_final(g, J, k2):
                o_ = Os[k2]
                rs = RSs[k2]
                f0, f1, f2 = fin
                kk = K("fin")
                hk, lk = K("hi"), K("lo")
                P.op("act", lambda a_: a_.activation(out=rs[0:64, :], in_=rs[0:64, :], func=AF.Ln),
                     reads=[K("RS", k2)], writes=[K("RS", k2)])
                P.op("act", lambda a_: a_.activation(out=rs[0:64, :], in_=rs[0:64, :], func=AF.Exp, scale=-1.0),
                     reads=[K("RS", k2)], writes=[K("RS", k2)])
                yield
                P.dma("sp", scrd[k2, 0:1, :], rs[0:1, :], reads=[K("RS", k2)], writes=["scr%d0" % k2], slot="bw0")
                P.dma("sp", scrd[k2, 1:2, :], rs[32:33, :], reads=[K("RS", k2)], writes=["scr%d1" % k2], slot="bw1")
                yield
                yield
                yield
                P.dma("sp", f1, scrd[k2, 0:1, :].broadcast_to([128, 512]), reads=["scr%d0" % k2],
                      writes=[kk + "1"], slot="bc0")
                P.dma("sp", f2, scrd[k2, 1:2, :].broadcast_to([128, 512]), reads=["scr%d1" % k2],
                      writes=[kk + "2"], slot="bc1")
                yield
                yield
                yield
                P.op("pool", lambda g_: g_.tensor_tensor(out=f1, in0=o_[:, 0, :], in1=f1, op=ALU.mult),
                     reads=[K("Os", k2), kk + "1"], writes=[kk + "1"])
                P.op("pool", lambda g_: g_.tensor_tensor(out=f2, in0=o_[:, 1, :], in1=f2, op=ALU.mult),
                     reads=[K("Os", k2), kk + "2"], writes=[kk + "2"])
                yield
                os_busy[k2] = False
                yield
                P.op("dve", lambda v: v.scalar_tensor_tensor(out=f0, in0=f2, scalar=NLAM, in1=f1, op0=ALU.mult,
                                                             op1=ALU.add),
                     reads=[kk + "1", kk + "2", "misc5"], writes=[kk + "0"])
                yield
                P.op("pool", lambda g_: g_.tensor_tensor(out=f1, in0=f0, in1=f0, op=ALU.mult),
                     reads=[kk + "0"], writes=[kk + "1"])
                P.op("pool", lambda g_: g_.tensor_copy(out=hib, in_=f1), reads=[kk + "1"], writes=[hk])
                P.op("pool", lambda g_: g_.tensor_tensor(out=lob, in0=f1, in1=hib, op=ALU.subtract),
                     reads=[kk + "1", hk], writes=[lk])
                yield
                yield
                yield
                b7["req"] = True
                while not b7["clean"]:
                    yield
                P.op("pe", lambda t: t.matmul(ps[:, 7, :], C128, hib, start=True, stop=False),
                     reads=["cmat", hk], writes=[bank(7)], signal=False)
                P.op("pe", lambda t: t.matmul(ps[:, 7, :], C128, lob, start=False, stop=True),
                     reads=["cmat", lk], writes=[bank(7)])
                P.op("dve", lambda v: v.tensor_copy(out=f2, in_=ps[:, 7, :]),
                     reads=[bank(7)], writes=[kk + "2"])
                b7["req"] = False
                yield
                yield
                P.op("act", lambda a: a.activation(out=f2, in_=f2, func=AF.Ln, bias=EPS5, scale=1.0),
                     reads=[kk + "2", "misc"], writes=[kk + "2"])
                P.op("act", lambda a: a.activation(out=f2, in_=f2, func=AF.Exp, scale=-0.5),
                     reads=[kk + "2"], writes=[kk + "2"])
                yield
                P.op("dve", lambda v: v.scalar_tensor_tensor(out=mixT[:, g, J * 512:(J + 1) * 512], in0=f0,
                                                             scalar=subg[:, 0:1], in1=f2, op0=ALU.mult, op1=ALU.mult),
                     reads=[kk + "0", kk + "2", "subg"], writes=[K("mix", g, J)])
                yield

            def diff_attention(g, par):
                q_ = qkT[par][:, 0, :]
                k_ = qkT[par][:, 1, :]
                steps = []
                for J in range(NJ):
                    nkb = 4 * J + 4
                    for kb in range(nkb):
                        steps.append((J, kb, kb == 0, kb == nkb - 1))
                N = len(steps)
                Pb = Wb

                def c0_of(J, kb):
                    return max(0, kb * 128 - 512 * J)

                def D1(i):
                    J, kb, first, last = steps[i]
                    c0 = c0_of(J, kb)
                    zb = (i % 2) * 2
                    rk = [K("qk", par, J), K("qk", par, kb // 4)]
                    for m in range(2):
                        lo, hi = m * 64, (m + 1) * 64
                        kT = k_[lo:hi, kb * 128:(kb + 1) * 128]
                        P.op("pe", lambda t: t.matmul(ps[:, zb + m, c0:], kT, q_[lo:hi, J * 512 + c0:(J + 1) * 512],
                                                      start=True, stop=True),
                             reads=rk, writes=[bank(zb + m)], signal=(m == 1))

                def D2(i):
                    J, kb, first, last = steps[i]
                    c0 = c0_of(J, kb)
                    zb = (i % 2) * 2
                    P.op("act", lambda a: a.activation(out=Pb[i % 3][:, :, c0:], in_=ps[:, zb:zb + 2, c0:],
                                                       func=AF.Exp, scale=0.125),
                         reads=[bank(zb), bank(zb + 1)], writes=[K("W", i % 3)])
                    if kb >= 4 * J:
                        pd = Pb[i % 3][:, :, c0:c0 + 128]
                        P.op("pool", lambda g_: g_.tensor_tensor(out=pd, in0=pd,
                                                                 in1=M01.unsqueeze(1).broadcast_to([128, 2, 128]),
                                                                 op=ALU.mult),
                             reads=[K("W", i % 3), "cmat"], writes=[K("W", i % 3)])

                def D3(i):
                    J, kb, first, last = steps[i]
                    c0 = c0_of(J, kb)
                    p_ = Pb[i % 3]
                    for m in range(2):
                        P.op("pe", lambda t: t.matmul(ps[:, 4 + m, c0:], vtok[par][:, kb, :], p_[:, m, c0:],
                                                      start=first, stop=last),
                             reads=[K("v", par, kb // 4), K("W", i % 3)], writes=[bank(4 + m)], signal=False)
                    for m in range(2):
                        P.op("pe", lambda t: t.matmul(ps[32 * m:32 * m + 32, 6, c0:], ONES[:, 0:32], p_[:, m, c0:],
                                                      start=first, stop=last),
                             reads=["cmat", K("W", i % 3)], writes=[bank(6)], signal=(m == 1))
                    if last:
                        k2 = J % 2
                        while os_busy[k2]:
                            bg_step()
                        P.op("act", lambda a: a.activation(out=Os[k2], in_=ps[:, 4:6, :], func=AF.Copy),
                             reads=[bank(4), bank(5)], writes=[K("Os", k2)])
                        P.op("dve", lambda v: v.tensor_copy(out=RSs[k2][0:64, :], in_=ps[0:64, 6, :]),
                             reads=[bank(6)], writes=[K("RS", k2)])
                        os_busy[k2] = True
                        finals.append((g, J, k2))

                for n in range(-1, N + 1):
                    if 0 <= n + 1 < N:
                        D1(n + 1)
                    if 0 <= n < N:
                        D2(n)
                    if 0 <= n - 1 < N:
                        D3(n - 1)
                    bg_step()

            start_group(0)
            bg_drain()
            closing[0] = False
            bg.append(final_worker())
            for pos, g in enumerate(ORDER):
                if pos + 1 < 8:
                    start_group(pos + 1)
                if g < 4:
                    sb_attention(g, pos % 2)
                else:
                    diff_attention(g, pos % 2)
                drain_inproj()
            closing[0] = True
            bg_drain()
            P.barrier()

            AR.reset()
            x1 = AR.alloc([8, 1024], F32)
            h2T = AR.alloc([8, TT], BF16)
            aT = AR.alloc([NFH, TT], BF16)
            wdb = AR.alloc([NFH, 1024], BF16)
            NRING = 3
            wgu = [AR.alloc([2, 8, 128], BF16) for _ in range(NRING)]
            xs2 = [AR.alloc([1024], F32) for _ in range(2)]
            xn2 = [AR.alloc([1024], BF16) for _ in range(2)]
            sg = [AR.alloc([2, 512], F32) for _ in range(2)]
            sqj2 = AR.alloc([1024], BF16)
            stat2 = AR.alloc([8, 4], F32)

            for tt in range(S // TT):
                tk = K
                trow = row0 + tt * TT
                def issue_gu(gf):
                    P.dma("pool", wgu[gf % NRING].rearrange("p a b c -> p (a b c)"), wgud[gf],
                          writes=[tk("wgu", gf % NRING)], slot="wgu%d" % (gf % NRING))

                def issue_wd(half):
                    for f in range(NFH):
                        lt = P.dma("pool", wdb[:, f, :], wdd[half * NFH + f], writes=[tk("wd", f)], slot="wd")
                    for f in range(NFH):
                        P.last_w[tk("wd", f)] = lt

                for f in range(NRING):
                    issue_gu(f)
                issue_wd(0)

                def c1_post(i):
                    for c in range(8):
                        P.op("pe", lambda t, c=c: t.transpose(ps_bf7[:, c * 128:(c + 1) * 128],
                                                              xn2[i % 2][:, c * 128:(c + 1) * 128], ident),
                             reads=[tk("xn2", i % 2), "cmat"], writes=[bank(7)], signal=(c == 7))
                    P.op("act", lambda a: a.activation(out=h2T[:, :, i * 128:(i + 1) * 128],
                                                       in_=ps_bf7.rearrange("p (c n) -> p c n", c=8), func=AF.Copy),
                         reads=[bank(7)], writes=[tk("h2T", i)])

                for i in range(8):
                    Jg = (tt * TT + i * 128) // 512
                    col = tt * TT + i * 128
                    yb = (i % 2) * 2
                    P.dma("sp", xs2[i % 2], xd[trow + i * 128: trow + (i + 1) * 128, :], writes=[tk("xs2", i % 2)],
                          slot="xs%d" % (i % 2))
                    for hh in range(2):
                        for kc in range(8):
                            P.op("pe", lambda t, hh=hh, kc=kc: t.matmul(
                                ps[:, yb + hh, :], mixT[:, kc, col:col + 128], wo[:, kc, hh * 512:(hh + 1) * 512],
                                start=(kc == 0), stop=(kc == 7)),
                                 reads=[K("mix", kc, Jg), "wo"], writes=[bank(yb + hh)],
                                 signal=(kc == 7 and hh == 1))
                    P.op("dve", lambda v: v.tensor_tensor(out=x1[:, i, :].rearrange("p (a n) -> p a n", a=2),
                                                          in0=ps[:, yb:yb + 2, :],
                                                          in1=xs2[i % 2].rearrange("p (a n) -> p a n", a=2),
                                                          op=ALU.add),
                         reads=[bank(yb), bank(yb + 1), tk("xs2", i % 2)], writes=[tk("x1", i)])
                    P.op("act", lambda a: a.activation(out=sqj2, in_=x1[:, i, :], func=AF.Square,
                                                       accum_out=stat2[:, i, 0:1]),
                         reads=[tk("x1", i)], writes=[tk("sqj2"), tk("st2", i)])
                    rstd_from_ss(stat2[:, i, 0:1], stat2[:, i, 1:2], D, EPS6, tk("st2", i), tk("rs2", i))
                    P.op("dve", lambda v: v.scalar_tensor_tensor(
                        out=xn2[i % 2], in0=x1[:, i, :], scalar=stat2[:, i, 1:2], in1=small[:, SP_FG:SP_FG + 1024],
                        op0=ALU.mult, op1=ALU.mult),
                         reads=[tk("x1", i), tk("rs2", i), "small"], writes=[tk("xn2", i % 2)])
                    if i >= 1:
                        c1_post(i - 1)
                c1_post(7)

                for half in range(2):
                    for f in range(NFH):
                        gf = half * NFH + f
                        slot = gf % NRING
                        gb_ = (gf % 2) * 2
                        ub_ = 4 + (gf % 2) * 2
                        for gu, bb in ((0, gb_), (1, ub_)):
                            for hh in range(2):
                                for kc in range(8):
                                    P.op("pe", lambda t, gu=gu, bb=bb, hh=hh, kc=kc: t.matmul(
                                        ps[:, bb + hh, :], wgu[slot][:, gu, kc, :], h2T[:, kc, hh * 512:(hh + 1) * 512],
                                        start=(kc == 0), stop=(kc == 7)),
                                         reads=[tk("wgu", slot)] + [tk("h2T", 4 * hh + q) for q in range(4)],
                                         writes=[bank(bb + hh)], signal=(kc == 7 and hh == 1))
                        P.op("act", lambda a: a.activation(out=sg[gf % 2], in_=ps[:, gb_:gb_ + 2, :], func=AF.Silu),
                             reads=[bank(gb_), bank(gb_ + 1)], writes=[tk("sg", gf % 2)])
                        P.op("dve", lambda v: v.tensor_tensor(out=aT[:, f, :].rearrange("p (a n) -> p a n", a=2),
                                                              in0=ps[:, ub_:ub_ + 2, :], in1=sg[gf % 2], op=ALU.mult),
                             reads=[bank(ub_), bank(ub_ + 1), tk("sg", gf % 2)], writes=[tk("aT", f)])
                        if f + NRING < NFH:
                            issue_gu(gf + NRING)
                    for i in range(8):
                        ob = (i % 2) * 2
                        for hh in range(2):
                            for f in range(NFH):
                                P.op("pe", lambda t, hh=hh, f=f: t.matmul(
                                    ps[:, ob + hh, :], aT[:, f, i * 128:(i + 1) * 128],
                                    wdb[:, f, hh * 512:(hh + 1) * 512], start=(f == 0), stop=(f == NFH - 1)),
                                     reads=[tk("aT", f), tk("wd", f)], writes=[bank(ob + hh)],
                                     signal=(f == NFH - 1 and hh == 1))
                        P.op("dve", lambda v: v.tensor_tensor(out=x1[:, i, :].rearrange("p (a n) -> p a n", a=2),
                                                              in0=ps[:, ob:ob + 2, :],
                                                              in1=x1[:, i, :].rearrange("p (a n) -> p a n", a=2),
                                                              op=ALU.add),
                             reads=[bank(ob), bank(ob + 1), tk("x1", i)], writes=[tk("x1", i)])
                        if half == 1:
                            P.dma("sp", yd[trow + i * 128: trow + (i + 1) * 128, :], x1[:, i, :],
                                  reads=[tk("x1", i)], slot="out%d" % i, is_out=True)
                    if half == 0:
                        for f in range(NRING):
                            issue_gu(NFH + f)
                        issue_wd(1)
            P.barrier()
        P.barrier()
        P.finish()
    return nc


def _host_constants():
    j = np.arange(128)
    ident = np.eye(128, dtype=np.float32)
    L1 = (j[:, None] >= j[None, :]).astype(np.float32)
    L2 = (j[:, None] < j[None, :]).astype(np.float32)
    negs = np.where(j[:, None] >= j[None, :], NEG, 0.0).astype(np.float32)
    negi = np.where(j[:, None] > j[None, :], NEG, 0.0).astype(np.float32)
    ones = np.ones((128, 128), np.float32)
    m01 = (j[:, None] <= j[None, :]).astype(np.float32)
    cmat = np.concatenate([ident, L1, L2, negs, negi, ones, ones / 32.0, ones / 128.0, m01], axis=1)
    half = HD // 2
    inv_freq = (np.float32(10000.0) ** (-np.arange(half, dtype=np.float32) / np.float32(half))).astype(np.float32)
    pos = np.arange(S, dtype=np.float32)
    ang = (pos[:, None] * inv_freq[None, :]).astype(np.float32)
    cos, sin = np.cos(ang.astype(np.float64)).astype(np.float32), np.sin(ang.astype(np.float64)).astype(np.float32)
    CC = np.concatenate([cos, cos], axis=1).reshape(NT, 128, 64).transpose(1, 0, 2)
    SS = np.concatenate([-sin, sin], axis=1).reshape(NT, 128, 64).transpose(1, 0, 2)
    rope = np.stack([CC, SS], axis=1).reshape(128, 2 * NT * 64).astype(np.float32)
    return np.ascontiguousarray(cmat), np.ascontiguousarray(rope)


def _prep_weights(inp):
    w_in = np.asarray(inp["w_in"], np.float32)[0]
    groups = []
    for g in range(8):
        if g < 4:
            cols = np.r_[128 * g:128 * g + 128, 512 + 128 * g:512 + 128 * g + 128,
                         1024 + 128 * g:1024 + 128 * g + 128]
        else:
            h = g - 4
            cols = np.r_[1536 + 128 * h:1536 + 128 * h + 128, 2048 + 128 * h:2048 + 128 * h + 128,
                         2560 + 128 * h:2560 + 128 * h + 128]
        wg = w_in[:, cols].reshape(8, 128, 384).transpose(1, 0, 2).reshape(128, 8 * 384)
        groups.append(wg)
    wing = np.ascontiguousarray(np.stack(groups, 0))
    wo = np.asarray(inp["w_o"], np.float32)[0].reshape(8, 128, 1024).transpose(1, 0, 2).reshape(128, 8 * 1024)
    wg_ = np.asarray(inp["w_gate"], np.float32)[0].reshape(8, 128, NF, 128)
    wu_ = np.asarray(inp["w_up"], np.float32)[0].reshape(8, 128, NF, 128)
    wgu = np.stack([wg_, wu_], 0).transpose(3, 2, 0, 1, 4).reshape(NF, 128, 2 * 8 * 128)
    wd = np.asarray(inp["w_down"], np.float32)[0].reshape(NF, 128, 1024)
    gq = np.asarray(inp["diff_q_norm_g"], np.float32)[0]
    gk = np.asarray(inp["diff_k_norm_g"], np.float32)[0]
    sw = lambda v: np.concatenate([v[32:], v[:32]])
    small = np.concatenate([
        np.asarray(inp["attn_norm_g"], np.float32)[0], np.asarray(inp["ffn_norm_g"], np.float32)[0],
        gq, sw(gq), gk, sw(gk),
        np.asarray(inp["lambda_q1"], np.float32)[0], np.asarray(inp["lambda_k1"], np.float32)[0],
        np.asarray(inp["lambda_q2"], np.float32)[0], np.asarray(inp["lambda_k2"], np.float32)[0]])
    small = np.ascontiguousarray(np.broadcast_to(small[None, :], (128, SP_N)))
    subln = np.ascontiguousarray(np.asarray(inp["diff_subln_g"], np.float32)[0].reshape(128, 1))
    return dict(wing=wing, wo=np.ascontiguousarray(wo), wgu=np.ascontiguousarray(wgu), wd=np.ascontiguousarray(wd),
                small=small, subln=subln)


def kernel(**inputs):
    x = np.asarray(inputs["x"], np.float32)
    wmaps = _prep_weights(inputs)
    cmat, rope = _host_constants()
    nc = build_program(NSEQ)
    in_maps = []
    for c in range(NCORES):
        m = dict(wmaps)
        m["x"] = np.ascontiguousarray(x[c * NSEQ:(c + 1) * NSEQ].reshape(NSEQ * S, D))
        m["cmat"] = cmat
        m["rope"] = rope
        in_maps.append(m)
    res = run_bass_kernel_spmd(nc, in_maps, core_ids=list(range(NCORES)))
    out = np.concatenate([np.asarray(r["y"], np.float32).reshape(NSEQ, S, D) for r in res.results], axis=0)
    return out
```

```python
import math
from contextlib import ExitStack

import numpy as np
import concourse.bass as bass
import concourse.mybir as mybir
from concourse.bass_utils import run_bass_kernel_spmd

F32 = mybir.dt.float32
BF16 = mybir.dt.bfloat16
AF = mybir.ActivationFunctionType
ALU = mybir.AluOpType
AX = mybir.AxisListType

NCORES = 8
D = 1024
S = 2048
BATCH = 32
NSEQ = BATCH // NCORES
DFF = 2816
NF = DFF // 128
NFH = NF // 2
HD = 64
NT = S // 128
NJ = S // 512
TT = 1024
NEG = -30000.0
LAMBDA_INIT = 0.8 - 0.6 * math.exp(-0.3 * 0)

SP_AG, SP_FG, SP_GQ, SP_GQS, SP_GK, SP_GKS, SP_L = 0, 1024, 2048, 2112, 2176, 2240, 2304
SP_N = 2304 + 256


class Tok:
    __slots__ = ("sem", "val", "eng")

    def __init__(self, sem, val, eng):
        self.sem, self.val, self.eng = sem, val, eng


class Prog:
    def __init__(self, nc, es):
        self.nc = nc
        self.E = {"pe": nc.tensor, "act": nc.scalar, "dve": nc.vector, "pool": nc.gpsimd, "sp": nc.sync}
        self.sem = {e: es.enter_context(nc.semaphore("s_" + e)) for e in self.E}
        self.cnt = {e: 0 for e in self.E}
        self.waited = {e: {} for e in self.E}
        self.last_w = {}
        self.readers = {}
        self.pending = {e: [] for e in self.E}
        self.dma_sem = {}
        self.dma_cnt = {}
        self.es = es
        self.out_toks = []
        self.all_dma = []

    def _wait(self, e, toks):
        best = {}
        for t in toks:
            if t is None:
                continue
            assert t.val is not None, "dependency on unsignaled op"
            if t.val > best.get(t.sem, 0):
                best[t.sem] = t.val
        for sname, v in best.items():
            if v > self.waited[e].get(sname, 0):
                self.E[e].wait_ge(self._semh(sname), v)
                self.waited[e][sname] = v

    def _semh(self, sname):
        return self.sem[sname] if sname in self.sem else self.dma_sem[sname]

    def _deps(self, e, reads, writes, is_dma=False, slot=None):
        deps = []
        for k in reads:
            t = self.last_w.get(k)
            if t is not None:
                if not (t.eng == e and e == "pe"):
                    deps.append(t)
            if len(k) == 2 and k[0] == "b":
                for r in self.readers.get(k, ()):
                    if r.eng != e:
                        deps.append(r)
        for k in writes:
            t = self.last_w.get(k)
            same_ok = (not is_dma) and e != "pool"
            if t is not None and not (t.eng == e and same_ok) and not (is_dma and t.sem == slot):
                deps.append(t)
            for r in self.readers.get(k, ()):
                if r.eng == e and same_ok:
                    continue
                deps.append(r)
        return deps

    def op(self, e, fn, reads=(), writes=(), signal=True):
        self._wait(e, self._deps(e, reads, writes))
        inst = fn(self.E[e])
        tok = Tok(e, None, e)
        if signal:
            inst.then_inc(self.sem[e], 1)
            self.cnt[e] += 1
            tok.val = self.cnt[e]
            for p in self.pending[e]:
                p.val = self.cnt[e]
            self.pending[e] = []
        else:
            self.pending[e].append(tok)
        for k in reads:
            self.readers.setdefault(k, []).append(tok)
        for k in writes:
            self.last_w[k] = tok
            self.readers[k] = []
        return tok

    def dma(self, q, out, in_, reads=(), writes=(), slot="d", is_out=False):
        sname = "dma_" + slot
        if sname not in self.dma_sem:
            self.dma_sem[sname] = self.es.enter_context(self.nc.semaphore(sname))
            self.dma_cnt[sname] = 0
        self._wait(q, self._deps(q, reads, writes, is_dma=True, slot=sname))
        self.E[q].dma_start(out=out, in_=in_).then_inc(self.dma_sem[sname], 16)
        self.dma_cnt[sname] += 16
        tok = Tok(sname, self.dma_cnt[sname], None)
        for k in reads:
            self.readers.setdefault(k, []).append(tok)
        for k in writes:
            self.last_w[k] = tok
            self.readers[k] = []
        if is_out:
            self.out_toks.append(tok)
        self.all_dma.append(tok)
        return tok

    def barrier(self):
        for e in self.E:
            assert not self.pending[e], "unsignaled tail on " + e
        toks = [Tok(e, self.cnt[e], e) for e in self.E if self.cnt[e] > 0]
        toks += [Tok(s, c, None) for s, c in self.dma_cnt.items()]
        for e in self.E:
            self._wait(e, [t for t in toks if t.eng != e])

    def finish(self):
        self._wait("sp", self.out_toks)


class Arena:
    log = []

    def __init__(self, t, nelem):
        self.t, self.n, self.off = t, nelem, 0

    def reset(self, off=0):
        self.off = off

    def alloc(self, free_shape, dt):
        n = int(np.prod(free_shape))
        sz = n * (2 if dt == F32 else 1)
        self.off = (self.off + 15) // 16 * 16
        a = self.t[:, self.off:self.off + sz]
        Arena.log.append((self.off, sz, str(dt), tuple(free_shape)))
        self.off += sz
        assert self.off <= self.n, ("arena overflow", self.off, self.n)
        if dt == F32:
            a = a.bitcast(F32)
        if len(free_shape) == 2:
            a = a.rearrange("p (a b) -> p a b", a=free_shape[0])
        elif len(free_shape) == 3:
            a = a.rearrange("p (a b c) -> p a b c", a=free_shape[0], b=free_shape[1])
        elif len(free_shape) == 4:
            a = a.rearrange("p (a b c d) -> p a b c d", a=free_shape[0], b=free_shape[1], c=free_shape[2])
        return a


def build_program(nseq=NSEQ):
    nc = bass.Bass("TRN2", target_bir_lowering=False)
    ntok = nseq * S
    xd = nc.dram_tensor("x", [ntok, D], F32, kind="ExternalInput").ap()
    wing = nc.dram_tensor("wing", [8, 128, 8 * 384], F32, kind="ExternalInput").ap()
    wod = nc.dram_tensor("wo", [128, 8 * 1024], F32, kind="ExternalInput").ap()
    wgud = nc.dram_tensor("wgu", [NF, 128, 2 * 8 * 128], F32, kind="ExternalInput").ap()
    wdd = nc.dram_tensor("wd", [NF, 128, 1024], F32, kind="ExternalInput").ap()
    smalld = nc.dram_tensor("small", [128, SP_N], F32, kind="ExternalInput").ap()
    sublnd = nc.dram_tensor("subln", [128, 1], F32, kind="ExternalInput").ap()
    cmatd = nc.dram_tensor("cmat", [128, 9 * 128], F32, kind="ExternalInput").ap()
    roped = nc.dram_tensor("rope", [128, 2 * NT * 64], F32, kind="ExternalInput").ap()
    yd = nc.dram_tensor("y", [ntok, D], F32, kind="ExternalOutput").ap()
    scrd = nc.dram_tensor("bc_scratch", [2, 2, 512], F32, kind="Internal").ap()

    with ExitStack() as es:
        P = Prog(nc, es)
        sb = lambda name, shape, dt: es.enter_context(nc.sbuf_tensor(name, shape, dt))
        ps = es.enter_context(nc.psum_tensor("ps", [128, 8, 512], F32))
        wo = sb("wo_sb", [128, 8, 1024], BF16)
        mixT = sb("mixT", [128, 8, S], BF16)
        small = sb("small_sb", [128, 2048], F32)
        cmat = sb("cmat_sb", [128, 9, 128], BF16)
        rtab = sb("rtab", [128, NT, 2, 2, 64], F32)
        misc = sb("misc", [128, 16], F32)
        zero_bf = sb("zero_bf", [128, 512], BF16)
        subg = sb("subg", [128, 1], F32)
        ltmp = sb("ltmp", [128, 2, 64], F32)
        ARENA = (int(nc.sbuf_bytes_remaining) - 2048) // 64 * 32
        arena_t = sb("arena", [128, ARENA], BF16)
        AR = Arena(arena_t, ARENA)
        ropeT = AR.alloc([2, NT, 64], F32)
        small2 = AR.alloc([SP_N - 2048], F32)

        ident = cmat[:, 0, :]
        L1 = cmat[:, 1, :]
        L2 = cmat[:, 2, :]
        NEGS = cmat[:, 3, :]
        NEGI = cmat[:, 4, :]
        ONES = cmat[:, 5, :]
        C32 = cmat[:, 6, :]
        C128 = cmat[:, 7, :]
        M01 = cmat[:, 8, :]
        EPS6, EPS5, ONE, LAM, NLAM = (misc[:, i:i + 1] for i in range(5))
        ps_bf7 = ps[:, 7, :].bitcast(BF16)

        def bank(b):
            return "b%d" % b

        P.dma("pool", cmat[:].rearrange("p a b -> p (a b)"), cmatd, writes=["cmat"], slot="c0")
        P.dma("sp", small[:], smalld[:, 0:2048], writes=["small"], slot="c1")
        P.dma("sp", small2, smalld[:, 2048:SP_N], writes=["small2"], slot="c5")
        P.dma("sp", ropeT.rearrange("p a t d -> p (a t d)"), roped, writes=["rope"], slot="c2")
        P.dma("sp", subg[:], sublnd, writes=["subg"], slot="c3")
        P.dma("pool", wo[:].rearrange("p a b -> p (a b)"), wod, writes=["wo"], slot="c4")
        P.op("dve", lambda v: v.memset(misc[:, 0:1], 1e-6), writes=["misc"])
        P.op("dve", lambda v: v.memset(misc[:, 1:2], 1e-5), writes=["misc"])
        P.op("dve", lambda v: v.memset(misc[:, 2:3], 1.0), writes=["misc"])
        P.op("dve", lambda v: v.memset(zero_bf[:], 0.0), writes=["zero"])
        P.op("dve", lambda v: v.tensor_scalar(out=subg[:], in0=subg[:], scalar1=1.0 - LAMBDA_INIT, scalar2=None,
                                              op0=ALU.mult), reads=["subg"], writes=["subg"])
        lv = lambda i: small2[:, SP_L - 2048 + 64 * i: SP_L - 2048 + 64 * (i + 1)]
        P.op("dve", lambda v: v.tensor_tensor(out=ltmp[:, 0, :], in0=lv(0), in1=lv(1), op=ALU.mult),
             reads=["small2"], writes=["ltmp"])
        P.op("dve", lambda v: v.tensor_tensor(out=ltmp[:, 1, :], in0=lv(2), in1=lv(3), op=ALU.mult),
             reads=["small2"], writes=["ltmp"])
        P.op("dve", lambda v: v.tensor_reduce(out=misc[:, 8:10], in_=ltmp[:], axis=AX.X, op=ALU.add),
             reads=["ltmp"], writes=["misc"])
        P.op("act", lambda a: a.activation(out=misc[:, 10:12], in_=misc[:, 8:10], func=AF.Exp),
             reads=["misc"], writes=["misc2"])
        P.op("dve", lambda v: v.tensor_tensor(out=misc[:, 12:13], in0=misc[:, 10:11], in1=misc[:, 11:12],
                                              op=ALU.subtract), reads=["misc2"], writes=["misc3"])
        P.op("dve", lambda v: v.tensor_scalar(out=misc[:, 3:4], in0=misc[:, 12:13], scalar1=LAMBDA_INIT, scalar2=None,
                                              op0=ALU.add), reads=["misc3"], writes=["misc4"])
        P.op("dve", lambda v: v.tensor_scalar(out=misc[:, 4:5], in0=misc[:, 3:4], scalar1=-1.0, scalar2=None,
                                              op0=ALU.mult), reads=["misc4"], writes=["misc5"])
        for qk, (go, gso) in enumerate(((SP_GQ, SP_GQS), (SP_GK, SP_GKS))):
            gb = small2[:, go - 2048:go - 2048 + 64].unsqueeze(1).broadcast_to([128, NT, 64])
            gsb = small2[:, gso - 2048:gso - 2048 + 64].unsqueeze(1).broadcast_to([128, NT, 64])
            P.op("pool", lambda g, gb=gb, qk=qk: g.tensor_tensor(out=rtab[:, :, qk, 0, :], in0=ropeT[:, 0, :, :],
                                                                 in1=gb, op=ALU.mult),
                 reads=["small2", "rope"], writes=["rtab"])
            P.op("pool", lambda g, gsb=gsb, qk=qk: g.tensor_tensor(out=rtab[:, :, qk, 1, :], in0=ropeT[:, 1, :, :],
                                                                   in1=gsb, op=ALU.mult),
                 reads=["small2", "rope"], writes=["rtab"])

        P.barrier()

        def rstd_from_ss(ss_ap, out_ap, n, eps_ap, rk, wk):
            P.op("act", lambda a: a.activation(out=out_ap, in_=ss_ap, func=AF.Ln, bias=eps_ap, scale=1.0 / n),
                 reads=[rk, "misc"], writes=[wk])
            P.op("act", lambda a: a.activation(out=out_ap, in_=out_ap, func=AF.Exp, scale=-0.5),
                 reads=[wk], writes=[wk])

        for b in range(nseq):
            row0 = b * S
            AR.reset()
            hT = AR.alloc([8, S], BF16)
            stat = AR.alloc([NT, 4], F32)
            wgb = [AR.alloc([8, 384], BF16) for _ in range(2)]
            qkT = [AR.alloc([2, S], BF16) for _ in range(2)]
            vtok = [AR.alloc([NT, 128], BF16) for _ in range(2)]
            Eb = [AR.alloc([2, 512], F32) for _ in range(3)]
            xs = [e_.rearrange("p a b -> p (a b)") for e_ in Eb]
            SQb = [AR.alloc([2, 512], BF16) for _ in range(2)]
            xn = [q_.rearrange("p a b -> p (a b)") for q_ in SQb]
            Fb = [AR.alloc([2, 512], F32) for _ in range(2)]
            Wb = [AR.alloc([2, 512], BF16) for _ in range(3)]
            sqj = Wb[0].rearrange("p a b -> p (a b)")
            qk32 = [AR.alloc([256], F32) for _ in range(2)]
            sqs = [AR.alloc([256], F32) for _ in range(2)]
            rt1 = [AR.alloc([256], F32) for _ in range(4)]
            rt2 = AR.alloc([256], F32)
            qkb = [AR.alloc([256], BF16) for _ in range(4)]
            dstat = [AR.alloc([8], F32) for _ in range(4)]
            Os = [AR.alloc([2, 512], F32) for _ in range(2)]
            RSs = [AR.alloc([512], F32) for _ in range(2)]
            fin = [AR.alloc([512], F32) for _ in range(3)]
            hib = AR.alloc([512], BF16)
            lob = AR.alloc([512], BF16)
            pfx = "s%d_" % b
            K = lambda *a: pfx + "_".join(str(x) for x in a)

            def phaseA_evac(i):
                for c in range(8):
                    P.op("pe", lambda t, c=c, i=i: t.transpose(ps_bf7[:, c * 128:(c + 1) * 128],
                                                               xn[i % 2][:, c * 128:(c + 1) * 128], ident),
                         reads=[K("SQ", i % 2), "cmat"], writes=[bank(7)], signal=(c == 7))
                if i % 2 == 0:
                    P.op("act", lambda a, i=i: a.activation(
                        out=hT[:, :, i * 128:(i + 1) * 128],
                        in_=ps_bf7.rearrange("p (c n) -> p c n", c=8), func=AF.Copy),
                         reads=[bank(7)], writes=[K("hT", i)])
                else:
                    P.op("dve", lambda v, i=i: v.tensor_copy(
                        out=hT[:, :, i * 128:(i + 1) * 128],
                        in_=ps_bf7.rearrange("p (c n) -> p c n", c=8)),
                         reads=[bank(7)], writes=[K("hT", i)])

            for i in range(NT):
                P.dma("sp", xs[i % 2], xd[row0 + i * 128: row0 + (i + 1) * 128, :], writes=[K("E", i % 2)],
                      slot="xs%d" % (i % 2))
                P.op("act", lambda a, i=i: a.activation(out=sqj, in_=xs[i % 2], func=AF.Square,
                                                        accum_out=stat[:, i, 0:1]),
                     reads=[K("E", i % 2)], writes=[K("W", 0), K("stat", i)])
                rstd_from_ss(stat[:, i, 0:1], stat[:, i, 1:2], D, EPS6, K("stat", i), K("rstd", i))
                P.op("dve", lambda v, i=i: v.scalar_tensor_tensor(
                    out=xn[i % 2], in0=xs[i % 2], scalar=stat[:, i, 1:2], in1=small[:, SP_AG:SP_AG + 1024],
                    op0=ALU.mult, op1=ALU.mult),
                     reads=[K("E", i % 2), K("rstd", i), "small"], writes=[K("SQ", i % 2)])
                if i >= 1:
                    phaseA_evac(i - 1)
            phaseA_evac(NT - 1)

            def load_wg(g, par):
                P.dma("pool", wgb[par].rearrange("p a b -> p (a b)"), wing[g], writes=[K("wg", par)],
                      slot="wg%d" % par)

            b7 = {"req": False, "clean": True}

            def run_stages(n_items, stages):
                done = {st[0]: 0 for st in stages}
                b7w = [(st[0], st[3]) for st in stages if st[3]]
                while any(done[st[0]] < n_items for st in stages):
                    snap = dict(done)
                    for name, fn, prods, rd7, limits in stages:
                        i = done[name]
                        if i >= n_items or any(snap[p] <= i for p in prods):
                            continue
                        if any(i - done[c] >= d for c, d in limits):
                            continue
                        if rd7 and (b7["req"] or done[rd7] < i):
                            continue
                        fn(i)
                        done[name] += 1
                    b7["clean"] = all(done[r] == done[w] for w, r in b7w)
                    yield
                b7["clean"] = True

            def inproj_sb(g, par, atomic):
                wgk = K("wg", par)
                units = []
                for j in range(NJ):
                    units += [("qk", j, 0), ("qk", j, 1), ("v", j, 0)]

                fg = (g == ORDER[0])
                bk = (lambda u: 6 + (u % 2)) if fg else (lambda u: 7)

                def mm(u):
                    kind, j, which = units[u]
                    hk = [K("hT", 4 * j + t) for t in range(4)]
                    if kind == "qk":
                        for kc in range(8):
                            P.op("pe", lambda t: t.matmul(
                                ps[:, bk(u), :], wgb[par][:, kc, which * 128:(which + 1) * 128],
                                hT[:, kc, j * 512:(j + 1) * 512], start=(kc == 0), stop=(kc == 7)),
                                 reads=[wgk] + hk, writes=[bank(bk(u))], signal=(kc == 7))
                    else:
                        for t4 in range(4):
                            for kc in range(8):
                                P.op("pe", lambda t: t.matmul(
                                    ps[:, bk(u), t4 * 128:(t4 + 1) * 128],
                                    hT[:, kc, (4 * j + t4) * 128:(4 * j + t4 + 1) * 128],
                                    wgb[par][:, kc, 256:384], start=(kc == 0), stop=(kc == 7)),
                                     reads=[wgk] + hk, writes=[bank(bk(u))], signal=(kc == 7 and t4 == 3))

                def ev(u):
                    kind, j, which = units[u]
                    if kind == "qk":
                        P.op("dve", lambda v: v.tensor_copy(
                            out=qkT[par][:, which, j * 512:(j + 1) * 512], in_=ps[:, bk(u), :]),
                             reads=[bank(bk(u))], writes=[K("qk", par, j)])
                    else:
                        P.op("dve", lambda v: v.tensor_copy(
                            out=vtok[par][:, 4 * j:4 * j + 4, :],
                            in_=ps[:, bk(u), :].rearrange("p (a n) -> p a n", a=4)),
                             reads=[bank(bk(u))], writes=[K("v", par, j)])

                if fg:
                    for u in range(len(units)):
                        mm(u)
                        if u >= 1:
                            ev(u - 1)
                        yield
                    ev(len(units) - 1)
                    yield
                else:
                    yield from run_stages(len(units), [("ev", ev, ["mm"], None, []), ("mm", mm, [], "ev", [])])

            def inproj_diff(g, par, atomic):
                wgk = K("wg", par)

                def st_mm(i):
                    for kc in range(8):
                        P.op("pe", lambda t: t.matmul(
                            ps[:, 7, 0:384], hT[:, kc, i * 128:(i + 1) * 128], wgb[par][:, kc, :],
                            start=(kc == 0), stop=(kc == 7)),
                             reads=[wgk, K("hT", i)], writes=[bank(7)], signal=(kc == 7))

                def st_ev(i):
                    k2 = i % 2
                    P.op("dve", lambda v: v.tensor_copy(out=qk32[k2], in_=ps[:, 7, 0:256]),
                         reads=[bank(7)], writes=[K("qk32", k2)])
                    P.op("dve", lambda v: v.tensor_copy(out=vtok[par][:, i, :], in_=ps[:, 7, 256:384]),
                         reads=[bank(7)], writes=[K("v", par, i // 4)])
                    P.op("dve", lambda v: v.tensor_tensor(out=sqs[k2], in0=qk32[k2], in1=qk32[k2], op=ALU.mult),
                         reads=[K("qk32", k2)], writes=[K("sqs", k2)])
                    k4 = i % 4
                    x4 = qk32[k2].rearrange("p (q m d) -> p q m d", q=2, m=2)
                    o1 = rt1[k4].rearrange("p (q m d) -> p q m d", q=2, m=2)
                    o2 = rt2.rearrange("p (q m d) -> p q m d", q=2, m=2)
                    tabA = rtab[:, i, :, 0, :].unsqueeze(2).broadcast_to([128, 2, 2, 64])
                    tabB_lo = rtab[:, i, :, 1, 0:32].unsqueeze(2).broadcast_to([128, 2, 2, 32])
                    tabB_hi = rtab[:, i, :, 1, 32:64].unsqueeze(2).broadcast_to([128, 2, 2, 32])
                    P.op("dve", lambda v: v.tensor_tensor(out=o1, in0=x4, in1=tabA, op=ALU.mult),
                         reads=[K("qk32", k2), "rtab"], writes=[K("rt1", k4)])
                    P.op("pool", lambda g_: g_.tensor_tensor(out=o2[:, :, :, 0:32], in0=x4[:, :, :, 32:64],
                                                             in1=tabB_lo, op=ALU.mult),
                         reads=[K("qk32", k2), "rtab"], writes=[K("rt2")])
                    P.op("pool", lambda g_: g_.tensor_tensor(out=o2[:, :, :, 32:64], in0=x4[:, :, :, 0:32],
                                                             in1=tabB_hi, op=ALU.mult),
                         reads=[K("qk32", k2), "rtab"], writes=[K("rt2")])

                def st_add(i):
                    k4 = i % 4
                    P.op("pool", lambda g_: g_.tensor_tensor(out=rt1[k4], in0=rt1[k4], in1=rt2, op=ALU.add),
                         reads=[K("rt1", k4), K("rt2")], writes=[K("rt1", k4)])

                def st_red(i):
                    k2, k4 = i % 2, i % 4
                    P.op("dve", lambda v: v.tensor_reduce(out=dstat[k4][:, 0:4],
                                                          in_=sqs[k2].rearrange("p (g d) -> p g d", g=4),
                                                          axis=AX.X, op=ALU.add),
                         reads=[K("sqs", k2)], writes=[K("dss", k4)])

                def st_rstd(i):
                    k4 = i % 4
                    rstd_from_ss(dstat[k4][:, 0:4], dstat[k4][:, 4:8], HD, EPS6, K("dss", k4), K("drs", k4))

                def st_fin(i):
                    k4 = i % 4
                    rb = dstat[k4][:, 4:8].unsqueeze(2).broadcast_to([128, 4, 64])
                    P.op("pool", lambda v: v.tensor_tensor(out=qkb[k4].rearrange("p (g d) -> p g d", g=4),
                                                           in0=rt1[k4].rearrange("p (g d) -> p g d", g=4),
                                                           in1=rb, op=ALU.mult),
                         reads=[K("rt1", k4), K("drs", k4)], writes=[K("qkb", k4)])

                def st_tr(i):
                    k4 = i % 4
                    for which in range(2):
                        P.op("pe", lambda t: t.transpose(
                            ps_bf7[:, 768 + which * 128: 768 + (which + 1) * 128],
                            qkb[k4][:, which * 128:(which + 1) * 128], ident),
                             reads=[K("qkb", k4), "cmat"], writes=[bank(7)], signal=(which == 1))

                def st_tev(i):
                    P.op("dve", lambda v: v.tensor_copy(
                        out=qkT[par][:, :, i * 128:(i + 1) * 128],
                        in_=ps_bf7[:, 768:1024].rearrange("p (a n) -> p a n", a=2)),
                         reads=[bank(7)], writes=[K("qk", par, i // 4)])

                yield from run_stages(NT, [
                    ("add", st_add, ["ev"], None, []),
                    ("ev", st_ev, ["mm"], None, [("fin", 4), ("red", 2), ("add", 1)]),
                    ("tev", st_tev, ["tr"], None, []),
                    ("tr", st_tr, ["w"], "tev", []),
                    ("mm", st_mm, [], "ev", []),
                    ("w", lambda i: None, ["fin"], None, []),
                    ("fin", st_fin, ["rstd", "add"], None, [("tr", 4)]),
                    ("rstd", st_rstd, ["red"], None, []),
                    ("red", st_red, ["ev"], None, [("fin", 4)]),
                ])

            bg = []

            def bg_step(n=1):
                for _ in range(n):
                    for gen in list(bg):
                        try:
                            next(gen)
                        except StopIteration:
                            bg.remove(gen)

            def bg_drain():
                while bg:
                    bg_step()

            inproj_gen = [None]
            finals = []
            os_busy = {0: False, 1: False}
            closing = [False]

            def final_worker():
                while True:
                    if finals:
                        yield from diff_final(*finals.pop(0))
                    elif closing[0]:
                        return
                    else:
                        yield

            ORDER = [0, 4, 1, 5, 2, 6, 3, 7]

            def throttle(gen, every):
                k = 0
                for _ in gen:
                    yield
                    k += 1
                    if k % every == 0:
                        yield

            def start_group(pos):
                g, par = ORDER[pos], pos % 2
                load_wg(g, par)
                during_diff = pos >= 1 and ORDER[pos - 1] >= 4
                inproj_gen[0] = inproj_sb(g, par, during_diff) if g < 4 else throttle(inproj_diff(g, par, True), 3)
                bg.append(inproj_gen[0])

            def drain_inproj():
                while inproj_gen[0] in bg:
                    bg_step()

            def sb_attention(g, par):
                q_ = qkT[par][:, 0, :]
                k_ = qkT[par][:, 1, :]
                steps = []
                for J in range(NJ):
                    nkb = 4 * J + 4
                    for kb in range(nkb - 1, -1, -1):
                        steps.append((J, kb, kb == nkb - 1, kb == 0))
                N = len(steps)

                def c0_of(J, kb):
                    return max(0, kb * 128 - 512 * J)

                def S1(i):
                    J, kb, first, last = steps[i]
                    c0 = c0_of(J, kb)
                    zb = (i % 2) * 2
                    diag = kb >= 4 * J
                    rk = [K("qk", par, J), K("qk", par, kb // 4)]
                    for h in range(2):
                        lo, hi = h * 64, (h + 1) * 64
                        kT = k_[lo:hi, kb * 128:(kb + 1) * 128]
                        if diag:
                            P.op("pe", lambda t: t.matmul(ps[:, zb + h, c0:c0 + 128], kT,
                                                          q_[lo:hi, J * 512 + c0: J * 512 + c0 + 128],
                                                          start=True, stop=False),
                                 reads=rk, writes=[bank(zb + h)], signal=False)
                            P.op("pe", lambda t: t.matmul(ps[:, zb + h, c0:c0 + 128], ident, NEGS,
                                                          start=False, stop=True),
                                 reads=["cmat"], writes=[bank(zb + h)], signal=(h == 1 and c0 + 128 >= 512))
                            if c0 + 128 < 512:
                                P.op("pe", lambda t: t.matmul(ps[:, zb + h, c0 + 128:512], kT,
                                                              q_[lo:hi, J * 512 + c0 + 128: (J + 1) * 512],
                                                              start=True, stop=True),
                                     reads=rk, writes=[bank(zb + h)], signal=(h == 1))
                        else:
                            P.op("pe", lambda t: t.matmul(ps[:, zb + h, :], kT, q_[lo:hi, J * 512:(J + 1) * 512],
                                                          start=True, stop=True),
                                 reads=rk, writes=[bank(zb + h)], signal=(h == 1))

                def S2a(i):
                    J, kb, first, last = steps[i]
                    c0 = c0_of(J, kb)
                    zb = (i % 2) * 2
                    P.op("act", lambda a: a.activation(out=Eb[i % 3][:, :, c0:], in_=ps[:, zb:zb + 2, c0:],
                                                       func=AF.Exp, scale=0.125),
                         reads=[bank(zb), bank(zb + 1)], writes=[K("E", i % 3)])

                def S2b(i):
                    J, kb, first, last = steps[i]
                    c0 = c0_of(J, kb)
                    P.op("act", lambda a: a.activation(out=SQb[i % 2][:, :, c0:], in_=Eb[i % 3][:, :, c0:],
                                                       func=AF.Ln, bias=ONE, scale=1.0),
                         reads=[K("E", i % 3), "misc"], writes=[K("SQ", i % 2)])

                def S3(i):
                    J, kb, first, last = steps[i]
                    c0 = c0_of(J, kb)
                    if first:
                        for h in range(2):
                            P.op("pe", lambda t: t.matmul(ps[:, 4 + h, :], L1, zero_bf[:], start=True, stop=True),
                                 reads=["cmat", "zero"], writes=[bank(4 + h)], signal=False)
                    for h in range(2):
                        P.op("pe", lambda t: t.matmul(ps[:, 4 + h, c0:], L1, SQb[i % 2][:, h, c0:],
                                                      start=False, stop=True, skip_group_check=True),
                             reads=["cmat", K("SQ", i % 2)], writes=[bank(4 + h)], signal=(h == 1))

                def S4(i):
                    J, kb, first, last = steps[i]
                    c0 = c0_of(J, kb)
                    P.op("act", lambda a: a.activation(out=Fb[i % 2][:, :, c0:], in_=ps[:, 4:6, c0:], func=AF.Exp,
                                                       scale=-1.0),
                         reads=[bank(4), bank(5)], writes=[K("F", i % 2)])

                def S5(i):
                    J, kb, first, last = steps[i]
                    c0 = c0_of(J, kb)
                    P.op("dve", lambda v: v.tensor_tensor(out=Wb[i % 3][:, :, c0:], in0=Eb[i % 3][:, :, c0:],
                                                          in1=Fb[i % 2][:, :, c0:], op=ALU.mult),
                         reads=[K("E", i % 3), K("F", i % 2)], writes=[K("W", i % 3)])

                def S6(i):
                    J, kb, first, last = steps[i]
                    c0 = c0_of(J, kb)
                    if last:
                        return
                    for h in range(2):
                        P.op("pe", lambda t: t.matmul(ps[:, 4 + h, c0:], L2, SQb[i % 2][:, h, c0:],
                                                      start=False, stop=True, skip_group_check=True),
                             reads=["cmat", K("SQ", i % 2)], writes=[bank(4 + h)], signal=(h == 1))

                def S7(i):
                    J, kb, first, last = steps[i]
                    c0 = c0_of(J, kb)
                    if first:
                        for h in range(2):
                            P.op("pe", lambda t: t.matmul(ps[h * 64:(h + 1) * 64, 6, :],
                                                          vtok[par][:, 0, h * 64:(h + 1) * 64], zero_bf[:],
                                                          start=True, stop=False),
                                 reads=["zero", K("v", par, 0)], writes=[bank(6)], signal=False)
                    for h in range(2):
                        P.op("pe", lambda t: t.matmul(ps[h * 64:(h + 1) * 64, 6, c0:],
                                                      vtok[par][:, kb, h * 64:(h + 1) * 64],
                                                      Wb[i % 3][:, h, c0:], start=False, stop=last),
                             reads=[K("v", par, kb // 4), K("W", i % 3)], writes=[bank(6)], signal=(h == 1))
                    if last:
                        P.op("dve", lambda v: v.tensor_copy(out=mixT[:, g, J * 512:(J + 1) * 512], in_=ps[:, 6, :]),
                             reads=[bank(6)], writes=[K("mix", g, J)])

                for n in range(-2, N + 1):
                    if 0 <= n + 2 < N:
                        S1(n + 2)
                    if 0 <= n < N:
                        S4(n)
                        S5(n)
                    if 0 <= n + 1 < N:
                        S2b(n + 1)
                    if 0 <= n + 2 < N:
                        S2a(n + 2)
                    if 0 <= n < N:
                        S6(n)
                    if 0 <= n + 1 < N:
                        S3(n + 1)
                    if 0 <= n - 1 < N:
                        S7(n - 1)
                    bg_step()

            def diff_final(g, J, k2):
                o_ = Os[k2]
                rs = RSs[k2]
                f0, f1, f2 = fin
                kk = K("fin")
                hk, lk = K("hi"), K("lo")
                P.op("act", lambda a_: a_.activation(out=rs[0:64, :], in_=rs[0:64, :], func=AF.Ln),
                     reads=[K("RS", k2)], writes=[K("RS", k2)])
                P.op("act", lambda a_: a_.activation(out=rs[0:64, :], in_=rs[0:64, :], func=AF.Exp, scale=-1.0),
                     reads=[K("RS", k2)], writes=[K("RS", k2)])
                yield
                P.dma("sp", scrd[k2, 0:1, :], rs[0:1, :], reads=[K("RS", k2)], writes=["scr%d0" % k2], slot="bw0")
                P.dma("sp", scrd[k2, 1:2, :], rs[32:33, :], reads=[K("RS", k2)], writes=["scr%d1" % k2], slot="bw1")
                yield
                yield
                yield
                P.dma("sp", f1, scrd[k2, 0:1, :].broadcast_to([128, 512]), reads=["scr%d0" % k2],
                      writes=[kk + "1"], slot="bc0")
                P.dma("sp", f2, scrd[k2, 1:2, :].broadcast_to([128, 512]), reads=["scr%d1" % k2],
                      writes=[kk + "2"], slot="bc1")
                yield
                yield
                yield
                P.op("pool", lambda g_: g_.tensor_tensor(out=f1, in0=o_[:, 0, :], in1=f1, op=ALU.mult),
                     reads=[K("Os", k2), kk + "1"], writes=[kk + "1"])
                P.op("pool", lambda g_: g_.tensor_tensor(out=f2, in0=o_[:, 1, :], in1=f2, op=ALU.mult),
                     reads=[K("Os", k2), kk + "2"], writes=[kk + "2"])
                yield
                os_busy[k2] = False
                yield
                P.op("dve", lambda v: v.scalar_tensor_tensor(out=f0, in0=f2, scalar=NLAM, in1=f1, op0=ALU.mult,
                                                             op1=ALU.add),
                     reads=[kk + "1", kk + "2", "misc5"], writes=[kk + "0"])
                yield
                P.op("pool", lambda g_: g_.tensor_tensor(out=f1, in0=f0, in1=f0, op=ALU.mult),
                     reads=[kk + "0"], writes=[kk + "1"])
                P.op("pool", lambda g_: g_.tensor_copy(out=hib, in_=f1), reads=[kk + "1"], writes=[hk])
                P.op("pool", lambda g_: g_.tensor_tensor(out=lob, in0=f1, in1=hib, op=ALU.subtract),
                     reads=[kk + "1", hk], writes=[lk])
                yield
                yield
                yield
                b7["req"] = True
                while not b7["clean"]:
                    yield
                P.op("pe", lambda t: t.matmul(ps[:, 7, :], C128, hib, start=True, stop=False),
                     reads=["cmat", hk], writes=[bank(7)], signal=False)
                P.op("pe", lambda t: t.matmul(ps[:, 7, :], C128, lob, start=False, stop=True),
                     reads=["cmat", lk], writes=[bank(7)])
                P.op("dve", lambda v: v.tensor_copy(out=f2, in_=ps[:, 7, :]),
                     reads=[bank(7)], writes=[kk + "2"])
                b7["req"] = False
                yield
                yield
                P.op("act", lambda a: a.activation(out=f2, in_=f2, func=AF.Ln, bias=EPS5, scale=1.0),
                     reads=[kk + "2", "misc"], writes=[kk + "2"])
                P.op("act", lambda a: a.activation(out=f2, in_=f2, func=AF.Exp, scale=-0.5),
                     reads=[kk + "2"], writes=[kk + "2"])
                yield
                P.op("dve", lambda v: v.scalar_tensor_tensor(out=mixT[:, g, J * 512:(J + 1) * 512], in0=f0,
                                                             scalar=subg[:, 0:1], in1=f2, op0=ALU.mult, op1=ALU.mult),
                     reads=[kk + "0", kk + "2", "subg"], writes=[K("mix", g, J)])
                yield

            def diff_attention(g, par):
                q_ = qkT[par][:, 0, :]
                k_ = qkT[par][:, 1, :]
                steps = []
                for J in range(NJ):
                    nkb = 4 * J + 4
                    for kb in range(nkb):
                        steps.append((J, kb, kb == 0, kb == nkb - 1))
                N = len(steps)
                Pb = Wb

                def c0_of(J, kb):
                    return max(0, kb * 128 - 512 * J)

                def D1(i):
                    J, kb, first, last = steps[i]
                    c0 = c0_of(J, kb)
                    zb = (i % 2) * 2
                    rk = [K("qk", par, J), K("qk", par, kb // 4)]
                    for m in range(2):
                        lo, hi = m * 64, (m + 1) * 64
                        kT = k_[lo:hi, kb * 128:(kb + 1) * 128]
                        P.op("pe", lambda t: t.matmul(ps[:, zb + m, c0:], kT, q_[lo:hi, J * 512 + c0:(J + 1) * 512],
                                                      start=True, stop=True),
                             reads=rk, writes=[bank(zb + m)], signal=(m == 1))

                def D2(i):
                    J, kb, first, last = steps[i]
                    c0 = c0_of(J, kb)
                    zb = (i % 2) * 2
                    P.op("act", lambda a: a.activation(out=Pb[i % 3][:, :, c0:], in_=ps[:, zb:zb + 2, c0:],
                                                       func=AF.Exp, scale=0.125),
                         reads=[bank(zb), bank(zb + 1)], writes=[K("W", i % 3)])
                    if kb >= 4 * J:
                        pd = Pb[i % 3][:, :, c0:c0 + 128]
                        P.op("pool", lambda g_: g_.tensor_tensor(out=pd, in0=pd,
                                                                 in1=M01.unsqueeze(1).broadcast_to([128, 2, 128]),
                                                                 op=ALU.mult),
                             reads=[K("W", i % 3), "cmat"], writes=[K("W", i % 3)])

                def D3(i):
                    J, kb, first, last = steps[i]
                    c0 = c0_of(J, kb)
                    p_ = Pb[i % 3]
                    for m in range(2):
                        P.op("pe", lambda t: t.matmul(ps[:, 4 + m, c0:], vtok[par][:, kb, :], p_[:, m, c0:],
                                                      start=first, stop=last),
                             reads=[K("v", par, kb // 4), K("W", i % 3)], writes=[bank(4 + m)], signal=False)
                    for m in range(2):
                        P.op("pe", lambda t: t.matmul(ps[32 * m:32 * m + 32, 6, c0:], ONES[:, 0:32], p_[:, m, c0:],
                                                      start=first, stop=last),
                             reads=["cmat", K("W", i % 3)], writes=[bank(6)], signal=(m == 1))
                    if last:
                        k2 = J % 2
                        while os_busy[k2]:
                            bg_step()
                        P.op("act", lambda a: a.activation(out=Os[k2], in_=ps[:, 4:6, :], func=AF.Copy),
                             reads=[bank(4), bank(5)], writes=[K("Os", k2)])
                        P.op("dve", lambda v: v.tensor_copy(out=RSs[k2][0:64, :], in_=ps[0:64, 6, :]),
                             reads=[bank(6)], writes=[K("RS", k2)])
                        os_busy[k2] = True
                        finals.append((g, J, k2))

                for n in range(-1, N + 1):
                    if 0 <= n + 1 < N:
                        D1(n + 1)
                    if 0 <= n < N:
                        D2(n)
                    if 0 <= n - 1 < N:
                        D3(n - 1)
                    bg_step()

            start_group(0)
            bg_drain()
            closing[0] = False
            bg.append(final_worker())
            for pos, g in enumerate(ORDER):
                if pos + 1 < 8:
                    start_group(pos + 1)
                if g < 4:
                    sb_attention(g, pos % 2)
                else:
                    diff_attention(g, pos % 2)
                drain_inproj()
            closing[0] = True
            bg_drain()
            P.barrier()

            AR.reset()
            x1 = AR.alloc([8, 1024], F32)
            h2T = AR.alloc([8, TT], BF16)
            aT = AR.alloc([NFH, TT], BF16)
            wdb = AR.alloc([NFH, 1024], BF16)
            NRING = 3
            wgu = [AR.alloc([2, 8, 128], BF16) for _ in range(NRING)]
            xs2 = [AR.alloc([1024], F32) for _ in range(2)]
            xn2 = [AR.alloc([1024], BF16) for _ in range(3)]
            sg = [AR.alloc([2, 512], F32) for _ in range(2)]
            sqj2 = AR.alloc([1024], BF16)
            stat2 = AR.alloc([8, 4], F32)

            for tt in range(S // TT):
                tk = K
                trow = row0 + tt * TT
                def issue_gu(gf):
                    P.dma("pool", wgu[gf % NRING].rearrange("p a b c -> p (a b c)"), wgud[gf],
                          writes=[tk("wgu", gf % NRING)], slot="wgu%d" % (gf % NRING))

                def issue_wd(half):
                    for f in range(NFH):
                        lt = P.dma("pool", wdb[:, f, :], wdd[half * NFH + f], writes=[tk("wd", f)], slot="wd")
                    for f in range(NFH):
                        P.last_w[tk("wd", f)] = lt

                for f in range(NRING):
                    issue_gu(f)
                issue_wd(0)

                def c1_post(i):
                    for c in range(8):
                        P.op("pe", lambda t, c=c: t.transpose(ps_bf7[:, c * 128:(c + 1) * 128],
                                                              xn2[i % 3][:, c * 128:(c + 1) * 128], ident),
                             reads=[tk("xn2", i % 3), "cmat"], writes=[bank(7)], signal=(c == 7))
                    P.op("act", lambda a: a.activation(out=h2T[:, :, i * 128:(i + 1) * 128],
                                                       in_=ps_bf7.rearrange("p (c n) -> p c n", c=8), func=AF.Copy),
                         reads=[bank(7)], writes=[tk("h2T", i)])

                for i in range(8):
                    Jg = (tt * TT + i * 128) // 512
                    col = tt * TT + i * 128
                    yb = (i % 2) * 2
                    P.dma("sp", xs2[i % 2], xd[trow + i * 128: trow + (i + 1) * 128, :], writes=[tk("xs2", i % 2)],
                          slot="xs%d" % (i % 2))
                    for hh in range(2):
                        for kc in range(8):
                            P.op("pe", lambda t, hh=hh, kc=kc: t.matmul(
                                ps[:, yb + hh, :], mixT[:, kc, col:col + 128], wo[:, kc, hh * 512:(hh + 1) * 512],
                                start=(kc == 0), stop=(kc == 7)),
                                 reads=[K("mix", kc, Jg), "wo"], writes=[bank(yb + hh)],
                                 signal=(kc == 7 and hh == 1))
                    P.op("dve", lambda v: v.tensor_tensor(out=x1[:, i, :].rearrange("p (a n) -> p a n", a=2),
                                                          in0=ps[:, yb:yb + 2, :],
                                                          in1=xs2[i % 2].rearrange("p (a n) -> p a n", a=2),
                                                          op=ALU.add),
                         reads=[bank(yb), bank(yb + 1), tk("xs2", i % 2)], writes=[tk("x1", i)])
                    P.op("act", lambda a: a.activation(out=sqj2, in_=x1[:, i, :], func=AF.Square,
                                                       accum_out=stat2[:, i, 0:1]),
                         reads=[tk("x1", i)], writes=[tk("sqj2"), tk("st2", i)])
                    rstd_from_ss(stat2[:, i, 0:1], stat2[:, i, 1:2], D, EPS6, tk("st2", i), tk("rs2", i))
                    P.op("dve", lambda v: v.scalar_tensor_tensor(
                        out=xn2[i % 3], in0=x1[:, i, :], scalar=stat2[:, i, 1:2], in1=small[:, SP_FG:SP_FG + 1024],
                        op0=ALU.mult, op1=ALU.mult),
                         reads=[tk("x1", i), tk("rs2", i), "small"], writes=[tk("xn2", i % 3)])
                    if i >= 2:
                        c1_post(i - 2)
                c1_post(6)
                c1_post(7)

                for half in range(2):
                    for f in range(NFH):
                        gf = half * NFH + f
                        slot = gf % NRING
                        gb_ = (gf % 2) * 2
                        ub_ = 4 + (gf % 2) * 2
                        for gu, bb in ((0, gb_), (1, ub_)):
                            for hh in range(2):
                                for kc in range(8):
                                    P.op("pe", lambda t, gu=gu, bb=bb, hh=hh, kc=kc: t.matmul(
                                        ps[:, bb + hh, :], wgu[slot][:, gu, kc, :], h2T[:, kc, hh * 512:(hh + 1) * 512],
                                        start=(kc == 0), stop=(kc == 7)),
                                         reads=[tk("wgu", slot)] + [tk("h2T", 4 * hh + q) for q in range(4)],
                                         writes=[bank(bb + hh)], signal=(kc == 7 and hh == 1))
                        P.op("act", lambda a: a.activation(out=sg[gf % 2], in_=ps[:, gb_:gb_ + 2, :], func=AF.Silu),
                             reads=[bank(gb_), bank(gb_ + 1)], writes=[tk("sg", gf % 2)])
                        P.op("dve", lambda v: v.tensor_tensor(out=aT[:, f, :].rearrange("p (a n) -> p a n", a=2),
                                                              in0=ps[:, ub_:ub_ + 2, :], in1=sg[gf % 2], op=ALU.mult),
                             reads=[bank(ub_), bank(ub_ + 1), tk("sg", gf % 2)], writes=[tk("aT", f)])
                        if f + NRING < NFH:
                            issue_gu(gf + NRING)
                    for i in range(8):
                        ob = (i % 2) * 2
                        for hh in range(2):
                            for f in range(NFH):
                                P.op("pe", lambda t, hh=hh, f=f: t.matmul(
                                    ps[:, ob + hh, :], aT[:, f, i * 128:(i + 1) * 128],
                                    wdb[:, f, hh * 512:(hh + 1) * 512], start=(f == 0), stop=(f == NFH - 1)),
                                     reads=[tk("aT", f), tk("wd", f)], writes=[bank(ob + hh)],
                                     signal=(f == NFH - 1 and hh == 1))
                        P.op("dve", lambda v: v.tensor_tensor(out=x1[:, i, :].rearrange("p (a n) -> p a n", a=2),
                                                              in0=ps[:, ob:ob + 2, :],
                                                              in1=x1[:, i, :].rearrange("p (a n) -> p a n", a=2),
                                                              op=ALU.add),
                             reads=[bank(ob), bank(ob + 1), tk("x1", i)], writes=[tk("x1", i)])
                        if half == 1:
                            P.dma("sp", yd[trow + i * 128: trow + (i + 1) * 128, :], x1[:, i, :],
                                  reads=[tk("x1", i)], slot="out%d" % i, is_out=True)
                    if half == 0:
                        for f in range(NRING):
                            issue_gu(NFH + f)
                        issue_wd(1)
            P.barrier()
        P.barrier()
        P.finish()
    return nc


def _host_constants():
    j = np.arange(128)
    ident = np.eye(128, dtype=np.float32)
    L1 = (j[:, None] >= j[None, :]).astype(np.float32)
    L2 = (j[:, None] < j[None, :]).astype(np.float32)
    negs = np.where(j[:, None] >= j[None, :], NEG, 0.0).astype(np.float32)
    negi = np.where(j[:, None] > j[None, :], NEG, 0.0).astype(np.float32)
    ones = np.ones((128, 128), np.float32)
    m01 = (j[:, None] <= j[None, :]).astype(np.float32)
    cmat = np.concatenate([ident, L1, L2, negs, negi, ones, ones / 32.0, ones / 128.0, m01], axis=1)
    half = HD // 2
    inv_freq = (np.float32(10000.0) ** (-np.arange(half, dtype=np.float32) / np.float32(half))).astype(np.float32)
    pos = np.arange(S, dtype=np.float32)
    ang = (pos[:, None] * inv_freq[None, :]).astype(np.float32)
    cos, sin = np.cos(ang.astype(np.float64)).astype(np.float32), np.sin(ang.astype(np.float64)).astype(np.float32)
    CC = np.concatenate([cos, cos], axis=1).reshape(NT, 128, 64).transpose(1, 0, 2)
    SS = np.concatenate([-sin, sin], axis=1).reshape(NT, 128, 64).transpose(1, 0, 2)
    rope = np.stack([CC, SS], axis=1).reshape(128, 2 * NT * 64).astype(np.float32)
    return np.ascontiguousarray(cmat), np.ascontiguousarray(rope)


def _prep_weights(inp):
    w_in = np.asarray(inp["w_in"], np.float32)[0]
    groups = []
    for g in range(8):
        if g < 4:
            cols = np.r_[128 * g:128 * g + 128, 512 + 128 * g:512 + 128 * g + 128,
                         1024 + 128 * g:1024 + 128 * g + 128]
        else:
            h = g - 4
            cols = np.r_[1536 + 128 * h:1536 + 128 * h + 128, 2048 + 128 * h:2048 + 128 * h + 128,
                         2560 + 128 * h:2560 + 128 * h + 128]
        wg = w_in[:, cols].reshape(8, 128, 384).transpose(1, 0, 2).reshape(128, 8 * 384)
        groups.append(wg)
    wing = np.ascontiguousarray(np.stack(groups, 0))
    wo = np.asarray(inp["w_o"], np.float32)[0].reshape(8, 128, 1024).transpose(1, 0, 2).reshape(128, 8 * 1024)
    wg_ = np.asarray(inp["w_gate"], np.float32)[0].reshape(8, 128, NF, 128)
    wu_ = np.asarray(inp["w_up"], np.float32)[0].reshape(8, 128, NF, 128)
    wgu = np.stack([wg_, wu_], 0).transpose(3, 2, 0, 1, 4).reshape(NF, 128, 2 * 8 * 128)
    wd = np.asarray(inp["w_down"], np.float32)[0].reshape(NF, 128, 1024)
    gq = np.asarray(inp["diff_q_norm_g"], np.float32)[0]
    gk = np.asarray(inp["diff_k_norm_g"], np.float32)[0]
    sw = lambda v: np.concatenate([v[32:], v[:32]])
    small = np.concatenate([
        np.asarray(inp["attn_norm_g"], np.float32)[0], np.asarray(inp["ffn_norm_g"], np.float32)[0],
        gq, sw(gq), gk, sw(gk),
        np.asarray(inp["lambda_q1"], np.float32)[0], np.asarray(inp["lambda_k1"], np.float32)[0],
        np.asarray(inp["lambda_q2"], np.float32)[0], np.asarray(inp["lambda_k2"], np.float32)[0]])
    small = np.ascontiguousarray(np.broadcast_to(small[None, :], (128, SP_N)))
    subln = np.ascontiguousarray(np.asarray(inp["diff_subln_g"], np.float32)[0].reshape(128, 1))
    return dict(wing=wing, wo=np.ascontiguousarray(wo), wgu=np.ascontiguousarray(wgu), wd=np.ascontiguousarray(wd),
                small=small, subln=subln)


def kernel(**inputs):
    x = np.asarray(inputs["x"], np.float32)
    wmaps = _prep_weights(inputs)
    cmat, rope = _host_constants()
    nc = build_program(NSEQ)
    in_maps = []
    for c in range(NCORES):
        m = dict(wmaps)
        m["x"] = np.ascontiguousarray(x[c * NSEQ:(c + 1) * NSEQ].reshape(NSEQ * S, D))
        m["cmat"] = cmat
        m["rope"] = rope
        in_maps.append(m)
    res = run_bass_kernel_spmd(nc, in_maps, core_ids=list(range(NCORES)))
    out = np.concatenate([np.asarray(r["y"], np.float32).reshape(NSEQ, S, D) for r in res.results], axis=0)
    return out
```

```python
import math
from contextlib import ExitStack

import numpy as np
import concourse.bass as bass
import concourse.mybir as mybir
from concourse.bass_utils import run_bass_kernel_spmd

F32 = mybir.dt.float32
BF16 = mybir.dt.bfloat16
AF = mybir.ActivationFunctionType
ALU = mybir.AluOpType
AX = mybir.AxisListType

NCORES = 8
D = 1024
S = 2048
BATCH = 32
NSEQ = BATCH // NCORES
DFF = 2816
NF = DFF // 128
NFH = NF // 2
HD = 64
NT = S // 128
NJ = S // 512
TT = 1024
NEG = -30000.0
LAMBDA_INIT = 0.8 - 0.6 * math.exp(-0.3 * 0)

SP_AG, SP_FG, SP_GQ, SP_GQS, SP_GK, SP_GKS, SP_L = 0, 1024, 2048, 2112, 2176, 2240, 2304
SP_N = 2304 + 256


class Tok:
    __slots__ = ("sem", "val", "eng")

    def __init__(self, sem, val, eng):
        self.sem, self.val, self.eng = sem, val, eng


class Prog:
    def __init__(self, nc, es):
        self.nc = nc
        self.E = {"pe": nc.tensor, "act": nc.scalar, "dve": nc.vector, "pool": nc.gpsimd, "sp": nc.sync}
        self.sem = {e: es.enter_context(nc.semaphore("s_" + e)) for e in self.E}
        self.cnt = {e: 0 for e in self.E}
        self.waited = {e: {} for e in self.E}
        self.last_w = {}
        self.readers = {}
        self.pending = {e: [] for e in self.E}
        self.dma_sem = {}
        self.dma_cnt = {}
        self.es = es
        self.out_toks = []
        self.all_dma = []

    def _wait(self, e, toks):
        best = {}
        for t in toks:
            if t is None:
                continue
            assert t.val is not None, "dependency on unsignaled op"
            if t.val > best.get(t.sem, 0):
                best[t.sem] = t.val
        for sname, v in best.items():
            if v > self.waited[e].get(sname, 0):
                self.E[e].wait_ge(self._semh(sname), v)
                self.waited[e][sname] = v

    def _semh(self, sname):
        return self.sem[sname] if sname in self.sem else self.dma_sem[sname]

    def _deps(self, e, reads, writes, is_dma=False, slot=None):
        deps = []
        for k in reads:
            t = self.last_w.get(k)
            if t is not None:
                if not (t.eng == e and e == "pe"):
                    deps.append(t)
            if len(k) == 2 and k[0] == "b":
                for r in self.readers.get(k, ()):
                    if r.eng != e:
                        deps.append(r)
        for k in writes:
            t = self.last_w.get(k)
            same_ok = (not is_dma) and e != "pool"
            if t is not None and not (t.eng == e and same_ok) and not (is_dma and t.sem == slot):
                deps.append(t)
            for r in self.readers.get(k, ()):
                if r.eng == e and same_ok:
                    continue
                deps.append(r)
        return deps

    def op(self, e, fn, reads=(), writes=(), signal=True):
        self._wait(e, self._deps(e, reads, writes))
        inst = fn(self.E[e])
        tok = Tok(e, None, e)
        if signal:
            inst.then_inc(self.sem[e], 1)
            self.cnt[e] += 1
            tok.val = self.cnt[e]
            for p in self.pending[e]:
                p.val = self.cnt[e]
            self.pending[e] = []
        else:
            self.pending[e].append(tok)
        for k in reads:
            self.readers.setdefault(k, []).append(tok)
        for k in writes:
            self.last_w[k] = tok
            self.readers[k] = []
        return tok

    def dma(self, q, out, in_, reads=(), writes=(), slot="d", is_out=False):
        sname = "dma_" + slot
        if sname not in self.dma_sem:
            self.dma_sem[sname] = self.es.enter_context(self.nc.semaphore(sname))
            self.dma_cnt[sname] = 0
        self._wait(q, self._deps(q, reads, writes, is_dma=True, slot=sname))
        self.E[q].dma_start(out=out, in_=in_).then_inc(self.dma_sem[sname], 16)
        self.dma_cnt[sname] += 16
        tok = Tok(sname, self.dma_cnt[sname], None)
        for k in reads:
            self.readers.setdefault(k, []).append(tok)
        for k in writes:
            self.last_w[k] = tok
            self.readers[k] = []
        if is_out:
            self.out_toks.append(tok)
        self.all_dma.append(tok)
        return tok

    def barrier(self):
        for e in self.E:
            assert not self.pending[e], "unsignaled tail on " + e
        toks = [Tok(e, self.cnt[e], e) for e in self.E if self.cnt[e] > 0]
        toks += [Tok(s, c, None) for s, c in self.dma_cnt.items()]
        for e in self.E:
            self._wait(e, [t for t in toks if t.eng != e])

    def finish(self):
        self._wait("sp", self.out_toks)


class Arena:
    log = []

    def __init__(self, t, nelem):
        self.t, self.n, self.off = t, nelem, 0

    def reset(self, off=0):
        self.off = off

    def alloc(self, free_shape, dt):
        n = int(np.prod(free_shape))
        sz = n * (2 if dt == F32 else 1)
        self.off = (self.off + 15) // 16 * 16
        a = self.t[:, self.off:self.off + sz]
        Arena.log.append((self.off, sz, str(dt), tuple(free_shape)))
        self.off += sz
        assert self.off <= self.n, ("arena overflow", self.off, self.n)
        if dt == F32:
            a = a.bitcast(F32)
        if len(free_shape) == 2:
            a = a.rearrange("p (a b) -> p a b", a=free_shape[0])
        elif len(free_shape) == 3:
            a = a.rearrange("p (a b c) -> p a b c", a=free_shape[0], b=free_shape[1])
        elif len(free_shape) == 4:
            a = a.rearrange("p (a b c d) -> p a b c d", a=free_shape[0], b=free_shape[1], c=free_shape[2])
        return a


def build_program(nseq=NSEQ):
    nc = bass.Bass("TRN2", target_bir_lowering=False)
    ntok = nseq * S
    xd = nc.dram_tensor("x", [ntok, D], F32, kind="ExternalInput").ap()
    wing = nc.dram_tensor("wing", [8, 128, 8 * 384], F32, kind="ExternalInput").ap()
    wod = nc.dram_tensor("wo", [128, 8 * 1024], F32, kind="ExternalInput").ap()
    wgud = nc.dram_tensor("wgu", [NF, 128, 2 * 8 * 128], F32, kind="ExternalInput").ap()
    wdd = nc.dram_tensor("wd", [NF, 128, 1024], F32, kind="ExternalInput").ap()
    smalld = nc.dram_tensor("small", [128, SP_N], F32, kind="ExternalInput").ap()
    sublnd = nc.dram_tensor("subln", [128, 1], F32, kind="ExternalInput").ap()
    cmatd = nc.dram_tensor("cmat", [128, 9 * 128], F32, kind="ExternalInput").ap()
    roped = nc.dram_tensor("rope", [128, 2 * NT * 64], F32, kind="ExternalInput").ap()
    yd = nc.dram_tensor("y", [ntok, D], F32, kind="ExternalOutput").ap()
    scrd = nc.dram_tensor("bc_scratch", [2, 2, 512], F32, kind="Internal").ap()

    with ExitStack() as es:
        P = Prog(nc, es)
        sb = lambda name, shape, dt: es.enter_context(nc.sbuf_tensor(name, shape, dt))
        ps = es.enter_context(nc.psum_tensor("ps", [128, 8, 512], F32))
        wo = sb("wo_sb", [128, 8, 1024], BF16)
        mixT = sb("mixT", [128, 8, S], BF16)
        small = sb("small_sb", [128, 2048], F32)
        cmat = sb("cmat_sb", [128, 9, 128], BF16)
        rtab = sb("rtab", [128, NT, 2, 2, 64], F32)
        misc = sb("misc", [128, 16], F32)
        zero_bf = sb("zero_bf", [128, 512], BF16)
        subg = sb("subg", [128, 1], F32)
        ltmp = sb("ltmp", [128, 2, 64], F32)
        ARENA = (int(nc.sbuf_bytes_remaining) - 2048) // 64 * 32
        arena_t = sb("arena", [128, ARENA], BF16)
        AR = Arena(arena_t, ARENA)
        ropeT = AR.alloc([2, NT, 64], F32)
        small2 = AR.alloc([SP_N - 2048], F32)

        ident = cmat[:, 0, :]
        L1 = cmat[:, 1, :]
        L2 = cmat[:, 2, :]
        NEGS = cmat[:, 3, :]
        NEGI = cmat[:, 4, :]
        ONES = cmat[:, 5, :]
        C32 = cmat[:, 6, :]
        C128 = cmat[:, 7, :]
        M01 = cmat[:, 8, :]
        EPS6, EPS5, ONE, LAM, NLAM = (misc[:, i:i + 1] for i in range(5))
        ps_bf7 = ps[:, 7, :].bitcast(BF16)

        def bank(b):
            return "b%d" % b

        P.dma("pool", cmat[:].rearrange("p a b -> p (a b)"), cmatd, writes=["cmat"], slot="c0")
        P.dma("sp", small[:], smalld[:, 0:2048], writes=["small"], slot="c1")
        P.dma("sp", small2, smalld[:, 2048:SP_N], writes=["small2"], slot="c5")
        P.dma("sp", ropeT.rearrange("p a t d -> p (a t d)"), roped, writes=["rope"], slot="c2")
        P.dma("sp", subg[:], sublnd, writes=["subg"], slot="c3")
        P.dma("pool", wo[:].rearrange("p a b -> p (a b)"), wod, writes=["wo"], slot="c4")
        P.op("dve", lambda v: v.memset(misc[:, 0:1], 1e-6), writes=["misc"])
        P.op("dve", lambda v: v.memset(misc[:, 1:2], 1e-5), writes=["misc"])
        P.op("dve", lambda v: v.memset(misc[:, 2:3], 1.0), writes=["misc"])
        P.op("dve", lambda v: v.memset(zero_bf[:], 0.0), writes=["zero"])
        P.op("dve", lambda v: v.tensor_scalar(out=subg[:], in0=subg[:], scalar1=1.0 - LAMBDA_INIT, scalar2=None,
                                              op0=ALU.mult), reads=["subg"], writes=["subg"])
        lv = lambda i: small2[:, SP_L - 2048 + 64 * i: SP_L - 2048 + 64 * (i + 1)]
        P.op("dve", lambda v: v.tensor_tensor(out=ltmp[:, 0, :], in0=lv(0), in1=lv(1), op=ALU.mult),
             reads=["small2"], writes=["ltmp"])
        P.op("dve", lambda v: v.tensor_tensor(out=ltmp[:, 1, :], in0=lv(2), in1=lv(3), op=ALU.mult),
             reads=["small2"], writes=["ltmp"])
        P.op("dve", lambda v: v.tensor_reduce(out=misc[:, 8:10], in_=ltmp[:], axis=AX.X, op=ALU.add),
             reads=["ltmp"], writes=["misc"])
        P.op("act", lambda a: a.activation(out=misc[:, 10:12], in_=misc[:, 8:10], func=AF.Exp),
             reads=["misc"], writes=["misc2"])
        P.op("dve", lambda v: v.tensor_tensor(out=misc[:, 12:13], in0=misc[:, 10:11], in1=misc[:, 11:12],
                                              op=ALU.subtract), reads=["misc2"], writes=["misc3"])
        P.op("dve", lambda v: v.tensor_scalar(out=misc[:, 3:4], in0=misc[:, 12:13], scalar1=LAMBDA_INIT, scalar2=None,
                                              op0=ALU.add), reads=["misc3"], writes=["misc4"])
        P.op("dve", lambda v: v.tensor_scalar(out=misc[:, 4:5], in0=misc[:, 3:4], scalar1=-1.0, scalar2=None,
                                              op0=ALU.mult), reads=["misc4"], writes=["misc5"])
        for qk, (go, gso) in enumerate(((SP_GQ, SP_GQS), (SP_GK, SP_GKS))):
            gb = small2[:, go - 2048:go - 2048 + 64].unsqueeze(1).broadcast_to([128, NT, 64])
            gsb = small2[:, gso - 2048:gso - 2048 + 64].unsqueeze(1).broadcast_to([128, NT, 64])
            P.op("pool", lambda g, gb=gb, qk=qk: g.tensor_tensor(out=rtab[:, :, qk, 0, :], in0=ropeT[:, 0, :, :],
                                                                 in1=gb, op=ALU.mult),
                 reads=["small2", "rope"], writes=["rtab"])
            P.op("pool", lambda g, gsb=gsb, qk=qk: g.tensor_tensor(out=rtab[:, :, qk, 1, :], in0=ropeT[:, 1, :, :],
                                                                   in1=gsb, op=ALU.mult),
                 reads=["small2", "rope"], writes=["rtab"])

        P.barrier()

        def rstd_from_ss(ss_ap, out_ap, n, eps_ap, rk, wk):
            P.op("act", lambda a: a.activation(out=out_ap, in_=ss_ap, func=AF.Ln, bias=eps_ap, scale=1.0 / n),
                 reads=[rk, "misc"], writes=[wk])
            P.op("act", lambda a: a.activation(out=out_ap, in_=out_ap, func=AF.Exp, scale=-0.5),
                 reads=[wk], writes=[wk])

        for b in range(nseq):
            row0 = b * S
            AR.reset()
            hT = AR.alloc([8, S], BF16)
            stat = AR.alloc([NT, 4], F32)
            wgb = [AR.alloc([8, 384], BF16) for _ in range(2)]
            qkT = [AR.alloc([2, S], BF16) for _ in range(2)]
            vtok = [AR.alloc([NT, 128], BF16) for _ in range(2)]
            Eb = [AR.alloc([2, 512], F32) for _ in range(3)]
            xs = [e_.rearrange("p a b -> p (a b)") for e_ in Eb]
            SQb = [AR.alloc([2, 512], BF16) for _ in range(2)]
            xn = [q_.rearrange("p a b -> p (a b)") for q_ in SQb]
            Fb = [AR.alloc([2, 512], F32) for _ in range(2)]
            Wb = [AR.alloc([2, 512], BF16) for _ in range(3)]
            sqj = Wb[0].rearrange("p a b -> p (a b)")
            qk32 = [AR.alloc([256], F32) for _ in range(2)]
            sqs = [AR.alloc([256], F32) for _ in range(2)]
            rt1 = [AR.alloc([256], F32) for _ in range(4)]
            rt2 = AR.alloc([256], F32)
            qkb = [AR.alloc([256], BF16) for _ in range(4)]
            dstat = [AR.alloc([8], F32) for _ in range(4)]
            Os = [AR.alloc([2, 512], F32) for _ in range(2)]
            RSs = [AR.alloc([512], F32) for _ in range(2)]
            fin = [AR.alloc([512], F32) for _ in range(3)]
            hib = AR.alloc([512], BF16)
            lob = AR.alloc([512], BF16)
            pfx = "s%d_" % b
            K = lambda *a: pfx + "_".join(str(x) for x in a)

            def phaseA_evac(i):
                for c in range(8):
                    P.op("pe", lambda t, c=c, i=i: t.transpose(ps_bf7[:, c * 128:(c + 1) * 128],
                                                               xn[i % 2][:, c * 128:(c + 1) * 128], ident),
                         reads=[K("SQ", i % 2), "cmat"], writes=[bank(7)], signal=(c == 7))
                if i % 2 == 0:
                    P.op("act", lambda a, i=i: a.activation(
                        out=hT[:, :, i * 128:(i + 1) * 128],
                        in_=ps_bf7.rearrange("p (c n) -> p c n", c=8), func=AF.Copy),
                         reads=[bank(7)], writes=[K("hT", i)])
                else:
                    P.op("dve", lambda v, i=i: v.tensor_copy(
                        out=hT[:, :, i * 128:(i + 1) * 128],
                        in_=ps_bf7.rearrange("p (c n) -> p c n", c=8)),
                         reads=[bank(7)], writes=[K("hT", i)])

            for i in range(NT):
                P.dma("sp", xs[i % 2], xd[row0 + i * 128: row0 + (i + 1) * 128, :], writes=[K("E", i % 2)],
                      slot="xs%d" % (i % 2))
                P.op("act", lambda a, i=i: a.activation(out=sqj, in_=xs[i % 2], func=AF.Square,
                                                        accum_out=stat[:, i, 0:1]),
                     reads=[K("E", i % 2)], writes=[K("W", 0), K("stat", i)])
                rstd_from_ss(stat[:, i, 0:1], stat[:, i, 1:2], D, EPS6, K("stat", i), K("rstd", i))
                P.op("dve", lambda v, i=i: v.scalar_tensor_tensor(
                    out=xn[i % 2], in0=xs[i % 2], scalar=stat[:, i, 1:2], in1=small[:, SP_AG:SP_AG + 1024],
                    op0=ALU.mult, op1=ALU.mult),
                     reads=[K("E", i % 2), K("rstd", i), "small"], writes=[K("SQ", i % 2)])
                if i >= 1:
                    phaseA_evac(i - 1)
            phaseA_evac(NT - 1)

            def load_wg(g, par):
                P.dma("pool", wgb[par].rearrange("p a b -> p (a b)"), wing[g], writes=[K("wg", par)],
                      slot="wg%d" % par)

            b7 = {"req": False, "clean": True}

            def run_stages(n_items, stages):
                done = {st[0]: 0 for st in stages}
                b7w = [(st[0], st[3]) for st in stages if st[3]]
                while any(done[st[0]] < n_items for st in stages):
                    snap = dict(done)
                    for name, fn, prods, rd7, limits in stages:
                        i = done[name]
                        if i >= n_items or any(snap[p] <= i for p in prods):
                            continue
                        if any(i - done[c] >= d for c, d in limits):
                            continue
                        if rd7 and (b7["req"] or done[rd7] < i):
                            continue
                        fn(i)
                        done[name] += 1
                    b7["clean"] = all(done[r] == done[w] for w, r in b7w)
                    yield
                b7["clean"] = True

            def inproj_sb(g, par, atomic):
                wgk = K("wg", par)
                units = []
                for j in range(NJ):
                    units += [("qk", j, 0), ("qk", j, 1), ("v", j, 0)]

                fg = (g == ORDER[0])
                bk = (lambda u: 6 + (u % 2)) if fg else (lambda u: 7)

                def mm(u):
                    kind, j, which = units[u]
                    hk = [K("hT", 4 * j + t) for t in range(4)]
                    if kind == "qk":
                        for kc in range(8):
                            P.op("pe", lambda t: t.matmul(
                                ps[:, bk(u), :], wgb[par][:, kc, which * 128:(which + 1) * 128],
                                hT[:, kc, j * 512:(j + 1) * 512], start=(kc == 0), stop=(kc == 7)),
                                 reads=[wgk] + hk, writes=[bank(bk(u))], signal=(kc == 7))
                    else:
                        for t4 in range(4):
                            for kc in range(8):
                                P.op("pe", lambda t: t.matmul(
                                    ps[:, bk(u), t4 * 128:(t4 + 1) * 128],
                                    hT[:, kc, (4 * j + t4) * 128:(4 * j + t4 + 1) * 128],
                                    wgb[par][:, kc, 256:384], start=(kc == 0), stop=(kc == 7)),
                                     reads=[wgk] + hk, writes=[bank(bk(u))], signal=(kc == 7 and t4 == 3))

                def ev(u):
                    kind, j, which = units[u]
                    if kind == "qk":
                        P.op("dve", lambda v: v.tensor_copy(
                            out=qkT[par][:, which, j * 512:(j + 1) * 512], in_=ps[:, bk(u), :]),
                             reads=[bank(bk(u))], writes=[K("qk", par, j)])
                    else:
                        P.op("dve", lambda v: v.tensor_copy(
                            out=vtok[par][:, 4 * j:4 * j + 4, :],
                            in_=ps[:, bk(u), :].rearrange("p (a n) -> p a n", a=4)),
                             reads=[bank(bk(u))], writes=[K("v", par, j)])

                if fg:
                    for u in range(len(units)):
                        mm(u)
                        if u >= 1:
                            ev(u - 1)
                        yield
                    ev(len(units) - 1)
                    yield
                else:
                    yield from run_stages(len(units), [("ev", ev, ["mm"], None, []), ("mm", mm, [], "ev", [])])

            def inproj_diff(g, par, atomic):
                wgk = K("wg", par)

                def st_mm(i):
                    for kc in range(8):
                        P.op("pe", lambda t: t.matmul(
                            ps[:, 7, 0:384], hT[:, kc, i * 128:(i + 1) * 128], wgb[par][:, kc, :],
                            start=(kc == 0), stop=(kc == 7)),
                             reads=[wgk, K("hT", i)], writes=[bank(7)], signal=(kc == 7))

                def st_ev(i):
                    k2 = i % 2
                    P.op("dve", lambda v: v.tensor_copy(out=qk32[k2], in_=ps[:, 7, 0:256]),
                         reads=[bank(7)], writes=[K("qk32", k2)])
                    P.op("dve", lambda v: v.tensor_copy(out=vtok[par][:, i, :], in_=ps[:, 7, 256:384]),
                         reads=[bank(7)], writes=[K("v", par, i // 4)])
                    P.op("dve", lambda v: v.tensor_tensor(out=sqs[k2], in0=qk32[k2], in1=qk32[k2], op=ALU.mult),
                         reads=[K("qk32", k2)], writes=[K("sqs", k2)])
                    k4 = i % 4
                    x4 = qk32[k2].rearrange("p (q m d) -> p q m d", q=2, m=2)
                    o1 = rt1[k4].rearrange("p (q m d) -> p q m d", q=2, m=2)
                    o2 = rt2.rearrange("p (q m d) -> p q m d", q=2, m=2)
                    tabA = rtab[:, i, :, 0, :].unsqueeze(2).broadcast_to([128, 2, 2, 64])
                    tabB_lo = rtab[:, i, :, 1, 0:32].unsqueeze(2).broadcast_to([128, 2, 2, 32])
                    tabB_hi = rtab[:, i, :, 1, 32:64].unsqueeze(2).broadcast_to([128, 2, 2, 32])
                    P.op("dve", lambda v: v.tensor_tensor(out=o1, in0=x4, in1=tabA, op=ALU.mult),
                         reads=[K("qk32", k2), "rtab"], writes=[K("rt1", k4)])
                    P.op("pool", lambda g_: g_.tensor_tensor(out=o2[:, :, :, 0:32], in0=x4[:, :, :, 32:64],
                                                             in1=tabB_lo, op=ALU.mult),
                         reads=[K("qk32", k2), "rtab"], writes=[K("rt2")])
                    P.op("pool", lambda g_: g_.tensor_tensor(out=o2[:, :, :, 32:64], in0=x4[:, :, :, 0:32],
                                                             in1=tabB_hi, op=ALU.mult),
                         reads=[K("qk32", k2), "rtab"], writes=[K("rt2")])

                def st_add(i):
                    k4 = i % 4
                    P.op("pool", lambda g_: g_.tensor_tensor(out=rt1[k4], in0=rt1[k4], in1=rt2, op=ALU.add),
                         reads=[K("rt1", k4), K("rt2")], writes=[K("rt1", k4)])

                def st_red(i):
                    k2, k4 = i % 2, i % 4
                    P.op("dve", lambda v: v.tensor_reduce(out=dstat[k4][:, 0:4],
                                                          in_=sqs[k2].rearrange("p (g d) -> p g d", g=4),
                                                          axis=AX.X, op=ALU.add),
                         reads=[K("sqs", k2)], writes=[K("dss", k4)])

                def st_rstd(i):
                    k4 = i % 4
                    rstd_from_ss(dstat[k4][:, 0:4], dstat[k4][:, 4:8], HD, EPS6, K("dss", k4), K("drs", k4))

                def st_fin(i):
                    k4 = i % 4
                    rb = dstat[k4][:, 4:8].unsqueeze(2).broadcast_to([128, 4, 64])
                    P.op("pool", lambda v: v.tensor_tensor(out=qkb[k4].rearrange("p (g d) -> p g d", g=4),
                                                           in0=rt1[k4].rearrange("p (g d) -> p g d", g=4),
                                                           in1=rb, op=ALU.mult),
                         reads=[K("rt1", k4), K("drs", k4)], writes=[K("qkb", k4)])

                def st_tr(i):
                    k4 = i % 4
                    for which in range(2):
                        P.op("pe", lambda t: t.transpose(
                            ps_bf7[:, 768 + which * 128: 768 + (which + 1) * 128],
                            qkb[k4][:, which * 128:(which + 1) * 128], ident),
                             reads=[K("qkb", k4), "cmat"], writes=[bank(7)], signal=(which == 1))

                def st_tev(i):
                    P.op("dve", lambda v: v.tensor_copy(
                        out=qkT[par][:, :, i * 128:(i + 1) * 128],
                        in_=ps_bf7[:, 768:1024].rearrange("p (a n) -> p a n", a=2)),
                         reads=[bank(7)], writes=[K("qk", par, i // 4)])

                yield from run_stages(NT, [
                    ("add", st_add, ["ev"], None, []),
                    ("ev", st_ev, ["mm"], None, [("fin", 4), ("red", 2), ("add", 1)]),
                    ("tev", st_tev, ["tr"], None, []),
                    ("tr", st_tr, ["w"], "tev", []),
                    ("mm", st_mm, [], "ev", []),
                    ("w", lambda i: None, ["fin"], None, []),
                    ("fin", st_fin, ["rstd", "add"], None, [("tr", 4)]),
                    ("rstd", st_rstd, ["red"], None, []),
                    ("red", st_red, ["ev"], None, [("fin", 4)]),
                ])

            bg = []

            def bg_step(n=1):
                for _ in range(n):
                    for gen in list(bg):
                        try:
                            next(gen)
                        except StopIteration:
                            bg.remove(gen)

            def bg_drain():
                while bg:
                    bg_step()

            inproj_gen = [None]
            finals = []
            os_busy = {0: False, 1: False}
            closing = [False]

            def final_worker():
                while True:
                    if finals:
                        yield from diff_final(*finals.pop(0))
                    elif closing[0]:
                        return
                    else:
                        yield

            ORDER = [0, 4, 5, 1, 6, 7, 2, 3]

            def throttle(gen, every):
                k = 0
                for _ in gen:
                    yield
                    k += 1
                    if k % every == 0:
                        yield

            def start_group(pos):
                g, par = ORDER[pos], pos % 2
                load_wg(g, par)
                during_diff = pos >= 1 and ORDER[pos - 1] >= 4
                inproj_gen[0] = inproj_sb(g, par, during_diff) if g < 4 else throttle(inproj_diff(g, par, True), 3)
                bg.append(inproj_gen[0])

            def drain_inproj():
                while inproj_gen[0] in bg:
                    bg_step()

            def sb_attention(g, par):
                q_ = qkT[par][:, 0, :]
                k_ = qkT[par][:, 1, :]
                steps = []
                for J in range(NJ):
                    nkb = 4 * J + 4
                    for kb in range(nkb - 1, -1, -1):
                        steps.append((J, kb, kb == nkb - 1, kb == 0))
                N = len(steps)

                def c0_of(J, kb):
                    return max(0, kb * 128 - 512 * J)

                def S1(i):
                    J, kb, first, last = steps[i]
                    c0 = c0_of(J, kb)
                    zb = (i % 2) * 2
                    diag = kb >= 4 * J
                    rk = [K("qk", par, J), K("qk", par, kb // 4)]
                    for h in range(2):
                        lo, hi = h * 64, (h + 1) * 64
                        kT = k_[lo:hi, kb * 128:(kb + 1) * 128]
                        if diag:
                            P.op("pe", lambda t: t.matmul(ps[:, zb + h, c0:c0 + 128], kT,
                                                          q_[lo:hi, J * 512 + c0: J * 512 + c0 + 128],
                                                          start=True, stop=False),
                                 reads=rk, writes=[bank(zb + h)], signal=False)
                            P.op("pe", lambda t: t.matmul(ps[:, zb + h, c0:c0 + 128], ident, NEGS,
                                                          start=False, stop=True),
                                 reads=["cmat"], writes=[bank(zb + h)], signal=(h == 1 and c0 + 128 >= 512))
                            if c0 + 128 < 512:
                                P.op("pe", lambda t: t.matmul(ps[:, zb + h, c0 + 128:512], kT,
                                                              q_[lo:hi, J * 512 + c0 + 128: (J + 1) * 512],
                                                              start=True, stop=True),
                                     reads=rk, writes=[bank(zb + h)], signal=(h == 1))
                        else:
                            P.op("pe", lambda t: t.matmul(ps[:, zb + h, :], kT, q_[lo:hi, J * 512:(J + 1) * 512],
                                                          start=True, stop=True),
                                 reads=rk, writes=[bank(zb + h)], signal=(h == 1))

                def S2a(i):
                    J, kb, first, last = steps[i]
                    c0 = c0_of(J, kb)
                    zb = (i % 2) * 2
                    P.op("act", lambda a: a.activation(out=Eb[i % 3][:, :, c0:], in_=ps[:, zb:zb + 2, c0:],
                                                       func=AF.Exp, scale=0.125),
                         reads=[bank(zb), bank(zb + 1)], writes=[K("E", i % 3)])

                def S2b(i):
                    J, kb, first, last = steps[i]
                    c0 = c0_of(J, kb)
                    P.op("act", lambda a: a.activation(out=SQb[i % 2][:, :, c0:], in_=Eb[i % 3][:, :, c0:],
                                                       func=AF.Ln, bias=ONE, scale=1.0),
                         reads=[K("E", i % 3), "misc"], writes=[K("SQ", i % 2)])

                def S3(i):
                    J, kb, first, last = steps[i]
                    c0 = c0_of(J, kb)
                    if first:
                        for h in range(2):
                            P.op("pe", lambda t: t.matmul(ps[:, 4 + h, :], L1, zero_bf[:], start=True, stop=True),
                                 reads=["cmat", "zero"], writes=[bank(4 + h)], signal=False)
                    for h in range(2):
                        P.op("pe", lambda t: t.matmul(ps[:, 4 + h, c0:], L1, SQb[i % 2][:, h, c0:],
                                                      start=False, stop=True, skip_group_check=True),
                             reads=["cmat", K("SQ", i % 2)], writes=[bank(4 + h)], signal=(h == 1))

                def S4(i):
                    J, kb, first, last = steps[i]
                    c0 = c0_of(J, kb)
                    P.op("act", lambda a: a.activation(out=Fb[i % 2][:, :, c0:], in_=ps[:, 4:6, c0:], func=AF.Exp,
                                                       scale=-1.0),
                         reads=[bank(4), bank(5)], writes=[K("F", i % 2)])

                def S5(i):
                    J, kb, first, last = steps[i]
                    c0 = c0_of(J, kb)
                    P.op("dve", lambda v: v.tensor_tensor(out=Wb[i % 3][:, :, c0:], in0=Eb[i % 3][:, :, c0:],
                                                          in1=Fb[i % 2][:, :, c0:], op=ALU.mult),
                         reads=[K("E", i % 3), K("F", i % 2)], writes=[K("W", i % 3)])

                def S6(i):
                    J, kb, first, last = steps[i]
                    c0 = c0_of(J, kb)
                    if last:
                        return
                    for h in range(2):
                        P.op("pe", lambda t: t.matmul(ps[:, 4 + h, c0:], L2, SQb[i % 2][:, h, c0:],
                                                      start=False, stop=True, skip_group_check=True),
                             reads=["cmat", K("SQ", i % 2)], writes=[bank(4 + h)], signal=(h == 1))

                def S7(i):
                    J, kb, first, last = steps[i]
                    c0 = c0_of(J, kb)
                    if first:
                        for h in range(2):
                            P.op("pe", lambda t: t.matmul(ps[h * 64:(h + 1) * 64, 6, :],
                                                          vtok[par][:, 0, h * 64:(h + 1) * 64], zero_bf[:],
                                                          start=True, stop=False),
                                 reads=["zero", K("v", par, 0)], writes=[bank(6)], signal=False)
                    for h in range(2):
                        P.op("pe", lambda t: t.matmul(ps[h * 64:(h + 1) * 64, 6, c0:],
                                                      vtok[par][:, kb, h * 64:(h + 1) * 64],
                                                      Wb[i % 3][:, h, c0:], start=False, stop=last),
                             reads=[K("v", par, kb // 4), K("W", i % 3)], writes=[bank(6)], signal=(h == 1))
                    if last:
                        P.op("dve", lambda v: v.tensor_copy(out=mixT[:, g, J * 512:(J + 1) * 512], in_=ps[:, 6, :]),
                             reads=[bank(6)], writes=[K("mix", g, J)])

                for n in range(-2, N + 1):
                    if 0 <= n + 2 < N:
                        S1(n + 2)
                    if 0 <= n < N:
                        S4(n)
                        S5(n)
                    if 0 <= n + 1 < N:
                        S2b(n + 1)
                    if 0 <= n + 2 < N:
                        S2a(n + 2)
                    if 0 <= n < N:
                        S6(n)
                    if 0 <= n + 1 < N:
                        S3(n + 1)
                    if 0 <= n - 1 < N:
                        S7(n - 1)
                    bg_step()

            def diff_final(g, J, k2):
                o_ = Os[k2]
                rs = RSs[k2]
                f0, f1, f2 = fin
                kk = K("fin")
                hk, lk = K("hi"), K("lo")
                P.op("act", lambda a_: a_.activation(out=rs[0:64, :], in_=rs[0:64, :], func=AF.Ln),
                     reads=[K("RS", k2)], writes=[K("RS", k2)])
                P.op("act", lambda a_: a_.activation(out=rs[0:64, :], in_=rs[0:64, :], func=AF.Exp, scale=-1.0),
                     reads=[K("RS", k2)], writes=[K("RS", k2)])
                yield
                P.dma("sp", scrd[k2, 0:1, :], rs[0:1, :], reads=[K("RS", k2)], writes=["scr%d0" % k2], slot="bw0")
                P.dma("sp", scrd[k2, 1:2, :], rs[32:33, :], reads=[K("RS", k2)], writes=["scr%d1" % k2], slot="bw1")
                yield
                yield
                yield
                P.dma("sp", f1, scrd[k2, 0:1, :].broadcast_to([128, 512]), reads=["scr%d0" % k2],
                      writes=[kk + "1"], slot="bc0")
                P.dma("sp", f2, scrd[k2, 1:2, :].broadcast_to([128, 512]), reads=["scr%d1" % k2],
                      writes=[kk + "2"], slot="bc1")
                yield
                yield
                yield
                P.op("pool", lambda g_: g_.tensor_tensor(out=f1, in0=o_[:, 0, :], in1=f1, op=ALU.mult),
                     reads=[K("Os", k2), kk + "1"], writes=[kk + "1"])
                P.op("pool", lambda g_: g_.tensor_tensor(out=f2, in0=o_[:, 1, :], in1=f2, op=ALU.mult),
                     reads=[K("Os", k2), kk + "2"], writes=[kk + "2"])
                yield
                os_busy[k2] = False
                yield
                P.op("dve", lambda v: v.scalar_tensor_tensor(out=f0, in0=f2, scalar=NLAM, in1=f1, op0=ALU.mult,
                                                             op1=ALU.add),
                     reads=[kk + "1", kk + "2", "misc5"], writes=[kk + "0"])
                yield
                P.op("pool", lambda g_: g_.tensor_tensor(out=f1, in0=f0, in1=f0, op=ALU.mult),
                     reads=[kk + "0"], writes=[kk + "1"])
                P.op("pool", lambda g_: g_.tensor_copy(out=hib, in_=f1), reads=[kk + "1"], writes=[hk])
                P.op("pool", lambda g_: g_.tensor_tensor(out=lob, in0=f1, in1=hib, op=ALU.subtract),
                     reads=[kk + "1", hk], writes=[lk])
                yield
                yield
                yield
                b7["req"] = True
                while not b7["clean"]:
                    yield
                P.op("pe", lambda t: t.matmul(ps[:, 7, :], C128, hib, start=True, stop=False),
                     reads=["cmat", hk], writes=[bank(7)], signal=False)
                P.op("pe", lambda t: t.matmul(ps[:, 7, :], C128, lob, start=False, stop=True),
                     reads=["cmat", lk], writes=[bank(7)])
                P.op("dve", lambda v: v.tensor_copy(out=f2, in_=ps[:, 7, :]),
                     reads=[bank(7)], writes=[kk + "2"])
                b7["req"] = False
                yield
                yield
                P.op("act", lambda a: a.activation(out=f2, in_=f2, func=AF.Ln, bias=EPS5, scale=1.0),
                     reads=[kk + "2", "misc"], writes=[kk + "2"])
                P.op("act", lambda a: a.activation(out=f2, in_=f2, func=AF.Exp, scale=-0.5),
                     reads=[kk + "2"], writes=[kk + "2"])
                yield
                P.op("dve", lambda v: v.scalar_tensor_tensor(out=mixT[:, g, J * 512:(J + 1) * 512], in0=f0,
                                                             scalar=subg[:, 0:1], in1=f2, op0=ALU.mult, op1=ALU.mult),
                     reads=[kk + "0", kk + "2", "subg"], writes=[K("mix", g, J)])
                yield

            def diff_attention(g, par):
                q_ = qkT[par][:, 0, :]
                k_ = qkT[par][:, 1, :]
                steps = []
                for J in range(NJ):
                    nkb = 4 * J + 4
                    for kb in range(nkb):
                        steps.append((J, kb, kb == 0, kb == nkb - 1))
                N = len(steps)
                Pb = Wb

                def c0_of(J, kb):
                    return max(0, kb * 128 - 512 * J)

                def D1(i):
                    J, kb, first, last = steps[i]
                    c0 = c0_of(J, kb)
                    zb = (i % 2) * 2
                    rk = [K("qk", par, J), K("qk", par, kb // 4)]
                    for m in range(2):
                        lo, hi = m * 64, (m + 1) * 64
                        kT = k_[lo:hi, kb * 128:(kb + 1) * 128]
                        P.op("pe", lambda t: t.matmul(ps[:, zb + m, c0:], kT, q_[lo:hi, J * 512 + c0:(J + 1) * 512],
                                                      start=True, stop=True),
                             reads=rk, writes=[bank(zb + m)], signal=(m == 1))

                def D2(i):
                    J, kb, first, last = steps[i]
                    c0 = c0_of(J, kb)
                    zb = (i % 2) * 2
                    P.op("act", lambda a: a.activation(out=Pb[i % 3][:, :, c0:], in_=ps[:, zb:zb + 2, c0:],
                                                       func=AF.Exp, scale=0.125),
                         reads=[bank(zb), bank(zb + 1)], writes=[K("W", i % 3)])
                    if kb >= 4 * J:
                        pd = Pb[i % 3][:, :, c0:c0 + 128]
                        P.op("pool", lambda g_: g_.tensor_tensor(out=pd, in0=pd,
                                                                 in1=M01.unsqueeze(1).broadcast_to([128, 2, 128]),
                                                                 op=ALU.mult),
                             reads=[K("W", i % 3), "cmat"], writes=[K("W", i % 3)])

                def D3(i):
                    J, kb, first, last = steps[i]
                    c0 = c0_of(J, kb)
                    p_ = Pb[i % 3]
                    for m in range(2):
                        P.op("pe", lambda t: t.matmul(ps[:, 4 + m, c0:], vtok[par][:, kb, :], p_[:, m, c0:],
                                                      start=first, stop=last),
                             reads=[K("v", par, kb // 4), K("W", i % 3)], writes=[bank(4 + m)], signal=False)
                    for m in range(2):
                        P.op("pe", lambda t: t.matmul(ps[32 * m:32 * m + 32, 6, c0:], ONES[:, 0:32], p_[:, m, c0:],
                                                      start=first, stop=last),
                             reads=["cmat", K("W", i % 3)], writes=[bank(6)], signal=(m == 1))
                    if last:
                        k2 = J % 2
                        while os_busy[k2]:
                            bg_step()
                        P.op("act", lambda a: a.activation(out=Os[k2], in_=ps[:, 4:6, :], func=AF.Copy),
                             reads=[bank(4), bank(5)], writes=[K("Os", k2)])
                        P.op("dve", lambda v: v.tensor_copy(out=RSs[k2][0:64, :], in_=ps[0:64, 6, :]),
                             reads=[bank(6)], writes=[K("RS", k2)])
                        os_busy[k2] = True
                        finals.append((g, J, k2))

                for n in range(-1, N + 1):
                    if 0 <= n + 1 < N:
                        D1(n + 1)
                    if 0 <= n < N:
                        D2(n)
                    if 0 <= n - 1 < N:
                        D3(n - 1)
                    bg_step()

            start_group(0)
            bg_drain()
            closing[0] = False
            bg.append(final_worker())
            for pos, g in enumerate(ORDER):
                if pos + 1 < 8:
                    start_group(pos + 1)
                if g < 4:
                    sb_attention(g, pos % 2)
                else:
                    diff_attention(g, pos % 2)
                drain_inproj()
            closing[0] = True
            bg_drain()
            P.barrier()

            AR.reset()
            x1 = AR.alloc([8, 1024], F32)
            h2T = AR.alloc([8, TT], BF16)
            aT = AR.alloc([NFH, TT], BF16)
            wdb = AR.alloc([NFH, 1024], BF16)
            NRING = 3
            wgu = [AR.alloc([2, 8, 128], BF16) for _ in range(NRING)]
            xs2 = [AR.alloc([1024], F32) for _ in range(2)]
            xn2 = [AR.alloc([1024], BF16) for _ in range(3)]
            sg = [AR.alloc([2, 512], F32) for _ in range(2)]
            sqj2 = AR.alloc([1024], BF16)
            stat2 = AR.alloc([8, 4], F32)

            for tt in range(S // TT):
                tk = K
                trow = row0 + tt * TT
                def issue_gu(gf):
                    P.dma("pool", wgu[gf % NRING].rearrange("p a b c -> p (a b c)"), wgud[gf],
                          writes=[tk("wgu", gf % NRING)], slot="wgu%d" % (gf % NRING))

                def issue_wd(half):
                    for f in range(NFH):
                        lt = P.dma("pool", wdb[:, f, :], wdd[half * NFH + f], writes=[tk("wd", f)], slot="wd")
                    for f in range(NFH):
                        P.last_w[tk("wd", f)] = lt

                for f in range(NRING):
                    issue_gu(f)
                issue_wd(0)

                def c1_post(i):
                    for c in range(8):
                        P.op("pe", lambda t, c=c: t.transpose(ps_bf7[:, c * 128:(c + 1) * 128],
                                                              xn2[i % 3][:, c * 128:(c + 1) * 128], ident),
                             reads=[tk("xn2", i % 3), "cmat"], writes=[bank(7)], signal=(c == 7))
                    P.op("act", lambda a: a.activation(out=h2T[:, :, i * 128:(i + 1) * 128],
                                                       in_=ps_bf7.rearrange("p (c n) -> p c n", c=8), func=AF.Copy),
                         reads=[bank(7)], writes=[tk("h2T", i)])

                for i in range(8):
                    Jg = (tt * TT + i * 128) // 512
                    col = tt * TT + i * 128
                    yb = (i % 2) * 2
                    P.dma("sp", xs2[i % 2], xd[trow + i * 128: trow + (i + 1) * 128, :], writes=[tk("xs2", i % 2)],
                          slot="xs%d" % (i % 2))
                    for hh in range(2):
                        for kc in range(8):
                            P.op("pe", lambda t, hh=hh, kc=kc: t.matmul(
                                ps[:, yb + hh, :], mixT[:, kc, col:col + 128], wo[:, kc, hh * 512:(hh + 1) * 512],
                                start=(kc == 0), stop=(kc == 7)),
                                 reads=[K("mix", kc, Jg), "wo"], writes=[bank(yb + hh)],
                                 signal=(kc == 7 and hh == 1))
                    P.op("dve", lambda v: v.tensor_tensor(out=x1[:, i, :].rearrange("p (a n) -> p a n", a=2),
                                                          in0=ps[:, yb:yb + 2, :],
                                                          in1=xs2[i % 2].rearrange("p (a n) -> p a n", a=2),
                                                          op=ALU.add),
                         reads=[bank(yb), bank(yb + 1), tk("xs2", i % 2)], writes=[tk("x1", i)])
                    P.op("act", lambda a: a.activation(out=sqj2, in_=x1[:, i, :], func=AF.Square,
                                                       accum_out=stat2[:, i, 0:1]),
                         reads=[tk("x1", i)], writes=[tk("sqj2"), tk("st2", i)])
                    rstd_from_ss(stat2[:, i, 0:1], stat2[:, i, 1:2], D, EPS6, tk("st2", i), tk("rs2", i))
                    P.op("dve", lambda v: v.scalar_tensor_tensor(
                        out=xn2[i % 3], in0=x1[:, i, :], scalar=stat2[:, i, 1:2], in1=small[:, SP_FG:SP_FG + 1024],
                        op0=ALU.mult, op1=ALU.mult),
                         reads=[tk("x1", i), tk("rs2", i), "small"], writes=[tk("xn2", i % 3)])
                    if i >= 2:
                        c1_post(i - 2)
                c1_post(6)
                c1_post(7)

                for half in range(2):
                    for f in range(NFH):
                        gf = half * NFH + f
                        slot = gf % NRING
                        gb_ = (gf % 2) * 2
                        ub_ = 4 + (gf % 2) * 2
                        for gu, bb in ((0, gb_), (1, ub_)):
                            for hh in range(2):
                                for kc in range(8):
                                    P.op("pe", lambda t, gu=gu, bb=bb, hh=hh, kc=kc: t.matmul(
                                        ps[:, bb + hh, :], wgu[slot][:, gu, kc, :], h2T[:, kc, hh * 512:(hh + 1) * 512],
                                        start=(kc == 0), stop=(kc == 7)),
                                         reads=[tk("wgu", slot)] + [tk("h2T", 4 * hh + q) for q in range(4)],
                                         writes=[bank(bb + hh)], signal=(kc == 7 and hh == 1))
                        P.op("act", lambda a: a.activation(out=sg[gf % 2], in_=ps[:, gb_:gb_ + 2, :], func=AF.Silu),
                             reads=[bank(gb_), bank(gb_ + 1)], writes=[tk("sg", gf % 2)])
                        P.op("dve", lambda v: v.tensor_tensor(out=aT[:, f, :].rearrange("p (a n) -> p a n", a=2),
                                                              in0=ps[:, ub_:ub_ + 2, :], in1=sg[gf % 2], op=ALU.mult),
                             reads=[bank(ub_), bank(ub_ + 1), tk("sg", gf % 2)], writes=[tk("aT", f)])
                        if f + NRING < NFH:
                            issue_gu(gf + NRING)
                    for i in range(8):
                        ob = (i % 2) * 2
                        for hh in range(2):
                            for f in range(NFH):
                                P.op("pe", lambda t, hh=hh, f=f: t.matmul(
                                    ps[:, ob + hh, :], aT[:, f, i * 128:(i + 1) * 128],
                                    wdb[:, f, hh * 512:(hh + 1) * 512], start=(f == 0), stop=(f == NFH - 1)),
                                     reads=[tk("aT", f), tk("wd", f)], writes=[bank(ob + hh)],
                                     signal=(f == NFH - 1 and hh == 1))
                        P.op("dve", lambda v: v.tensor_tensor(out=x1[:, i, :].rearrange("p (a n) -> p a n", a=2),
                                                              in0=ps[:, ob:ob + 2, :],
                                                              in1=x1[:, i, :].rearrange("p (a n) -> p a n", a=2),
                                                              op=ALU.add),
                             reads=[bank(ob), bank(ob + 1), tk("x1", i)], writes=[tk("x1", i)])
                        if half == 1:
                            P.dma("sp", yd[trow + i * 128: trow + (i + 1) * 128, :], x1[:, i, :],
                                  reads=[tk("x1", i)], slot="out%d" % i, is_out=True)
                    if half == 0:
                        for f in range(NRING):
                            issue_gu(NFH + f)
                        issue_wd(1)
            P.barrier()
        P.barrier()
        P.finish()
    return nc


def _host_constants():
    j = np.arange(128)
    ident = np.eye(128, dtype=np.float32)
    L1 = (j[:, None] >= j[None, :]).astype(np.float32)
    L2 = (j[:, None] < j[None, :]).astype(np.float32)
    negs = np.where(j[:, None] >= j[None, :], NEG, 0.0).astype(np.float32)
    negi = np.where(j[:, None] > j[None, :], NEG, 0.0).astype(np.float32)
    ones = np.ones((128, 128), np.float32)
    m01 = (j[:, None] <= j[None, :]).astype(np.float32)
    cmat = np.concatenate([ident, L1, L2, negs, negi, ones, ones / 32.0, ones / 128.0, m01], axis=1)
    half = HD // 2
    inv_freq = (np.float32(10000.0) ** (-np.arange(half, dtype=np.float32) / np.float32(half))).astype(np.float32)
    pos = np.arange(S, dtype=np.float32)
    ang = (pos[:, None] * inv_freq[None, :]).astype(np.float32)
    cos, sin = np.cos(ang.astype(np.float64)).astype(np.float32), np.sin(ang.astype(np.float64)).astype(np.float32)
    CC = np.concatenate([cos, cos], axis=1).reshape(NT, 128, 64).transpose(1, 0, 2)
    SS = np.concatenate([-sin, sin], axis=1).reshape(NT, 128, 64).transpose(1, 0, 2)
    rope = np.stack([CC, SS], axis=1).reshape(128, 2 * NT * 64).astype(np.float32)
    return np.ascontiguousarray(cmat), np.ascontiguousarray(rope)


def _prep_weights(inp):
    w_in = np.asarray(inp["w_in"], np.float32)[0]
    groups = []
    for g in range(8):
        if g < 4:
            cols = np.r_[128 * g:128 * g + 128, 512 + 128 * g:512 + 128 * g + 128,
                         1024 + 128 * g:1024 + 128 * g + 128]
        else:
            h = g - 4
            cols = np.r_[1536 + 128 * h:1536 + 128 * h + 128, 2048 + 128 * h:2048 + 128 * h + 128,
                         2560 + 128 * h:2560 + 128 * h + 128]
        wg = w_in[:, cols].reshape(8, 128, 384).transpose(1, 0, 2).reshape(128, 8 * 384)
        groups.append(wg)
    wing = np.ascontiguousarray(np.stack(groups, 0))
    wo = np.asarray(inp["w_o"], np.float32)[0].reshape(8, 128, 1024).transpose(1, 0, 2).reshape(128, 8 * 1024)
    wg_ = np.asarray(inp["w_gate"], np.float32)[0].reshape(8, 128, NF, 128)
    wu_ = np.asarray(inp["w_up"], np.float32)[0].reshape(8, 128, NF, 128)
    wgu = np.stack([wg_, wu_], 0).transpose(3, 2, 0, 1, 4).reshape(NF, 128, 2 * 8 * 128)
    wd = np.asarray(inp["w_down"], np.float32)[0].reshape(NF, 128, 1024)
    gq = np.asarray(inp["diff_q_norm_g"], np.float32)[0]
    gk = np.asarray(inp["diff_k_norm_g"], np.float32)[0]
    sw = lambda v: np.concatenate([v[32:], v[:32]])
    small = np.concatenate([
        np.asarray(inp["attn_norm_g"], np.float32)[0], np.asarray(inp["ffn_norm_g"], np.float32)[0],
        gq, sw(gq), gk, sw(gk),
        np.asarray(inp["lambda_q1"], np.float32)[0], np.asarray(inp["lambda_k1"], np.float32)[0],
        np.asarray(inp["lambda_q2"], np.float32)[0], np.asarray(inp["lambda_k2"], np.float32)[0]])
    small = np.ascontiguousarray(np.broadcast_to(small[None, :], (128, SP_N)))
    subln = np.ascontiguousarray(np.asarray(inp["diff_subln_g"], np.float32)[0].reshape(128, 1))
    return dict(wing=wing, wo=np.ascontiguousarray(wo), wgu=np.ascontiguousarray(wgu), wd=np.ascontiguousarray(wd),
                small=small, subln=subln)


def kernel(**inputs):
    x = np.asarray(inputs["x"], np.float32)
    wmaps = _prep_weights(inputs)
    cmat, rope = _host_constants()
    nc = build_program(NSEQ)
    in_maps = []
    for c in range(NCORES):
        m = dict(wmaps)
        m["x"] = np.ascontiguousarray(x[c * NSEQ:(c + 1) * NSEQ].reshape(NSEQ * S, D))
        m["cmat"] = cmat
        m["rope"] = rope
        in_maps.append(m)
    res = run_bass_kernel_spmd(nc, in_maps, core_ids=list(range(NCORES)))
    out = np.concatenate([np.asarray(r["y"], np.float32).reshape(NSEQ, S, D) for r in res.results], axis=0)
    return out
```

```python
import math
from contextlib import ExitStack

import numpy as np
import concourse.bass as bass
import concourse.mybir as mybir
from concourse.bass_utils import run_bass_kernel_spmd

F32 = mybir.dt.float32
BF16 = mybir.dt.bfloat16
AF = mybir.ActivationFunctionType
ALU = mybir.AluOpType
AX = mybir.AxisListType

NCORES = 8
D = 1024
S = 2048
BATCH = 32
NSEQ = BATCH // NCORES
DFF = 2816
NF = DFF // 128
NFH = NF // 2
HD = 64
NT = S // 128
NJ = S // 512
TT = 1024
NEG = -30000.0
LAMBDA_INIT = 0.8 - 0.6 * math.exp(-0.3 * 0)

SP_AG, SP_FG, SP_GQ, SP_GQS, SP_GK, SP_GKS, SP_L = 0, 1024, 2048, 2112, 2176, 2240, 2304
SP_N = 2304 + 256


class Tok:
    __slots__ = ("sem", "val", "eng")

    def __init__(self, sem, val, eng):
        self.sem, self.val, self.eng = sem, val, eng


class Prog:
    def __init__(self, nc, es):
        self.nc = nc
        self.E = {"pe": nc.tensor, "act": nc.scalar, "dve": nc.vector, "pool": nc.gpsimd, "sp": nc.sync}
        self.sem = {e: es.enter_context(nc.semaphore("s_" + e)) for e in self.E}
        self.cnt = {e: 0 for e in self.E}
        self.waited = {e: {} for e in self.E}
        self.last_w = {}
        self.readers = {}
        self.pending = {e: [] for e in self.E}
        self.dma_sem = {}
        self.dma_cnt = {}
        self.es = es
        self.out_toks = []
        self.all_dma = []

    def _wait(self, e, toks):
        best = {}
        for t in toks:
            if t is None:
                continue
            assert t.val is not None, "dependency on unsignaled op"
            if t.val > best.get(t.sem, 0):
                best[t.sem] = t.val
        for sname, v in best.items():
            if v > self.waited[e].get(sname, 0):
                self.E[e].wait_ge(self._semh(sname), v)
                self.waited[e][sname] = v

    def _semh(self, sname):
        return self.sem[sname] if sname in self.sem else self.dma_sem[sname]

    def _deps(self, e, reads, writes, is_dma=False, slot=None):
        deps = []
        for k in reads:
            t = self.last_w.get(k)
            if t is not None:
                if not (t.eng == e and e == "pe"):
                    deps.append(t)
            if len(k) == 2 and k[0] == "b":
                for r in self.readers.get(k, ()):
                    if r.eng != e:
                        deps.append(r)
        for k in writes:
            t = self.last_w.get(k)
            same_ok = (not is_dma) and e != "pool"
            if t is not None and not (t.eng == e and same_ok) and not (is_dma and t.sem == slot):
                deps.append(t)
            for r in self.readers.get(k, ()):
                if r.eng == e and same_ok:
                    continue
                deps.append(r)
        return deps

    def op(self, e, fn, reads=(), writes=(), signal=True):
        self._wait(e, self._deps(e, reads, writes))
        inst = fn(self.E[e])
        tok = Tok(e, None, e)
        if signal:
            inst.then_inc(self.sem[e], 1)
            self.cnt[e] += 1
            tok.val = self.cnt[e]
            for p in self.pending[e]:
                p.val = self.cnt[e]
            self.pending[e] = []
        else:
            self.pending[e].append(tok)
        for k in reads:
            self.readers.setdefault(k, []).append(tok)
        for k in writes:
            self.last_w[k] = tok
            self.readers[k] = []
        return tok

    def dma(self, q, out, in_, reads=(), writes=(), slot="d", is_out=False):
        sname = "dma_" + slot
        if sname not in self.dma_sem:
            self.dma_sem[sname] = self.es.enter_context(self.nc.semaphore(sname))
            self.dma_cnt[sname] = 0
        self._wait(q, self._deps(q, reads, writes, is_dma=True, slot=sname))
        self.E[q].dma_start(out=out, in_=in_).then_inc(self.dma_sem[sname], 16)
        self.dma_cnt[sname] += 16
        tok = Tok(sname, self.dma_cnt[sname], None)
        for k in reads:
            self.readers.setdefault(k, []).append(tok)
        for k in writes:
            self.last_w[k] = tok
            self.readers[k] = []
        if is_out:
            self.out_toks.append(tok)
        self.all_dma.append(tok)
        return tok

    def barrier(self):
        for e in self.E:
            assert not self.pending[e], "unsignaled tail on " + e
        toks = [Tok(e, self.cnt[e], e) for e in self.E if self.cnt[e] > 0]
        toks += [Tok(s, c, None) for s, c in self.dma_cnt.items()]
        for e in self.E:
            self._wait(e, [t for t in toks if t.eng != e])

    def finish(self):
        self._wait("sp", self.out_toks)


class Arena:
    log = []

    def __init__(self, t, nelem):
        self.t, self.n, self.off = t, nelem, 0

    def reset(self, off=0):
        self.off = off

    def alloc(self, free_shape, dt):
        n = int(np.prod(free_shape))
        sz = n * (2 if dt == F32 else 1)
        self.off = (self.off + 15) // 16 * 16
        a = self.t[:, self.off:self.off + sz]
        Arena.log.append((self.off, sz, str(dt), tuple(free_shape)))
        self.off += sz
        assert self.off <= self.n, ("arena overflow", self.off, self.n)
        if dt == F32:
            a = a.bitcast(F32)
        if len(free_shape) == 2:
            a = a.rearrange("p (a b) -> p a b", a=free_shape[0])
        elif len(free_shape) == 3:
            a = a.rearrange("p (a b c) -> p a b c", a=free_shape[0], b=free_shape[1])
        elif len(free_shape) == 4:
            a = a.rearrange("p (a b c d) -> p a b c d", a=free_shape[0], b=free_shape[1], c=free_shape[2])
        return a


def build_program(nseq=NSEQ):
    nc = bass.Bass("TRN2", target_bir_lowering=False)
    ntok = nseq * S
    xd = nc.dram_tensor("x", [ntok, D], F32, kind="ExternalInput").ap()
    wing = nc.dram_tensor("wing", [8, 128, 8 * 384], F32, kind="ExternalInput").ap()
    wod = nc.dram_tensor("wo", [128, 8 * 1024], F32, kind="ExternalInput").ap()
    wgud = nc.dram_tensor("wgu", [NF, 128, 2 * 8 * 128], F32, kind="ExternalInput").ap()
    wdd = nc.dram_tensor("wd", [NF, 128, 1024], F32, kind="ExternalInput").ap()
    smalld = nc.dram_tensor("small", [128, SP_N], F32, kind="ExternalInput").ap()
    sublnd = nc.dram_tensor("subln", [128, 1], F32, kind="ExternalInput").ap()
    cmatd = nc.dram_tensor("cmat", [128, 9 * 128], F32, kind="ExternalInput").ap()
    roped = nc.dram_tensor("rope", [128, 2 * NT * 64], F32, kind="ExternalInput").ap()
    yd = nc.dram_tensor("y", [ntok, D], F32, kind="ExternalOutput").ap()
    scrd = nc.dram_tensor("bc_scratch", [2, 2, 512], F32, kind="Internal").ap()

    with ExitStack() as es:
        P = Prog(nc, es)
        sb = lambda name, shape, dt: es.enter_context(nc.sbuf_tensor(name, shape, dt))
        ps = es.enter_context(nc.psum_tensor("ps", [128, 8, 512], F32))
        wo = sb("wo_sb", [128, 8, 1024], BF16)
        mixT = sb("mixT", [128, 8, S], BF16)
        small = sb("small_sb", [128, 2048], F32)
        cmat = sb("cmat_sb", [128, 9, 128], BF16)
        rtab = sb("rtab", [128, NT, 2, 2, 64], F32)
        misc = sb("misc", [128, 16], F32)
        zero_bf = sb("zero_bf", [128, 512], BF16)
        subg = sb("subg", [128, 1], F32)
        ltmp = sb("ltmp", [128, 2, 64], F32)
        ARENA = (int(nc.sbuf_bytes_remaining) - 2048) // 64 * 32
        arena_t = sb("arena", [128, ARENA], BF16)
        AR = Arena(arena_t, ARENA)
        ropeT = AR.alloc([2, NT, 64], F32)
        small2 = AR.alloc([SP_N - 2048], F32)

        ident = cmat[:, 0, :]
        L1 = cmat[:, 1, :]
        L2 = cmat[:, 2, :]
        NEGS = cmat[:, 3, :]
        NEGI = cmat[:, 4, :]
        ONES = cmat[:, 5, :]
        C32 = cmat[:, 6, :]
        C128 = cmat[:, 7, :]
        M01 = cmat[:, 8, :]
        EPS6, EPS5, ONE, LAM, NLAM = (misc[:, i:i + 1] for i in range(5))
        ps_bf7 = ps[:, 7, :].bitcast(BF16)

        def bank(b):
            return "b%d" % b

        P.dma("pool", cmat[:].rearrange("p a b -> p (a b)"), cmatd, writes=["cmat"], slot="c0")
        P.dma("sp", small[:], smalld[:, 0:2048], writes=["small"], slot="c1")
        P.dma("sp", small2, smalld[:, 2048:SP_N], writes=["small2"], slot="c5")
        P.dma("sp", ropeT.rearrange("p a t d -> p (a t d)"), roped, writes=["rope"], slot="c2")
        P.dma("sp", subg[:], sublnd, writes=["subg"], slot="c3")
        P.dma("pool", wo[:].rearrange("p a b -> p (a b)"), wod, writes=["wo"], slot="c4")
        P.op("dve", lambda v: v.memset(misc[:, 0:1], 1e-6), writes=["misc"])
        P.op("dve", lambda v: v.memset(misc[:, 1:2], 1e-5), writes=["misc"])
        P.op("dve", lambda v: v.memset(misc[:, 2:3], 1.0), writes=["misc"])
        P.op("dve", lambda v: v.memset(zero_bf[:], 0.0), writes=["zero"])
        P.op("dve", lambda v: v.tensor_scalar(out=subg[:], in0=subg[:], scalar1=1.0 - LAMBDA_INIT, scalar2=None,
                                              op0=ALU.mult), reads=["subg"], writes=["subg"])
        lv = lambda i: small2[:, SP_L - 2048 + 64 * i: SP_L - 2048 + 64 * (i + 1)]
        P.op("dve", lambda v: v.tensor_tensor(out=ltmp[:, 0, :], in0=lv(0), in1=lv(1), op=ALU.mult),
             reads=["small2"], writes=["ltmp"])
        P.op("dve", lambda v: v.tensor_tensor(out=ltmp[:, 1, :], in0=lv(2), in1=lv(3), op=ALU.mult),
             reads=["small2"], writes=["ltmp"])
        P.op("dve", lambda v: v.tensor_reduce(out=misc[:, 8:10], in_=ltmp[:], axis=AX.X, op=ALU.add),
             reads=["ltmp"], writes=["misc"])
        P.op("act", lambda a: a.activation(out=misc[:, 10:12], in_=misc[:, 8:10], func=AF.Exp),
             reads=["misc"], writes=["misc2"])
        P.op("dve", lambda v: v.tensor_tensor(out=misc[:, 12:13], in0=misc[:, 10:11], in1=misc[:, 11:12],
                                              op=ALU.subtract), reads=["misc2"], writes=["misc3"])
        P.op("dve", lambda v: v.tensor_scalar(out=misc[:, 3:4], in0=misc[:, 12:13], scalar1=LAMBDA_INIT, scalar2=None,
                                              op0=ALU.add), reads=["misc3"], writes=["misc4"])
        P.op("dve", lambda v: v.tensor_scalar(out=misc[:, 4:5], in0=misc[:, 3:4], scalar1=-1.0, scalar2=None,
                                              op0=ALU.mult), reads=["misc4"], writes=["misc5"])
        for qk, (go, gso) in enumerate(((SP_GQ, SP_GQS), (SP_GK, SP_GKS))):
            gb = small2[:, go - 2048:go - 2048 + 64].unsqueeze(1).broadcast_to([128, NT, 64])
            gsb = small2[:, gso - 2048:gso - 2048 + 64].unsqueeze(1).broadcast_to([128, NT, 64])
            P.op("pool", lambda g, gb=gb, qk=qk: g.tensor_tensor(out=rtab[:, :, qk, 0, :], in0=ropeT[:, 0, :, :],
                                                                 in1=gb, op=ALU.mult),
                 reads=["small2", "rope"], writes=["rtab"])
            P.op("pool", lambda g, gsb=gsb, qk=qk: g.tensor_tensor(out=rtab[:, :, qk, 1, :], in0=ropeT[:, 1, :, :],
                                                                   in1=gsb, op=ALU.mult),
                 reads=["small2", "rope"], writes=["rtab"])

        P.barrier()

        def rstd_from_ss(ss_ap, out_ap, n, eps_ap, rk, wk):
            P.op("act", lambda a: a.activation(out=out_ap, in_=ss_ap, func=AF.Ln, bias=eps_ap, scale=1.0 / n),
                 reads=[rk, "misc"], writes=[wk])
            P.op("act", lambda a: a.activation(out=out_ap, in_=out_ap, func=AF.Exp, scale=-0.5),
                 reads=[wk], writes=[wk])

        for b in range(nseq):
            row0 = b * S
            AR.reset()
            hT = AR.alloc([8, S], BF16)
            stat = AR.alloc([NT, 4], F32)
            wgb = [AR.alloc([8, 384], BF16) for _ in range(2)]
            qkT = [AR.alloc([2, S], BF16) for _ in range(2)]
            vtok = [AR.alloc([NT, 128], BF16) for _ in range(2)]
            Eb = [AR.alloc([2, 512], F32) for _ in range(3)]
            xs = [e_.rearrange("p a b -> p (a b)") for e_ in Eb]
            SQb = [AR.alloc([2, 512], BF16) for _ in range(2)]
            xn = [q_.rearrange("p a b -> p (a b)") for q_ in SQb]
            Fb = [AR.alloc([2, 512], F32) for _ in range(2)]
            Wb = [AR.alloc([2, 512], BF16) for _ in range(3)]
            sqj = Wb[0].rearrange("p a b -> p (a b)")
            qk32 = [AR.alloc([256], F32) for _ in range(2)]
            sqs = [AR.alloc([256], F32) for _ in range(2)]
            rt1 = [AR.alloc([256], F32) for _ in range(4)]
            rt2 = AR.alloc([256], F32)
            qkb = [AR.alloc([256], BF16) for _ in range(4)]
            dstat = [AR.alloc([8], F32) for _ in range(4)]
            Os = [AR.alloc([2, 512], F32) for _ in range(2)]
            RSs = [AR.alloc([512], F32) for _ in range(2)]
            fin = [AR.alloc([512], F32) for _ in range(3)]
            hib = AR.alloc([512], BF16)
            lob = AR.alloc([512], BF16)
            pfx = "s%d_" % b
            K = lambda *a: pfx + "_".join(str(x) for x in a)

            def phaseA_evac(i):
                for c in range(8):
                    P.op("pe", lambda t, c=c, i=i: t.transpose(ps_bf7[:, c * 128:(c + 1) * 128],
                                                               xn[i % 2][:, c * 128:(c + 1) * 128], ident),
                         reads=[K("SQ", i % 2), "cmat"], writes=[bank(7)], signal=(c == 7))
                if i % 2 == 0:
                    P.op("act", lambda a, i=i: a.activation(
                        out=hT[:, :, i * 128:(i + 1) * 128],
                        in_=ps_bf7.rearrange("p (c n) -> p c n", c=8), func=AF.Copy),
                         reads=[bank(7)], writes=[K("hT", i)])
                else:
                    P.op("dve", lambda v, i=i: v.tensor_copy(
                        out=hT[:, :, i * 128:(i + 1) * 128],
                        in_=ps_bf7.rearrange("p (c n) -> p c n", c=8)),
                         reads=[bank(7)], writes=[K("hT", i)])

            for i in range(NT):
                P.dma("sp", xs[i % 2], xd[row0 + i * 128: row0 + (i + 1) * 128, :], writes=[K("E", i % 2)],
                      slot="xs%d" % (i % 2))
                P.op("act", lambda a, i=i: a.activation(out=sqj, in_=xs[i % 2], func=AF.Square,
                                                        accum_out=stat[:, i, 0:1]),
                     reads=[K("E", i % 2)], writes=[K("W", 0), K("stat", i)])
                rstd_from_ss(stat[:, i, 0:1], stat[:, i, 1:2], D, EPS6, K("stat", i), K("rstd", i))
                P.op("dve", lambda v, i=i: v.scalar_tensor_tensor(
                    out=xn[i % 2], in0=xs[i % 2], scalar=stat[:, i, 1:2], in1=small[:, SP_AG:SP_AG + 1024],
                    op0=ALU.mult, op1=ALU.mult),
                     reads=[K("E", i % 2), K("rstd", i), "small"], writes=[K("SQ", i % 2)])
                if i >= 1:
                    phaseA_evac(i - 1)
            phaseA_evac(NT - 1)

            def load_wg(g, par):
                P.dma("pool", wgb[par].rearrange("p a b -> p (a b)"), wing[g], writes=[K("wg", par)],
                      slot="wg%d" % par)

            b7 = {"req": False, "clean": True}

            def run_stages(n_items, stages):
                done = {st[0]: 0 for st in stages}
                b7w = [(st[0], st[3]) for st in stages if st[3]]
                while any(done[st[0]] < n_items for st in stages):
                    snap = dict(done)
                    for name, fn, prods, rd7, limits in stages:
                        i = done[name]
                        if i >= n_items or any(snap[p] <= i for p in prods):
                            continue
                        if any(i - done[c] >= d for c, d in limits):
                            continue
                        if rd7 and (b7["req"] or done[rd7] < i):
                            continue
                        fn(i)
                        done[name] += 1
                    b7["clean"] = all(done[r] == done[w] for w, r in b7w)
                    yield
                b7["clean"] = True

            def inproj_sb(g, par, atomic):
                wgk = K("wg", par)
                units = []
                for j in range(NJ):
                    units += [("qk", j, 0), ("qk", j, 1), ("v", j, 0)]

                fg = (g == ORDER[0])
                bk = (lambda u: 6 + (u % 2)) if fg else (lambda u: 7)

                def mm(u):
                    kind, j, which = units[u]
                    hk = [K("hT", 4 * j + t) for t in range(4)]
                    if kind == "qk":
                        for kc in range(8):
                            P.op("pe", lambda t: t.matmul(
                                ps[:, bk(u), :], wgb[par][:, kc, which * 128:(which + 1) * 128],
                                hT[:, kc, j * 512:(j + 1) * 512], start=(kc == 0), stop=(kc == 7)),
                                 reads=[wgk] + hk, writes=[bank(bk(u))], signal=(kc == 7))
                    else:
                        for t4 in range(4):
                            for kc in range(8):
                                P.op("pe", lambda t: t.matmul(
                                    ps[:, bk(u), t4 * 128:(t4 + 1) * 128],
                                    hT[:, kc, (4 * j + t4) * 128:(4 * j + t4 + 1) * 128],
                                    wgb[par][:, kc, 256:384], start=(kc == 0), stop=(kc == 7)),
                                     reads=[wgk] + hk, writes=[bank(bk(u))], signal=(kc == 7 and t4 == 3))

                def ev(u):
                    kind, j, which = units[u]
                    if kind == "qk":
                        P.op("dve", lambda v: v.tensor_copy(
                            out=qkT[par][:, which, j * 512:(j + 1) * 512], in_=ps[:, bk(u), :]),
                             reads=[bank(bk(u))], writes=[K("qk", par, j)])
                    else:
                        P.op("dve", lambda v: v.tensor_copy(
                            out=vtok[par][:, 4 * j:4 * j + 4, :],
                            in_=ps[:, bk(u), :].rearrange("p (a n) -> p a n", a=4)),
                             reads=[bank(bk(u))], writes=[K("v", par, j)])

                if fg:
                    for u in range(len(units)):
                        mm(u)
                        if u >= 1:
                            ev(u - 1)
                        yield
                    ev(len(units) - 1)
                    yield
                else:
                    yield from run_stages(len(units), [("ev", ev, ["mm"], None, []), ("mm", mm, [], "ev", [])])

            def inproj_diff(g, par, atomic):
                wgk = K("wg", par)

                def st_mm(i):
                    for kc in range(8):
                        P.op("pe", lambda t: t.matmul(
                            ps[:, 7, 0:384], hT[:, kc, i * 128:(i + 1) * 128], wgb[par][:, kc, :],
                            start=(kc == 0), stop=(kc == 7)),
                             reads=[wgk, K("hT", i)], writes=[bank(7)], signal=(kc == 7))

                def st_ev(i):
                    k2 = i % 2
                    P.op("dve", lambda v: v.tensor_copy(out=qk32[k2], in_=ps[:, 7, 0:256]),
                         reads=[bank(7)], writes=[K("qk32", k2)])
                    P.op("dve", lambda v: v.tensor_copy(out=vtok[par][:, i, :], in_=ps[:, 7, 256:384]),
                         reads=[bank(7)], writes=[K("v", par, i // 4)])
                    P.op("dve", lambda v: v.tensor_tensor(out=sqs[k2], in0=qk32[k2], in1=qk32[k2], op=ALU.mult),
                         reads=[K("qk32", k2)], writes=[K("sqs", k2)])
                    k4 = i % 4
                    x4 = qk32[k2].rearrange("p (q m d) -> p q m d", q=2, m=2)
                    o1 = rt1[k4].rearrange("p (q m d) -> p q m d", q=2, m=2)
                    o2 = rt2.rearrange("p (q m d) -> p q m d", q=2, m=2)
                    tabA = rtab[:, i, :, 0, :].unsqueeze(2).broadcast_to([128, 2, 2, 64])
                    tabB_lo = rtab[:, i, :, 1, 0:32].unsqueeze(2).broadcast_to([128, 2, 2, 32])
                    tabB_hi = rtab[:, i, :, 1, 32:64].unsqueeze(2).broadcast_to([128, 2, 2, 32])
                    P.op("dve", lambda v: v.tensor_tensor(out=o1, in0=x4, in1=tabA, op=ALU.mult),
                         reads=[K("qk32", k2), "rtab"], writes=[K("rt1", k4)])
                    P.op("pool", lambda g_: g_.tensor_tensor(out=o2[:, :, :, 0:32], in0=x4[:, :, :, 32:64],
                                                             in1=tabB_lo, op=ALU.mult),
                         reads=[K("qk32", k2), "rtab"], writes=[K("rt2")])
                    P.op("pool", lambda g_: g_.tensor_tensor(out=o2[:, :, :, 32:64], in0=x4[:, :, :, 0:32],
                                                             in1=tabB_hi, op=ALU.mult),
                         reads=[K("qk32", k2), "rtab"], writes=[K("rt2")])

                def st_add(i):
                    k4 = i % 4
                    P.op("pool", lambda g_: g_.tensor_tensor(out=rt1[k4], in0=rt1[k4], in1=rt2, op=ALU.add),
                         reads=[K("rt1", k4), K("rt2")], writes=[K("rt1", k4)])

                def st_red(i):
                    k2, k4 = i % 2, i % 4
                    P.op("dve", lambda v: v.tensor_reduce(out=dstat[k4][:, 0:4],
                                                          in_=sqs[k2].rearrange("p (g d) -> p g d", g=4),
                                                          axis=AX.X, op=ALU.add),
                         reads=[K("sqs", k2)], writes=[K("dss", k4)])

                def st_rstd(i):
                    k4 = i % 4
                    rstd_from_ss(dstat[k4][:, 0:4], dstat[k4][:, 4:8], HD, EPS6, K("dss", k4), K("drs", k4))

                def st_fin(i):
                    k4 = i % 4
                    rb = dstat[k4][:, 4:8].unsqueeze(2).broadcast_to([128, 4, 64])
                    P.op("pool", lambda v: v.tensor_tensor(out=qkb[k4].rearrange("p (g d) -> p g d", g=4),
                                                           in0=rt1[k4].rearrange("p (g d) -> p g d", g=4),
                                                           in1=rb, op=ALU.mult),
                         reads=[K("rt1", k4), K("drs", k4)], writes=[K("qkb", k4)])

                def st_tr(i):
                    k4 = i % 4
                    for which in range(2):
                        P.op("pe", lambda t: t.transpose(
                            ps_bf7[:, 768 + which * 128: 768 + (which + 1) * 128],
                            qkb[k4][:, which * 128:(which + 1) * 128], ident),
                             reads=[K("qkb", k4), "cmat"], writes=[bank(7)], signal=(which == 1))

                def st_tev(i):
                    P.op("dve", lambda v: v.tensor_copy(
                        out=qkT[par][:, :, i * 128:(i + 1) * 128],
                        in_=ps_bf7[:, 768:1024].rearrange("p (a n) -> p a n", a=2)),
                         reads=[bank(7)], writes=[K("qk", par, i // 4)])

                yield from run_stages(NT, [
                    ("add", st_add, ["ev"], None, []),
                    ("ev", st_ev, ["mm"], None, [("fin", 4), ("red", 2), ("add", 1)]),
                    ("tev", st_tev, ["tr"], None, []),
                    ("tr", st_tr, ["w"], "tev", []),
                    ("mm", st_mm, [], "ev", []),
                    ("w", lambda i: None, ["fin"], None, []),
                    ("fin", st_fin, ["rstd", "add"], None, [("tr", 4)]),
                    ("rstd", st_rstd, ["red"], None, []),
                    ("red", st_red, ["ev"], None, [("fin", 4)]),
                ])

            bg = []

            def bg_step(n=1):
                for _ in range(n):
                    for gen in list(bg):
                        try:
                            next(gen)
                        except StopIteration:
                            bg.remove(gen)

            def bg_drain():
                while bg:
                    bg_step()

            inproj_gen = [None]
            finals = []
            os_busy = {0: False, 1: False}
            closing = [False]

            def final_worker():
                while True:
                    if finals:
                        yield from diff_final(*finals.pop(0))
                    elif closing[0]:
                        return
                    else:
                        yield

            ORDER = [0, 4, 1, 5, 2, 6, 7, 3]

            def throttle(gen, every):
                k = 0
                for _ in gen:
                    yield
                    k += 1
                    if k % every == 0:
                        yield

            def start_group(pos):
                g, par = ORDER[pos], pos % 2
                load_wg(g, par)
                during_diff = pos >= 1 and ORDER[pos - 1] >= 4
                inproj_gen[0] = inproj_sb(g, par, during_diff) if g < 4 else throttle(inproj_diff(g, par, True), 2)
                bg.append(inproj_gen[0])

            def drain_inproj():
                while inproj_gen[0] in bg:
                    bg_step()

            def sb_attention(g, par):
                q_ = qkT[par][:, 0, :]
                k_ = qkT[par][:, 1, :]
                steps = []
                for J in range(NJ):
                    nkb = 4 * J + 4
                    for kb in range(nkb - 1, -1, -1):
                        steps.append((J, kb, kb == nkb - 1, kb == 0))
                N = len(steps)

                def c0_of(J, kb):
                    return max(0, kb * 128 - 512 * J)

                def S1(i):
                    J, kb, first, last = steps[i]
                    c0 = c0_of(J, kb)
                    zb = (i % 2) * 2
                    diag = kb >= 4 * J
                    rk = [K("qk", par, J), K("qk", par, kb // 4)]
                    for h in range(2):
                        lo, hi = h * 64, (h + 1) * 64
                        kT = k_[lo:hi, kb * 128:(kb + 1) * 128]
                        if diag:
                            P.op("pe", lambda t: t.matmul(ps[:, zb + h, c0:c0 + 128], kT,
                                                          q_[lo:hi, J * 512 + c0: J * 512 + c0 + 128],
                                                          start=True, stop=False),
                                 reads=rk, writes=[bank(zb + h)], signal=False)
                            P.op("pe", lambda t: t.matmul(ps[:, zb + h, c0:c0 + 128], ident, NEGS,
                                                          start=False, stop=True),
                                 reads=["cmat"], writes=[bank(zb + h)], signal=(h == 1 and c0 + 128 >= 512))
                            if c0 + 128 < 512:
                                P.op("pe", lambda t: t.matmul(ps[:, zb + h, c0 + 128:512], kT,
                                                              q_[lo:hi, J * 512 + c0 + 128: (J + 1) * 512],
                                                              start=True, stop=True),
                                     reads=rk, writes=[bank(zb + h)], signal=(h == 1))
                        else:
                            P.op("pe", lambda t: t.matmul(ps[:, zb + h, :], kT, q_[lo:hi, J * 512:(J + 1) * 512],
                                                          start=True, stop=True),
                                 reads=rk, writes=[bank(zb + h)], signal=(h == 1))

                def S2a(i):
                    J, kb, first, last = steps[i]
                    c0 = c0_of(J, kb)
                    zb = (i % 2) * 2
                    P.op("act", lambda a: a.activation(out=Eb[i % 3][:, :, c0:], in_=ps[:, zb:zb + 2, c0:],
                                                       func=AF.Exp, scale=0.125),
                         reads=[bank(zb), bank(zb + 1)], writes=[K("E", i % 3)])

                def S2b(i):
                    J, kb, first, last = steps[i]
                    c0 = c0_of(J, kb)
                    P.op("act", lambda a: a.activation(out=SQb[i % 2][:, :, c0:], in_=Eb[i % 3][:, :, c0:],
                                                       func=AF.Ln, bias=ONE, scale=1.0),
                         reads=[K("E", i % 3), "misc"], writes=[K("SQ", i % 2)])

                def S3(i):
                    J, kb, first, last = steps[i]
                    c0 = c0_of(J, kb)
                    if first:
                        for h in range(2):
                            P.op("pe", lambda t: t.matmul(ps[:, 4 + h, :], L1, zero_bf[:], start=True, stop=True),
                                 reads=["cmat", "zero"], writes=[bank(4 + h)], signal=False)
                    for h in range(2):
                        P.op("pe", lambda t: t.matmul(ps[:, 4 + h, c0:], L1, SQb[i % 2][:, h, c0:],
                                                      start=False, stop=True, skip_group_check=True),
                             reads=["cmat", K("SQ", i % 2)], writes=[bank(4 + h)], signal=(h == 1))

                def S4(i):
                    J, kb, first, last = steps[i]
                    c0 = c0_of(J, kb)
                    P.op("act", lambda a: a.activation(out=Fb[i % 2][:, :, c0:], in_=ps[:, 4:6, c0:], func=AF.Exp,
                                                       scale=-1.0),
                         reads=[bank(4), bank(5)], writes=[K("F", i % 2)])

                def S5(i):
                    J, kb, first, last = steps[i]
                    c0 = c0_of(J, kb)
                    P.op("dve", lambda v: v.tensor_tensor(out=Wb[i % 3][:, :, c0:], in0=Eb[i % 3][:, :, c0:],
                                                          in1=Fb[i % 2][:, :, c0:], op=ALU.mult),
                         reads=[K("E", i % 3), K("F", i % 2)], writes=[K("W", i % 3)])

                def S6(i):
                    J, kb, first, last = steps[i]
                    c0 = c0_of(J, kb)
                    if last:
                        return
                    for h in range(2):
                        P.op("pe", lambda t: t.matmul(ps[:, 4 + h, c0:], L2, SQb[i % 2][:, h, c0:],
                                                      start=False, stop=True, skip_group_check=True),
                             reads=["cmat", K("SQ", i % 2)], writes=[bank(4 + h)], signal=(h == 1))

                def S7(i):
                    J, kb, first, last = steps[i]
                    c0 = c0_of(J, kb)
                    if first:
                        for h in range(2):
                            P.op("pe", lambda t: t.matmul(ps[h * 64:(h + 1) * 64, 6, :],
                                                          vtok[par][:, 0, h * 64:(h + 1) * 64], zero_bf[:],
                                                          start=True, stop=False),
                                 reads=["zero", K("v", par, 0)], writes=[bank(6)], signal=False)
                    for h in range(2):
                        P.op("pe", lambda t: t.matmul(ps[h * 64:(h + 1) * 64, 6, c0:],
                                                      vtok[par][:, kb, h * 64:(h + 1) * 64],
                                                      Wb[i % 3][:, h, c0:], start=False, stop=last),
                             reads=[K("v", par, kb // 4), K("W", i % 3)], writes=[bank(6)], signal=(h == 1))
                    if last:
                        P.op("dve", lambda v: v.tensor_copy(out=mixT[:, g, J * 512:(J + 1) * 512], in_=ps[:, 6, :]),
                             reads=[bank(6)], writes=[K("mix", g, J)])

                for n in range(-2, N + 1):
                    if 0 <= n + 2 < N:
                        S1(n + 2)
                    if 0 <= n < N:
                        S4(n)
                        S5(n)
                    if 0 <= n + 1 < N:
                        S2b(n + 1)
                    if 0 <= n + 2 < N:
                        S2a(n + 2)
                    if 0 <= n < N:
                        S6(n)
                    if 0 <= n + 1 < N:
                        S3(n + 1)
                    if 0 <= n - 1 < N:
                        S7(n - 1)
                    bg_step()

            def diff_final(g, J, k2):
                o_ = Os[k2]
                rs = RSs[k2]
                f0, f1, f2 = fin
                kk = K("fin")
                hk, lk = K("hi"), K("lo")
                P.op("act", lambda a_: a_.activation(out=rs[0:64, :], in_=rs[0:64, :], func=AF.Ln),
                     reads=[K("RS", k2)], writes=[K("RS", k2)])
                P.op("act", lambda a_: a_.activation(out=rs[0:64, :], in_=rs[0:64, :], func=AF.Exp, scale=-1.0),
                     reads=[K("RS", k2)], writes=[K("RS", k2)])
                yield
                P.dma("sp", scrd[k2, 0:1, :], rs[0:1, :], reads=[K("RS", k2)], writes=["scr%d0" % k2], slot="bw0")
                P.dma("sp", scrd[k2, 1:2, :], rs[32:33, :], reads=[K("RS", k2)], writes=["scr%d1" % k2], slot="bw1")
                yield
                yield
                yield
                P.dma("sp", f1, scrd[k2, 0:1, :].broadcast_to([128, 512]), reads=["scr%d0" % k2],
                      writes=[kk + "1"], slot="bc0")
                P.dma("sp", f2, scrd[k2, 1:2, :].broadcast_to([128, 512]), reads=["scr%d1" % k2],
                      writes=[kk + "2"], slot="bc1")
                yield
                yield
                yield
                P.op("pool", lambda g_: g_.tensor_tensor(out=f1, in0=o_[:, 0, :], in1=f1, op=ALU.mult),
                     reads=[K("Os", k2), kk + "1"], writes=[kk + "1"])
                P.op("pool", lambda g_: g_.tensor_tensor(out=f2, in0=o_[:, 1, :], in1=f2, op=ALU.mult),
                     reads=[K("Os", k2), kk + "2"], writes=[kk + "2"])
                yield
                os_busy[k2] = False
                yield
                P.op("dve", lambda v: v.scalar_tensor_tensor(out=f0, in0=f2, scalar=NLAM, in1=f1, op0=ALU.mult,
                                                             op1=ALU.add),
                     reads=[kk + "1", kk + "2", "misc5"], writes=[kk + "0"])
                yield
                P.op("pool", lambda g_: g_.tensor_tensor(out=f1, in0=f0, in1=f0, op=ALU.mult),
                     reads=[kk + "0"], writes=[kk + "1"])
                P.op("pool", lambda g_: g_.tensor_copy(out=hib, in_=f1), reads=[kk + "1"], writes=[hk])
                P.op("pool", lambda g_: g_.tensor_tensor(out=lob, in0=f1, in1=hib, op=ALU.subtract),
                     reads=[kk + "1", hk], writes=[lk])
                yield
                yield
                yield
                b7["req"] = True
                while not b7["clean"]:
                    yield
                P.op("pe", lambda t: t.matmul(ps[:, 7, :], C128, hib, start=True, stop=False),
                     reads=["cmat", hk], writes=[bank(7)], signal=False)
                P.op("pe", lambda t: t.matmul(ps[:, 7, :], C128, lob, start=False, stop=True),
                     reads=["cmat", lk], writes=[bank(7)])
                P.op("dve", lambda v: v.tensor_copy(out=f2, in_=ps[:, 7, :]),
                     reads=[bank(7)], writes=[kk + "2"])
                b7["req"] = False
                yield
                yield
                P.op("act", lambda a: a.activation(out=f2, in_=f2, func=AF.Ln, bias=EPS5, scale=1.0),
                     reads=[kk + "2", "misc"], writes=[kk + "2"])
                P.op("act", lambda a: a.activation(out=f2, in_=f2, func=AF.Exp, scale=-0.5),
                     reads=[kk + "2"], writes=[kk + "2"])
                yield
                P.op("dve", lambda v: v.scalar_tensor_tensor(out=mixT[:, g, J * 512:(J + 1) * 512], in0=f0,
                                                             scalar=subg[:, 0:1], in1=f2, op0=ALU.mult, op1=ALU.mult),
                     reads=[kk + "0", kk + "2", "subg"], writes=[K("mix", g, J)])
                yield

            def diff_attention(g, par):
                q_ = qkT[par][:, 0, :]
                k_ = qkT[par][:, 1, :]
                steps = []
                for J in range(NJ):
                    nkb = 4 * J + 4
                    for kb in range(nkb):
                        steps.append((J, kb, kb == 0, kb == nkb - 1))
                N = len(steps)
                Pb = Wb

                def c0_of(J, kb):
                    return max(0, kb * 128 - 512 * J)

                def D1(i):
                    J, kb, first, last = steps[i]
                    c0 = c0_of(J, kb)
                    zb = (i % 2) * 2
                    rk = [K("qk", par, J), K("qk", par, kb // 4)]
                    for m in range(2):
                        lo, hi = m * 64, (m + 1) * 64
                        kT = k_[lo:hi, kb * 128:(kb + 1) * 128]
                        P.op("pe", lambda t: t.matmul(ps[:, zb + m, c0:], kT, q_[lo:hi, J * 512 + c0:(J + 1) * 512],
                                                      start=True, stop=True),
                             reads=rk, writes=[bank(zb + m)], signal=(m == 1))

                def D2(i):
                    J, kb, first, last = steps[i]
                    c0 = c0_of(J, kb)
                    zb = (i % 2) * 2
                    P.op("act", lambda a: a.activation(out=Pb[i % 3][:, :, c0:], in_=ps[:, zb:zb + 2, c0:],
                                                       func=AF.Exp, scale=0.125),
                         reads=[bank(zb), bank(zb + 1)], writes=[K("W", i % 3)])
                    if kb >= 4 * J:
                        pd = Pb[i % 3][:, :, c0:c0 + 128]
                        P.op("pool", lambda g_: g_.tensor_tensor(out=pd, in0=pd,
                                                                 in1=M01.unsqueeze(1).broadcast_to([128, 2, 128]),
                                                                 op=ALU.mult),
                             reads=[K("W", i % 3), "cmat"], writes=[K("W", i % 3)])

                def D3(i):
                    J, kb, first, last = steps[i]
                    c0 = c0_of(J, kb)
                    p_ = Pb[i % 3]
                    for m in range(2):
                        P.op("pe", lambda t: t.matmul(ps[:, 4 + m, c0:], vtok[par][:, kb, :], p_[:, m, c0:],
                                                      start=first, stop=last),
                             reads=[K("v", par, kb // 4), K("W", i % 3)], writes=[bank(4 + m)], signal=False)
                    for m in range(2):
                        P.op("pe", lambda t: t.matmul(ps[32 * m:32 * m + 32, 6, c0:], ONES[:, 0:32], p_[:, m, c0:],
                                                      start=first, stop=last),
                             reads=["cmat", K("W", i % 3)], writes=[bank(6)], signal=(m == 1))
                    if last:
                        k2 = J % 2
                        while os_busy[k2]:
                            bg_step()
                        P.op("act", lambda a: a.activation(out=Os[k2], in_=ps[:, 4:6, :], func=AF.Copy),
                             reads=[bank(4), bank(5)], writes=[K("Os", k2)])
                        P.op("dve", lambda v: v.tensor_copy(out=RSs[k2][0:64, :], in_=ps[0:64, 6, :]),
                             reads=[bank(6)], writes=[K("RS", k2)])
                        os_busy[k2] = True
                        finals.append((g, J, k2))

                for n in range(-1, N + 1):
                    if 0 <= n + 1 < N:
                        D1(n + 1)
                    if 0 <= n < N:
                        D2(n)
                    if 0 <= n - 1 < N:
                        D3(n - 1)
                    bg_step()

            start_group(0)
            bg_drain()
            closing[0] = False
            bg.append(final_worker())
            for pos, g in enumerate(ORDER):
                if pos + 1 < 8:
                    start_group(pos + 1)
                if g < 4:
                    sb_attention(g, pos % 2)
                else:
                    diff_attention(g, pos % 2)
                drain_inproj()
            closing[0] = True
            bg_drain()
            P.barrier()

            AR.reset()
            x1 = AR.alloc([8, 1024], F32)
            h2T = AR.alloc([8, TT], BF16)
            aT = AR.alloc([NFH, TT], BF16)
            wdb = AR.alloc([NFH, 1024], BF16)
            NRING = 3
            wgu = [AR.alloc([2, 8, 128], BF16) for _ in range(NRING)]
            xs2 = [AR.alloc([1024], F32) for _ in range(2)]
            xn2 = [AR.alloc([1024], BF16) for _ in range(3)]
            sg = [AR.alloc([2, 512], F32) for _ in range(2)]
            sqj2 = AR.alloc([1024], BF16)
            stat2 = AR.alloc([8, 4], F32)

            for tt in range(S // TT):
                tk = K
                trow = row0 + tt * TT
                def issue_gu(gf):
                    P.dma("pool", wgu[gf % NRING].rearrange("p a b c -> p (a b c)"), wgud[gf],
                          writes=[tk("wgu", gf % NRING)], slot="wgu%d" % (gf % NRING))

                def issue_wd(half):
                    for f in range(NFH):
                        lt = P.dma("pool", wdb[:, f, :], wdd[half * NFH + f], writes=[tk("wd", f)], slot="wd")
                    for f in range(NFH):
                        P.last_w[tk("wd", f)] = lt

                for f in range(NRING):
                    issue_gu(f)
                issue_wd(0)

                def c1_post(i):
                    for c in range(8):
                        P.op("pe", lambda t, c=c: t.transpose(ps_bf7[:, c * 128:(c + 1) * 128],
                                                              xn2[i % 3][:, c * 128:(c + 1) * 128], ident),
                             reads=[tk("xn2", i % 3), "cmat"], writes=[bank(7)], signal=(c == 7))
                    P.op("act", lambda a: a.activation(out=h2T[:, :, i * 128:(i + 1) * 128],
                                                       in_=ps_bf7.rearrange("p (c n) -> p c n", c=8), func=AF.Copy),
                         reads=[bank(7)], writes=[tk("h2T", i)])

                for i in range(8):
                    Jg = (tt * TT + i * 128) // 512
                    col = tt * TT + i * 128
                    yb = (i % 2) * 2
                    P.dma("sp", xs2[i % 2], xd[trow + i * 128: trow + (i + 1) * 128, :], writes=[tk("xs2", i % 2)],
                          slot="xs%d" % (i % 2))
                    for hh in range(2):
                        for kc in range(8):
                            P.op("pe", lambda t, hh=hh, kc=kc: t.matmul(
                                ps[:, yb + hh, :], mixT[:, kc, col:col + 128], wo[:, kc, hh * 512:(hh + 1) * 512],
                                start=(kc == 0), stop=(kc == 7)),
                                 reads=[K("mix", kc, Jg), "wo"], writes=[bank(yb + hh)],
                                 signal=(kc == 7 and hh == 1))
                    P.op("dve", lambda v: v.tensor_tensor(out=x1[:, i, :].rearrange("p (a n) -> p a n", a=2),
                                                          in0=ps[:, yb:yb + 2, :],
                                                          in1=xs2[i % 2].rearrange("p (a n) -> p a n", a=2),
                                                          op=ALU.add),
                         reads=[bank(yb), bank(yb + 1), tk("xs2", i % 2)], writes=[tk("x1", i)])
                    P.op("act", lambda a: a.activation(out=sqj2, in_=x1[:, i, :], func=AF.Square,
                                                       accum_out=stat2[:, i, 0:1]),
                         reads=[tk("x1", i)], writes=[tk("sqj2"), tk("st2", i)])
                    rstd_from_ss(stat2[:, i, 0:1], stat2[:, i, 1:2], D, EPS6, tk("st2", i), tk("rs2", i))
                    P.op("dve", lambda v: v.scalar_tensor_tensor(
                        out=xn2[i % 3], in0=x1[:, i, :], scalar=stat2[:, i, 1:2], in1=small[:, SP_FG:SP_FG + 1024],
                        op0=ALU.mult, op1=ALU.mult),
                         reads=[tk("x1", i), tk("rs2", i), "small"], writes=[tk("xn2", i % 3)])
                    if i >= 2:
                        c1_post(i - 2)
                c1_post(6)
                c1_post(7)

                for half in range(2):
                    for f in range(NFH):
                        gf = half * NFH + f
                        slot = gf % NRING
                        gb_ = (gf % 2) * 2
                        ub_ = 4 + (gf % 2) * 2
                        for gu, bb in ((0, gb_), (1, ub_)):
                            for hh in range(2):
                                for kc in range(8):
                                    P.op("pe", lambda t, gu=gu, bb=bb, hh=hh, kc=kc: t.matmul(
                                        ps[:, bb + hh, :], wgu[slot][:, gu, kc, :], h2T[:, kc, hh * 512:(hh + 1) * 512],
                                        start=(kc == 0), stop=(kc == 7)),
                                         reads=[tk("wgu", slot)] + [tk("h2T", 4 * hh + q) for q in range(4)],
                                         writes=[bank(bb + hh)], signal=(kc == 7 and hh == 1))
                        P.op("act", lambda a: a.activation(out=sg[gf % 2], in_=ps[:, gb_:gb_ + 2, :], func=AF.Silu),
                             reads=[bank(gb_), bank(gb_ + 1)], writes=[tk("sg", gf % 2)])
                        P.op("dve", lambda v: v.tensor_tensor(out=aT[:, f, :].rearrange("p (a n) -> p a n", a=2),
                                                              in0=ps[:, ub_:ub_ + 2, :], in1=sg[gf % 2], op=ALU.mult),
                             reads=[bank(ub_), bank(ub_ + 1), tk("sg", gf % 2)], writes=[tk("aT", f)])
                        if f + NRING < NFH:
                            issue_gu(gf + NRING)
                    for i in range(8):
                        ob = (i % 2) * 2
                        for hh in range(2):
                            for f in range(NFH):
                                P.op("pe", lambda t, hh=hh, f=f: t.matmul(
                                    ps[:, ob + hh, :], aT[:, f, i * 128:(i + 1) * 128],
                                    wdb[:, f, hh * 512:(hh + 1) * 512], start=(f == 0), stop=(f == NFH - 1)),
                                     reads=[tk("aT", f), tk("wd", f)], writes=[bank(ob + hh)],
                                     signal=(f == NFH - 1 and hh == 1))
                        P.op("dve", lambda v: v.tensor_tensor(out=x1[:, i, :].rearrange("p (a n) -> p a n", a=2),
                                                              in0=ps[:, ob:ob + 2, :],
                                                              in1=x1[:, i, :].rearrange("p (a n) -> p a n", a=2),
                                                              op=ALU.add),
                             reads=[bank(ob), bank(ob + 1), tk("x1", i)], writes=[tk("x1", i)])
                        if half == 1:
                            P.dma("sp", yd[trow + i * 128: trow + (i + 1) * 128, :], x1[:, i, :],
                                  reads=[tk("x1", i)], slot="out%d" % i, is_out=True)
                    if half == 0:
                        for f in range(NRING):
                            issue_gu(NFH + f)
                        issue_wd(1)
            P.barrier()
        P.barrier()
        P.finish()
    return nc


def _host_constants():
    j = np.arange(128)
    ident = np.eye(128, dtype=np.float32)
    L1 = (j[:, None] >= j[None, :]).astype(np.float32)
    L2 = (j[:, None] < j[None, :]).astype(np.float32)
    negs = np.where(j[:, None] >= j[None, :], NEG, 0.0).astype(np.float32)
    negi = np.where(j[:, None] > j[None, :], NEG, 0.0).astype(np.float32)
    ones = np.ones((128, 128), np.float32)
    m01 = (j[:, None] <= j[None, :]).astype(np.float32)
    cmat = np.concatenate([ident, L1, L2, negs, negi, ones, ones / 32.0, ones / 128.0, m01], axis=1)
    half = HD // 2
    inv_freq = (np.float32(10000.0) ** (-np.arange(half, dtype=np.float32) / np.float32(half))).astype(np.float32)
    pos = np.arange(S, dtype=np.float32)
    ang = (pos[:, None] * inv_freq[None, :]).astype(np.float32)
    cos, sin = np.cos(ang.astype(np.float64)).astype(np.float32), np.sin(ang.astype(np.float64)).astype(np.float32)
    CC = np.concatenate([cos, cos], axis=1).reshape(NT, 128, 64).transpose(1, 0, 2)
    SS = np.concatenate([-sin, sin], axis=1).reshape(NT, 128, 64).transpose(1, 0, 2)
    rope = np.stack([CC, SS], axis=1).reshape(128, 2 * NT * 64).astype(np.float32)
    return np.ascontiguousarray(cmat), np.ascontiguousarray(rope)


def _prep_weights(inp):
    w_in = np.asarray(inp["w_in"], np.float32)[0]
    groups = []
    for g in range(8):
        if g < 4:
            cols = np.r_[128 * g:128 * g + 128, 512 + 128 * g:512 + 128 * g + 128,
                         1024 + 128 * g:1024 + 128 * g + 128]
        else:
            h = g - 4
            cols = np.r_[1536 + 128 * h:1536 + 128 * h + 128, 2048 + 128 * h:2048 + 128 * h + 128,
                         2560 + 128 * h:2560 + 128 * h + 128]
        wg = w_in[:, cols].reshape(8, 128, 384).transpose(1, 0, 2).reshape(128, 8 * 384)
        groups.append(wg)
    wing = np.ascontiguousarray(np.stack(groups, 0))
    wo = np.asarray(inp["w_o"], np.float32)[0].reshape(8, 128, 1024).transpose(1, 0, 2).reshape(128, 8 * 1024)
    wg_ = np.asarray(inp["w_gate"], np.float32)[0].reshape(8, 128, NF, 128)
    wu_ = np.asarray(inp["w_up"], np.float32)[0].reshape(8, 128, NF, 128)
    wgu = np.stack([wg_, wu_], 0).transpose(3, 2, 0, 1, 4).reshape(NF, 128, 2 * 8 * 128)
    wd = np.asarray(inp["w_down"], np.float32)[0].reshape(NF, 128, 1024)
    gq = np.asarray(inp["diff_q_norm_g"], np.float32)[0]
    gk = np.asarray(inp["diff_k_norm_g"], np.float32)[0]
    sw = lambda v: np.concatenate([v[32:], v[:32]])
    small = np.concatenate([
        np.asarray(inp["attn_norm_g"], np.float32)[0], np.asarray(inp["ffn_norm_g"], np.float32)[0],
        gq, sw(gq), gk, sw(gk),
        np.asarray(inp["lambda_q1"], np.float32)[0], np.asarray(inp["lambda_k1"], np.float32)[0],
        np.asarray(inp["lambda_q2"], np.float32)[0], np.asarray(inp["lambda_k2"], np.float32)[0]])
    small = np.ascontiguousarray(np.broadcast_to(small[None, :], (128, SP_N)))
    subln = np.ascontiguousarray(np.asarray(inp["diff_subln_g"], np.float32)[0].reshape(128, 1))
    return dict(wing=wing, wo=np.ascontiguousarray(wo), wgu=np.ascontiguousarray(wgu), wd=np.ascontiguousarray(wd),
                small=small, subln=subln)


def kernel(**inputs):
    x = np.asarray(inputs["x"], np.float32)
    wmaps = _prep_weights(inputs)
    cmat, rope = _host_constants()
    nc = build_program(NSEQ)
    in_maps = []
    for c in range(NCORES):
        m = dict(wmaps)
        m["x"] = np.ascontiguousarray(x[c * NSEQ:(c + 1) * NSEQ].reshape(NSEQ * S, D))
        m["cmat"] = cmat
        m["rope"] = rope
        in_maps.append(m)
    res = run_bass_kernel_spmd(nc, in_maps, core_ids=list(range(NCORES)))
    out = np.concatenate([np.asarray(r["y"], np.float32).reshape(NSEQ, S, D) for r in res.results], axis=0)
    return out
```

```python
import math
from contextlib import ExitStack

import numpy as np
import concourse.bass as bass
import concourse.mybir as mybir
from concourse.bass_utils import run_bass_kernel_spmd

F32 = mybir.dt.float32
BF16 = mybir.dt.bfloat16
AF = mybir.ActivationFunctionType
ALU = mybir.AluOpType
AX = mybir.AxisListType

NCORES = 8
D = 1024
S = 2048
BATCH = 32
NSEQ = BATCH // NCORES
DFF = 2816
NF = DFF // 128
NFH = NF // 2
HD = 64
NT = S // 128
NJ = S // 512
TT = 1024
NEG = -30000.0
LAMBDA_INIT = 0.8 - 0.6 * math.exp(-0.3 * 0)

SP_AG, SP_FG, SP_GQ, SP_GQS, SP_GK, SP_GKS, SP_L = 0, 1024, 2048, 2112, 2176, 2240, 2304
SP_N = 2304 + 256


class Tok:
    __slots__ = ("sem", "val", "eng")

    def __init__(self, sem, val, eng):
        self.sem, self.val, self.eng = sem, val, eng


class Prog:
    def __init__(self, nc, es):
        self.nc = nc
        self.E = {"pe": nc.tensor, "act": nc.scalar, "dve": nc.vector, "pool": nc.gpsimd, "sp": nc.sync}
        self.sem = {e: es.enter_context(nc.semaphore("s_" + e)) for e in self.E}
        self.cnt = {e: 0 for e in self.E}
        self.waited = {e: {} for e in self.E}
        self.last_w = {}
        self.readers = {}
        self.pending = {e: [] for e in self.E}
        self.dma_sem = {}
        self.dma_cnt = {}
        self.es = es
        self.out_toks = []
        self.all_dma = []

    def _wait(self, e, toks):
        best = {}
        for t in toks:
            if t is None:
                continue
            assert t.val is not None, "dependency on unsignaled op"
            if t.val > best.get(t.sem, 0):
                best[t.sem] = t.val
        for sname, v in best.items():
            if v > self.waited[e].get(sname, 0):
                self.E[e].wait_ge(self._semh(sname), v)
                self.waited[e][sname] = v

    def _semh(self, sname):
        return self.sem[sname] if sname in self.sem else self.dma_sem[sname]

    def _deps(self, e, reads, writes, is_dma=False, slot=None):
        deps = []
        for k in reads:
            t = self.last_w.get(k)
            if t is not None:
                if not (t.eng == e and e == "pe"):
                    deps.append(t)
            if len(k) == 2 and k[0] == "b":
                for r in self.readers.get(k, ()):
                    if r.eng != e:
                        deps.append(r)
        for k in writes:
            t = self.last_w.get(k)
            same_ok = (not is_dma) and e != "pool"
            if t is not None and not (t.eng == e and same_ok) and not (is_dma and t.sem == slot):
                deps.append(t)
            for r in self.readers.get(k, ()):
                if r.eng == e and same_ok:
                    continue
                deps.append(r)
        return deps

    def op(self, e, fn, reads=(), writes=(), signal=True):
        self._wait(e, self._deps(e, reads, writes))
        inst = fn(self.E[e])
        tok = Tok(e, None, e)
        if signal:
            inst.then_inc(self.sem[e], 1)
            self.cnt[e] += 1
            tok.val = self.cnt[e]
            for p in self.pending[e]:
                p.val = self.cnt[e]
            self.pending[e] = []
        else:
            self.pending[e].append(tok)
        for k in reads:
            self.readers.setdefault(k, []).append(tok)
        for k in writes:
            self.last_w[k] = tok
            self.readers[k] = []
        return tok

    def dma(self, q, out, in_, reads=(), writes=(), slot="d", is_out=False):
        sname = "dma_" + slot
        if sname not in self.dma_sem:
            self.dma_sem[sname] = self.es.enter_context(self.nc.semaphore(sname))
            self.dma_cnt[sname] = 0
        self._wait(q, self._deps(q, reads, writes, is_dma=True, slot=sname))
        self.E[q].dma_start(out=out, in_=in_).then_inc(self.dma_sem[sname], 16)
        self.dma_cnt[sname] += 16
        tok = Tok(sname, self.dma_cnt[sname], None)
        for k in reads:
            self.readers.setdefault(k, []).append(tok)
        for k in writes:
            self.last_w[k] = tok
            self.readers[k] = []
        if is_out:
            self.out_toks.append(tok)
        self.all_dma.append(tok)
        return tok

    def barrier(self):
        for e in self.E:
            assert not self.pending[e], "unsignaled tail on " + e
        toks = [Tok(e, self.cnt[e], e) for e in self.E if self.cnt[e] > 0]
        toks += [Tok(s, c, None) for s, c in self.dma_cnt.items()]
        for e in self.E:
            self._wait(e, [t for t in toks if t.eng != e])

    def finish(self):
        self._wait("sp", self.out_toks)


class Arena:
    log = []

    def __init__(self, t, nelem):
        self.t, self.n, self.off = t, nelem, 0

    def reset(self, off=0):
        self.off = off

    def alloc(self, free_shape, dt):
        n = int(np.prod(free_shape))
        sz = n * (2 if dt == F32 else 1)
        self.off = (self.off + 15) // 16 * 16
        a = self.t[:, self.off:self.off + sz]
        Arena.log.append((self.off, sz, str(dt), tuple(free_shape)))
        self.off += sz
        assert self.off <= self.n, ("arena overflow", self.off, self.n)
        if dt == F32:
            a = a.bitcast(F32)
        if len(free_shape) == 2:
            a = a.rearrange("p (a b) -> p a b", a=free_shape[0])
        elif len(free_shape) == 3:
            a = a.rearrange("p (a b c) -> p a b c", a=free_shape[0], b=free_shape[1])
        elif len(free_shape) == 4:
            a = a.rearrange("p (a b c d) -> p a b c d", a=free_shape[0], b=free_shape[1], c=free_shape[2])
        return a


def build_program(nseq=NSEQ):
    nc = bass.Bass("TRN2", target_bir_lowering=False)
    ntok = nseq * S
    xd = nc.dram_tensor("x", [ntok, D], F32, kind="ExternalInput").ap()
    wing = nc.dram_tensor("wing", [8, 128, 8 * 384], F32, kind="ExternalInput").ap()
    wod = nc.dram_tensor("wo", [128, 8 * 1024], F32, kind="ExternalInput").ap()
    wgud = nc.dram_tensor("wgu", [NF, 128, 2 * 8 * 128], F32, kind="ExternalInput").ap()
    wdd = nc.dram_tensor("wd", [NF, 128, 1024], F32, kind="ExternalInput").ap()
    smalld = nc.dram_tensor("small", [128, SP_N], F32, kind="ExternalInput").ap()
    sublnd = nc.dram_tensor("subln", [128, 1], F32, kind="ExternalInput").ap()
    cmatd = nc.dram_tensor("cmat", [128, 9 * 128], F32, kind="ExternalInput").ap()
    roped = nc.dram_tensor("rope", [128, 2 * NT * 64], F32, kind="ExternalInput").ap()
    yd = nc.dram_tensor("y", [ntok, D], F32, kind="ExternalOutput").ap()
    scrd = nc.dram_tensor("bc_scratch", [2, 2, 512], F32, kind="Internal").ap()

    with ExitStack() as es:
        P = Prog(nc, es)
        sb = lambda name, shape, dt: es.enter_context(nc.sbuf_tensor(name, shape, dt))
        ps = es.enter_context(nc.psum_tensor("ps", [128, 8, 512], F32))
        wo = sb("wo_sb", [128, 8, 1024], BF16)
        mixT = sb("mixT", [128, 8, S], BF16)
        small = sb("small_sb", [128, 2048], F32)
        cmat = sb("cmat_sb", [128, 9, 128], BF16)
        rtab = sb("rtab", [128, NT, 2, 2, 64], F32)
        misc = sb("misc", [128, 16], F32)
        zero_bf = sb("zero_bf", [128, 512], BF16)
        subg = sb("subg", [128, 1], F32)
        ltmp = sb("ltmp", [128, 2, 64], F32)
        ARENA = (int(nc.sbuf_bytes_remaining) - 2048) // 64 * 32
        arena_t = sb("arena", [128, ARENA], BF16)
        AR = Arena(arena_t, ARENA)
        ropeT = AR.alloc([2, NT, 64], F32)
        small2 = AR.alloc([SP_N - 2048], F32)

        ident = cmat[:, 0, :]
        L1 = cmat[:, 1, :]
        L2 = cmat[:, 2, :]
        NEGS = cmat[:, 3, :]
        NEGI = cmat[:, 4, :]
        ONES = cmat[:, 5, :]
        C32 = cmat[:, 6, :]
        C128 = cmat[:, 7, :]
        M01 = cmat[:, 8, :]
        EPS6, EPS5, ONE, LAM, NLAM = (misc[:, i:i + 1] for i in range(5))
        ps_bf7 = ps[:, 7, :].bitcast(BF16)

        def bank(b):
            return "b%d" % b

        P.dma("pool", cmat[:].rearrange("p a b -> p (a b)"), cmatd, writes=["cmat"], slot="c0")
        P.dma("sp", small[:], smalld[:, 0:2048], writes=["small"], slot="c1")
        P.dma("sp", small2, smalld[:, 2048:SP_N], writes=["small2"], slot="c5")
        P.dma("sp", ropeT.rearrange("p a t d -> p (a t d)"), roped, writes=["rope"], slot="c2")
        P.dma("sp", subg[:], sublnd, writes=["subg"], slot="c3")
        P.dma("pool", wo[:].rearrange("p a b -> p (a b)"), wod, writes=["wo"], slot="c4")
        P.op("dve", lambda v: v.memset(misc[:, 0:1], 1e-6), writes=["misc"])
        P.op("dve", lambda v: v.memset(misc[:, 1:2], 1e-5), writes=["misc"])
        P.op("dve", lambda v: v.memset(misc[:, 2:3], 1.0), writes=["misc"])
        P.op("dve", lambda v: v.memset(zero_bf[:], 0.0), writes=["zero"])
        P.op("dve", lambda v: v.tensor_scalar(out=subg[:], in0=subg[:], scalar1=1.0 - LAMBDA_INIT, scalar2=None,
                                              op0=ALU.mult), reads=["subg"], writes=["subg"])
        lv = lambda i: small2[:, SP_L - 2048 + 64 * i: SP_L - 2048 + 64 * (i + 1)]
        P.op("dve", lambda v: v.tensor_tensor(out=ltmp[:, 0, :], in0=lv(0), in1=lv(1), op=ALU.mult),
             reads=["small2"], writes=["ltmp"])
        P.op("dve", lambda v: v.tensor_tensor(out=ltmp[:, 1, :], in0=lv(2), in1=lv(3), op=ALU.mult),
             reads=["small2"], writes=["ltmp"])
        P.op("dve", lambda v: v.tensor_reduce(out=misc[:, 8:10], in_=ltmp[:], axis=AX.X, op=ALU.add),
             reads=["ltmp"], writes=["misc"])
        P.op("act", lambda a: a.activation(out=misc[:, 10:12], in_=misc[:, 8:10], func=AF.Exp),
             reads=["misc"], writes=["misc2"])
        P.op("dve", lambda v: v.tensor_tensor(out=misc[:, 12:13], in0=misc[:, 10:11], in1=misc[:, 11:12],
                                              op=ALU.subtract), reads=["misc2"], writes=["misc3"])
        P.op("dve", lambda v: v.tensor_scalar(out=misc[:, 3:4], in0=misc[:, 12:13], scalar1=LAMBDA_INIT, scalar2=None,
                                              op0=ALU.add), reads=["misc3"], writes=["misc4"])
        P.op("dve", lambda v: v.tensor_scalar(out=misc[:, 4:5], in0=misc[:, 3:4], scalar1=-1.0, scalar2=None,
                                              op0=ALU.mult), reads=["misc4"], writes=["misc5"])
        for qk, (go, gso) in enumerate(((SP_GQ, SP_GQS), (SP_GK, SP_GKS))):
            gb = small2[:, go - 2048:go - 2048 + 64].unsqueeze(1).broadcast_to([128, NT, 64])
            gsb = small2[:, gso - 2048:gso - 2048 + 64].unsqueeze(1).broadcast_to([128, NT, 64])
            P.op("pool", lambda g, gb=gb, qk=qk: g.tensor_tensor(out=rtab[:, :, qk, 0, :], in0=ropeT[:, 0, :, :],
                                                                 in1=gb, op=ALU.mult),
                 reads=["small2", "rope"], writes=["rtab"])
            P.op("pool", lambda g, gsb=gsb, qk=qk: g.tensor_tensor(out=rtab[:, :, qk, 1, :], in0=ropeT[:, 1, :, :],
                                                                   in1=gsb, op=ALU.mult),
                 reads=["small2", "rope"], writes=["rtab"])

        P.barrier()

        def rstd_from_ss(ss_ap, out_ap, n, eps_ap, rk, wk):
            P.op("act", lambda a: a.activation(out=out_ap, in_=ss_ap, func=AF.Ln, bias=eps_ap, scale=1.0 / n),
                 reads=[rk, "misc"], writes=[wk])
            P.op("act", lambda a: a.activation(out=out_ap, in_=out_ap, func=AF.Exp, scale=-0.5),
                 reads=[wk], writes=[wk])

        for b in range(nseq):
            row0 = b * S
            AR.reset()
            hT = AR.alloc([8, S], BF16)
            stat = AR.alloc([NT, 4], F32)
            wgb = [AR.alloc([8, 384], BF16) for _ in range(2)]
            qkT = [AR.alloc([2, S], BF16) for _ in range(2)]
            vtok = [AR.alloc([NT, 128], BF16) for _ in range(2)]
            Eb = [AR.alloc([2, 512], F32) for _ in range(3)]
            xs = [e_.rearrange("p a b -> p (a b)") for e_ in Eb]
            SQb = [AR.alloc([2, 512], BF16) for _ in range(2)]
            xn = [q_.rearrange("p a b -> p (a b)") for q_ in SQb]
            Fb = [AR.alloc([2, 512], F32) for _ in range(2)]
            Wb = [AR.alloc([2, 512], BF16) for _ in range(3)]
            sqj = Wb[0].rearrange("p a b -> p (a b)")
            qk32 = [AR.alloc([256], F32) for _ in range(2)]
            sqs = [AR.alloc([256], F32) for _ in range(2)]
            rt1 = [AR.alloc([256], F32) for _ in range(4)]
            rt2 = AR.alloc([256], F32)
            qkb = [AR.alloc([256], BF16) for _ in range(4)]
            dstat = [AR.alloc([8], F32) for _ in range(4)]
            Os = [AR.alloc([2, 512], F32) for _ in range(2)]
            RSs = [AR.alloc([512], F32) for _ in range(2)]
            fin = [AR.alloc([512], F32) for _ in range(3)]
            hib = AR.alloc([512], BF16)
            lob = AR.alloc([512], BF16)
            pfx = "s%d_" % b
            K = lambda *a: pfx + "_".join(str(x) for x in a)

            def phaseA_evac(i):
                for c in range(8):
                    P.op("pe", lambda t, c=c, i=i: t.transpose(ps_bf7[:, c * 128:(c + 1) * 128],
                                                               xn[i % 2][:, c * 128:(c + 1) * 128], ident),
                         reads=[K("SQ", i % 2), "cmat"], writes=[bank(7)], signal=(c == 7))
                if i % 2 == 0:
                    P.op("act", lambda a, i=i: a.activation(
                        out=hT[:, :, i * 128:(i + 1) * 128],
                        in_=ps_bf7.rearrange("p (c n) -> p c n", c=8), func=AF.Copy),
                         reads=[bank(7)], writes=[K("hT", i)])
                else:
                    P.op("dve", lambda v, i=i: v.tensor_copy(
                        out=hT[:, :, i * 128:(i + 1) * 128],
                        in_=ps_bf7.rearrange("p (c n) -> p c n", c=8)),
                         reads=[bank(7)], writes=[K("hT", i)])

            for i in range(NT):
                P.dma("sp", xs[i % 2], xd[row0 + i * 128: row0 + (i + 1) * 128, :], writes=[K("E", i % 2)],
                      slot="xs%d" % (i % 2))
                P.op("act", lambda a, i=i: a.activation(out=sqj, in_=xs[i % 2], func=AF.Square,
                                                        accum_out=stat[:, i, 0:1]),
                     reads=[K("E", i % 2)], writes=[K("W", 0), K("stat", i)])
                rstd_from_ss(stat[:, i, 0:1], stat[:, i, 1:2], D, EPS6, K("stat", i), K("rstd", i))
                P.op("dve", lambda v, i=i: v.scalar_tensor_tensor(
                    out=xn[i % 2], in0=xs[i % 2], scalar=stat[:, i, 1:2], in1=small[:, SP_AG:SP_AG + 1024],
                    op0=ALU.mult, op1=ALU.mult),
                     reads=[K("E", i % 2), K("rstd", i), "small"], writes=[K("SQ", i % 2)])
                if i >= 1:
                    phaseA_evac(i - 1)
            phaseA_evac(NT - 1)

            def load_wg(g, par):
                P.dma("pool", wgb[par].rearrange("p a b -> p (a b)"), wing[g], writes=[K("wg", par)],
                      slot="wg%d" % par)

            b7 = {"req": False, "clean": True}

            def run_stages(n_items, stages):
                done = {st[0]: 0 for st in stages}
                b7w = [(st[0], st[3]) for st in stages if st[3]]
                while any(done[st[0]] < n_items for st in stages):
                    snap = dict(done)
                    for name, fn, prods, rd7, limits in stages:
                        i = done[name]
                        if i >= n_items or any(snap[p] <= i for p in prods):
                            continue
                        if any(i - done[c] >= d for c, d in limits):
                            continue
                        if rd7 and (b7["req"] or done[rd7] < i):
                            continue
                        fn(i)
                        done[name] += 1
                    b7["clean"] = all(done[r] == done[w] for w, r in b7w)
                    yield
                b7["clean"] = True

            def inproj_sb(g, par, atomic):
                wgk = K("wg", par)
                units = []
                for j in range(NJ):
                    units += [("qk", j, 0), ("qk", j, 1), ("v", j, 0)]

                fg = (g == ORDER[0])
                bk = (lambda u: 6 + (u % 2)) if fg else (lambda u: 7)

                def mm(u):
                    kind, j, which = units[u]
                    hk = [K("hT", 4 * j + t) for t in range(4)]
                    if kind == "qk":
                        for kc in range(8):
                            P.op("pe", lambda t: t.matmul(
                                ps[:, bk(u), :], wgb[par][:, kc, which * 128:(which + 1) * 128],
                                hT[:, kc, j * 512:(j + 1) * 512], start=(kc == 0), stop=(kc == 7)),
                                 reads=[wgk] + hk, writes=[bank(bk(u))], signal=(kc == 7))
                    else:
                        for t4 in range(4):
                            for kc in range(8):
                                P.op("pe", lambda t: t.matmul(
                                    ps[:, bk(u), t4 * 128:(t4 + 1) * 128],
                                    hT[:, kc, (4 * j + t4) * 128:(4 * j + t4 + 1) * 128],
                                    wgb[par][:, kc, 256:384], start=(kc == 0), stop=(kc == 7)),
                                     reads=[wgk] + hk, writes=[bank(bk(u))], signal=(kc == 7 and t4 == 3))

                def ev(u):
                    kind, j, which = units[u]
                    if kind == "qk":
                        P.op("dve", lambda v: v.tensor_copy(
                            out=qkT[par][:, which, j * 512:(j + 1) * 512], in_=ps[:, bk(u), :]),
                             reads=[bank(bk(u))], writes=[K("qk", par, j)])
                    else:
                        P.op("dve", lambda v: v.tensor_copy(
                            out=vtok[par][:, 4 * j:4 * j + 4, :],
                            in_=ps[:, bk(u), :].rearrange("p (a n) -> p a n", a=4)),
                             reads=[bank(bk(u))], writes=[K("v", par, j)])

                if fg:
                    for u in range(len(units)):
                        mm(u)
                        if u >= 1:
                            ev(u - 1)
                        yield
                    ev(len(units) - 1)
                    yield
                else:
                    yield from run_stages(len(units), [("ev", ev, ["mm"], None, []), ("mm", mm, [], "ev", [])])

            def inproj_diff(g, par, atomic):
                wgk = K("wg", par)

                def st_mm(i):
                    for kc in range(8):
                        P.op("pe", lambda t: t.matmul(
                            ps[:, 7, 0:384], hT[:, kc, i * 128:(i + 1) * 128], wgb[par][:, kc, :],
                            start=(kc == 0), stop=(kc == 7)),
                             reads=[wgk, K("hT", i)], writes=[bank(7)], signal=(kc == 7))

                def st_ev(i):
                    k2 = i % 2
                    P.op("dve", lambda v: v.tensor_copy(out=qk32[k2], in_=ps[:, 7, 0:256]),
                         reads=[bank(7)], writes=[K("qk32", k2)])
                    P.op("dve", lambda v: v.tensor_copy(out=vtok[par][:, i, :], in_=ps[:, 7, 256:384]),
                         reads=[bank(7)], writes=[K("v", par, i // 4)])
                    P.op("dve", lambda v: v.tensor_tensor(out=sqs[k2], in0=qk32[k2], in1=qk32[k2], op=ALU.mult),
                         reads=[K("qk32", k2)], writes=[K("sqs", k2)])
                    k4 = i % 4
                    x4 = qk32[k2].rearrange("p (q m d) -> p q m d", q=2, m=2)
                    o1 = rt1[k4].rearrange("p (q m d) -> p q m d", q=2, m=2)
                    o2 = rt2.rearrange("p (q m d) -> p q m d", q=2, m=2)
                    tabA = rtab[:, i, :, 0, :].unsqueeze(2).broadcast_to([128, 2, 2, 64])
                    tabB_lo = rtab[:, i, :, 1, 0:32].unsqueeze(2).broadcast_to([128, 2, 2, 32])
                    tabB_hi = rtab[:, i, :, 1, 32:64].unsqueeze(2).broadcast_to([128, 2, 2, 32])
                    P.op("dve", lambda v: v.tensor_tensor(out=o1, in0=x4, in1=tabA, op=ALU.mult),
                         reads=[K("qk32", k2), "rtab"], writes=[K("rt1", k4)])
                    P.op("pool", lambda g_: g_.tensor_tensor(out=o2[:, :, :, 0:32], in0=x4[:, :, :, 32:64],
                                                             in1=tabB_lo, op=ALU.mult),
                         reads=[K("qk32", k2), "rtab"], writes=[K("rt2")])
                    P.op("pool", lambda g_: g_.tensor_tensor(out=o2[:, :, :, 32:64], in0=x4[:, :, :, 0:32],
                                                             in1=tabB_hi, op=ALU.mult),
                         reads=[K("qk32", k2), "rtab"], writes=[K("rt2")])

                def st_add(i):
                    k4 = i % 4
                    P.op("pool", lambda g_: g_.tensor_tensor(out=rt1[k4], in0=rt1[k4], in1=rt2, op=ALU.add),
                         reads=[K("rt1", k4), K("rt2")], writes=[K("rt1", k4)])

                def st_red(i):
                    k2, k4 = i % 2, i % 4
                    P.op("dve", lambda v: v.tensor_reduce(out=dstat[k4][:, 0:4],
                                                          in_=sqs[k2].rearrange("p (g d) -> p g d", g=4),
                                                          axis=AX.X, op=ALU.add),
                         reads=[K("sqs", k2)], writes=[K("dss", k4)])

                def st_rstd(i):
                    k4 = i % 4
                    rstd_from_ss(dstat[k4][:, 0:4], dstat[k4][:, 4:8], HD, EPS6, K("dss", k4), K("drs", k4))

                def st_fin(i):
                    k4 = i % 4
                    rb = dstat[k4][:, 4:8].unsqueeze(2).broadcast_to([128, 4, 64])
                    P.op("pool", lambda v: v.tensor_tensor(out=qkb[k4].rearrange("p (g d) -> p g d", g=4),
                                                           in0=rt1[k4].rearrange("p (g d) -> p g d", g=4),
                                                           in1=rb, op=ALU.mult),
                         reads=[K("rt1", k4), K("drs", k4)], writes=[K("qkb", k4)])

                def st_tr(i):
                    k4 = i % 4
                    for which in range(2):
                        P.op("pe", lambda t: t.transpose(
                            ps_bf7[:, 768 + which * 128: 768 + (which + 1) * 128],
                            qkb[k4][:, which * 128:(which + 1) * 128], ident),
                             reads=[K("qkb", k4), "cmat"], writes=[bank(7)], signal=(which == 1))

                def st_tev(i):
                    P.op("dve", lambda v: v.tensor_copy(
                        out=qkT[par][:, :, i * 128:(i + 1) * 128],
                        in_=ps_bf7[:, 768:1024].rearrange("p (a n) -> p a n", a=2)),
                         reads=[bank(7)], writes=[K("qk", par, i // 4)])

                yield from run_stages(NT, [
                    ("add", st_add, ["ev"], None, []),
                    ("ev", st_ev, ["mm"], None, [("fin", 4), ("red", 2), ("add", 1)]),
                    ("tev", st_tev, ["tr"], None, []),
                    ("tr", st_tr, ["w"], "tev", []),
                    ("mm", st_mm, [], "ev", []),
                    ("w", lambda i: None, ["fin"], None, []),
                    ("fin", st_fin, ["rstd", "add"], None, [("tr", 4)]),
                    ("rstd", st_rstd, ["red"], None, []),
                    ("red", st_red, ["ev"], None, [("fin", 4)]),
                ])

            bg = []

            def bg_step(n=1):
                for _ in range(n):
                    for gen in list(bg):
                        try:
                            next(gen)
                        except StopIteration:
                            bg.remove(gen)

            def bg_drain():
                while bg:
                    bg_step()

            inproj_gen = [None]
            finals = []
            os_busy = {0: False, 1: False}
            closing = [False]

            def final_worker():
                while True:
                    if finals:
                        yield from diff_final(*finals.pop(0))
                    elif closing[0]:
                        return
                    else:
                        yield

            ORDER = [0, 4, 1, 5, 2, 6, 7, 3]

            def throttle(gen, every):
                k = 0
                for _ in gen:
                    yield
                    k += 1
                    if k % every == 0:
                        yield

            def start_group(pos):
                g, par = ORDER[pos], pos % 2
                load_wg(g, par)
                during_diff = pos >= 1 and ORDER[pos - 1] >= 4
                inproj_gen[0] = inproj_sb(g, par, during_diff) if g < 4 else throttle(inproj_diff(g, par, True), 2)
                bg.append(inproj_gen[0])

            def drain_inproj():
                while inproj_gen[0] in bg:
                    bg_step()

            def sb_attention(g, par):
                q_ = qkT[par][:, 0, :]
                k_ = qkT[par][:, 1, :]
                steps = []
                for J in range(NJ):
                    nkb = 4 * J + 4
                    for kb in range(nkb - 1, -1, -1):
                        steps.append((J, kb, kb == nkb - 1, kb == 0))
                N = len(steps)

                def c0_of(J, kb):
                    return max(0, kb * 128 - 512 * J)

                def S1(i):
                    J, kb, first, last = steps[i]
                    c0 = c0_of(J, kb)
                    zb = (i % 2) * 2
                    diag = kb >= 4 * J
                    rk = [K("qk", par, J), K("qk", par, kb // 4)]
                    for h in range(2):
                        lo, hi = h * 64, (h + 1) * 64
                        kT = k_[lo:hi, kb * 128:(kb + 1) * 128]
                        if diag:
                            P.op("pe", lambda t: t.matmul(ps[:, zb + h, c0:c0 + 128], kT,
                                                          q_[lo:hi, J * 512 + c0: J * 512 + c0 + 128],
                                                          start=True, stop=False),
                                 reads=rk, writes=[bank(zb + h)], signal=False)
                            P.op("pe", lambda t: t.matmul(ps[:, zb + h, c0:c0 + 128], ident, NEGS,
                                                          start=False, stop=True),
                                 reads=["cmat"], writes=[bank(zb + h)], signal=(h == 1 and c0 + 128 >= 512))
                            if c0 + 128 < 512:
                                P.op("pe", lambda t: t.matmul(ps[:, zb + h, c0 + 128:512], kT,
                                                              q_[lo:hi, J * 512 + c0 + 128: (J + 1) * 512],
                                                              start=True, stop=True),
                                     reads=rk, writes=[bank(zb + h)], signal=(h == 1))
                        else:
                            P.op("pe", lambda t: t.matmul(ps[:, zb + h, :], kT, q_[lo:hi, J * 512:(J + 1) * 512],
                                                          start=True, stop=True),
                                 reads=rk, writes=[bank(zb + h)], signal=(h == 1))

                def S2a(i):
                    J, kb, first, last = steps[i]
                    c0 = c0_of(J, kb)
                    zb = (i % 2) * 2
                    P.op("act", lambda a: a.activation(out=Eb[i % 3][:, :, c0:], in_=ps[:, zb:zb + 2, c0:],
                                                       func=AF.Exp, scale=0.125),
                         reads=[bank(zb), bank(zb + 1)], writes=[K("E", i % 3)])

                def S2b(i):
                    J, kb, first, last = steps[i]
                    c0 = c0_of(J, kb)
                    P.op("act", lambda a: a.activation(out=SQb[i % 2][:, :, c0:], in_=Eb[i % 3][:, :, c0:],
                                                       func=AF.Ln, bias=ONE, scale=1.0),
                         reads=[K("E", i % 3), "misc"], writes=[K("SQ", i % 2)])

                def S3(i):
                    J, kb, first, last = steps[i]
                    c0 = c0_of(J, kb)
                    if first:
                        for h in range(2):
                            P.op("pe", lambda t: t.matmul(ps[:, 4 + h, :], L1, zero_bf[:], start=True, stop=True),
                                 reads=["cmat", "zero"], writes=[bank(4 + h)], signal=False)
                    for h in range(2):
                        P.op("pe", lambda t: t.matmul(ps[:, 4 + h, c0:], L1, SQb[i % 2][:, h, c0:],
                                                      start=False, stop=True, skip_group_check=True),
                             reads=["cmat", K("SQ", i % 2)], writes=[bank(4 + h)], signal=(h == 1))

                def S4(i):
                    J, kb, first, last = steps[i]
                    c0 = c0_of(J, kb)
                    P.op("act", lambda a: a.activation(out=Fb[i % 2][:, :, c0:], in_=ps[:, 4:6, c0:], func=AF.Exp,
                                                       scale=-1.0),
                         reads=[bank(4), bank(5)], writes=[K("F", i % 2)])

                def S5(i):
                    J, kb, first, last = steps[i]
                    c0 = c0_of(J, kb)
                    P.op("dve", lambda v: v.tensor_tensor(out=Wb[i % 3][:, :, c0:], in0=Eb[i % 3][:, :, c0:],
                                                          in1=Fb[i % 2][:, :, c0:], op=ALU.mult),
                         reads=[K("E", i % 3), K("F", i % 2)], writes=[K("W", i % 3)])

                def S6(i):
                    J, kb, first, last = steps[i]
                    c0 = c0_of(J, kb)
                    if last:
                        return
                    for h in range(2):
                        P.op("pe", lambda t: t.matmul(ps[:, 4 + h, c0:], L2, SQb[i % 2][:, h, c0:],
                                                      start=False, stop=True, skip_group_check=True),
                             reads=["cmat", K("SQ", i % 2)], writes=[bank(4 + h)], signal=(h == 1))

                def S7(i):
                    J, kb, first, last = steps[i]
                    c0 = c0_of(J, kb)
                    if first:
                        for h in range(2):
                            P.op("pe", lambda t: t.matmul(ps[h * 64:(h + 1) * 64, 6, :],
                                                          vtok[par][:, 0, h * 64:(h + 1) * 64], zero_bf[:],
                                                          start=True, stop=False),
                                 reads=["zero", K("v", par, 0)], writes=[bank(6)], signal=False)
                    for h in range(2):
                        P.op("pe", lambda t: t.matmul(ps[h * 64:(h + 1) * 64, 6, c0:],
                                                      vtok[par][:, kb, h * 64:(h + 1) * 64],
                                                      Wb[i % 3][:, h, c0:], start=False, stop=last),
                             reads=[K("v", par, kb // 4), K("W", i % 3)], writes=[bank(6)], signal=(h == 1))
                    if last:
                        P.op("dve", lambda v: v.tensor_copy(out=mixT[:, g, J * 512:(J + 1) * 512], in_=ps[:, 6, :]),
                             reads=[bank(6)], writes=[K("mix", g, J)])

                for n in range(-2, N + 1):
                    if 0 <= n + 2 < N:
                        S1(n + 2)
                    if 0 <= n < N:
                        S4(n)
                        S5(n)
                    if 0 <= n + 1 < N:
                        S2b(n + 1)
                    if 0 <= n + 2 < N:
                        S2a(n + 2)
                    if 0 <= n < N:
                        S6(n)
                    if 0 <= n + 1 < N:
                        S3(n + 1)
                    if 0 <= n - 1 < N:
                        S7(n - 1)
                    bg_step()

            def diff_final(g, J, k2):
                o_ = Os[k2]
                rs = RSs[k2]
                f0, f1, f2 = fin
                kk = K("fin")
                hk, lk = K("hi"), K("lo")
                P.op("act", lambda a_: a_.activation(out=rs[0:64, :], in_=rs[0:64, :], func=AF.Ln),
                     reads=[K("RS", k2)], writes=[K("RS", k2)])
                P.op("act", lambda a_: a_.activation(out=rs[0:64, :], in_=rs[0:64, :], func=AF.Exp, scale=-1.0),
                     reads=[K("RS", k2)], writes=[K("RS", k2)])
                yield
                P.dma("sp", scrd[k2, 0:1, :], rs[0:1, :], reads=[K("RS", k2)], writes=["scr%d0" % k2], slot="bw0")
                P.dma("sp", scrd[k2, 1:2, :], rs[32:33, :], reads=[K("RS", k2)], writes=["scr%d1" % k2], slot="bw1")
                yield
                yield
                P.dma("sp", f1, scrd[k2, 0:1, :].broadcast_to([128, 512]), reads=["scr%d0" % k2],
                      writes=[kk + "1"], slot="bc0")
                P.dma("sp", f2, scrd[k2, 1:2, :].broadcast_to([128, 512]), reads=["scr%d1" % k2],
                      writes=[kk + "2"], slot="bc1")
                yield
                yield
                yield
                P.op("pool", lambda g_: g_.tensor_tensor(out=f1, in0=o_[:, 0, :], in1=f1, op=ALU.mult),
                     reads=[K("Os", k2), kk + "1"], writes=[kk + "1"])
                P.op("pool", lambda g_: g_.tensor_tensor(out=f2, in0=o_[:, 1, :], in1=f2, op=ALU.mult),
                     reads=[K("Os", k2), kk + "2"], writes=[kk + "2"])
                yield
                os_busy[k2] = False
                yield
                P.op("dve", lambda v: v.scalar_tensor_tensor(out=f0, in0=f2, scalar=NLAM, in1=f1, op0=ALU.mult,
                                                             op1=ALU.add),
                     reads=[kk + "1", kk + "2", "misc5"], writes=[kk + "0"])
                yield
                P.op("pool", lambda g_: g_.tensor_tensor(out=f1, in0=f0, in1=f0, op=ALU.mult),
                     reads=[kk + "0"], writes=[kk + "1"])
                P.op("pool", lambda g_: g_.tensor_copy(out=hib, in_=f1), reads=[kk + "1"], writes=[hk])
                P.op("pool", lambda g_: g_.tensor_tensor(out=lob, in0=f1, in1=hib, op=ALU.subtract),
                     reads=[kk + "1", hk], writes=[lk])
                yield
                yield
                b7["req"] = True
                while not b7["clean"]:
                    yield
                P.op("pe", lambda t: t.matmul(ps[:, 7, :], C128, hib, start=True, stop=False),
                     reads=["cmat", hk], writes=[bank(7)], signal=False)
                P.op("pe", lambda t: t.matmul(ps[:, 7, :], C128, lob, start=False, stop=True),
                     reads=["cmat", lk], writes=[bank(7)])
                P.op("dve", lambda v: v.tensor_copy(out=f2, in_=ps[:, 7, :]),
                     reads=[bank(7)], writes=[kk + "2"])
                b7["req"] = False
                yield
                P.op("act", lambda a: a.activation(out=f2, in_=f2, func=AF.Ln, bias=EPS5, scale=1.0),
                     reads=[kk + "2", "misc"], writes=[kk + "2"])
                P.op("act", lambda a: a.activation(out=f2, in_=f2, func=AF.Exp, scale=-0.5),
                     reads=[kk + "2"], writes=[kk + "2"])
                yield
                P.op("dve", lambda v: v.scalar_tensor_tensor(out=mixT[:, g, J * 512:(J + 1) * 512], in0=f0,
                                                             scalar=subg[:, 0:1], in1=f2, op0=ALU.mult, op1=ALU.mult),
                     reads=[kk + "0", kk + "2", "subg"], writes=[K("mix", g, J)])
                yield

            def diff_attention(g, par):
                q_ = qkT[par][:, 0, :]
                k_ = qkT[par][:, 1, :]
                steps = []
                for J in range(NJ):
                    nkb = 4 * J + 4
                    for kb in range(nkb):
                        steps.append((J, kb, kb == 0, kb == nkb - 1))
                N = len(steps)
                Pb = Wb

                def c0_of(J, kb):
                    return max(0, kb * 128 - 512 * J)

                def D1(i):
                    J, kb, first, last = steps[i]
                    c0 = c0_of(J, kb)
                    zb = (i % 2) * 2
                    rk = [K("qk", par, J), K("qk", par, kb // 4)]
                    for m in range(2):
                        lo, hi = m * 64, (m + 1) * 64
                        kT = k_[lo:hi, kb * 128:(kb + 1) * 128]
                        P.op("pe", lambda t: t.matmul(ps[:, zb + m, c0:], kT, q_[lo:hi, J * 512 + c0:(J + 1) * 512],
                                                      start=True, stop=True),
                             reads=rk, writes=[bank(zb + m)], signal=(m == 1))

                def D2(i):
                    J, kb, first, last = steps[i]
                    c0 = c0_of(J, kb)
                    zb = (i % 2) * 2
                    P.op("act", lambda a: a.activation(out=Pb[i % 3][:, :, c0:], in_=ps[:, zb:zb + 2, c0:],
                                                       func=AF.Exp, scale=0.125),
                         reads=[bank(zb), bank(zb + 1)], writes=[K("W", i % 3)])
                    if kb >= 4 * J:
                        pd = Pb[i % 3][:, :, c0:c0 + 128]
                        P.op("pool", lambda g_: g_.tensor_tensor(out=pd, in0=pd,
                                                                 in1=M01.unsqueeze(1).broadcast_to([128, 2, 128]),
                                                                 op=ALU.mult),
                             reads=[K("W", i % 3), "cmat"], writes=[K("W", i % 3)])

                def D3(i):
                    J, kb, first, last = steps[i]
                    c0 = c0_of(J, kb)
                    p_ = Pb[i % 3]
                    for m in range(2):
                        P.op("pe", lambda t: t.matmul(ps[:, 4 + m, c0:], vtok[par][:, kb, :], p_[:, m, c0:],
                                                      start=first, stop=last),
                             reads=[K("v", par, kb // 4), K("W", i % 3)], writes=[bank(4 + m)], signal=False)
                    for m in range(2):
                        P.op("pe", lambda t: t.matmul(ps[32 * m:32 * m + 32, 6, c0:], ONES[:, 0:32], p_[:, m, c0:],
                                                      start=first, stop=last),
                             reads=["cmat", K("W", i % 3)], writes=[bank(6)], signal=(m == 1))
                    if last:
                        k2 = J % 2
                        while os_busy[k2]:
                            bg_step()
                        P.op("act", lambda a: a.activation(out=Os[k2], in_=ps[:, 4:6, :], func=AF.Copy),
                             reads=[bank(4), bank(5)], writes=[K("Os", k2)])
                        P.op("dve", lambda v: v.tensor_copy(out=RSs[k2][0:64, :], in_=ps[0:64, 6, :]),
                             reads=[bank(6)], writes=[K("RS", k2)])
                        os_busy[k2] = True
                        finals.append((g, J, k2))

                for n in range(-1, N + 1):
                    if 0 <= n + 1 < N:
                        D1(n + 1)
                    if 0 <= n < N:
                        D2(n)
                    if 0 <= n - 1 < N:
                        D3(n - 1)
                    bg_step()

            start_group(0)
            bg_drain()
            closing[0] = False
            bg.append(final_worker())
            for pos, g in enumerate(ORDER):
                if pos + 1 < 8:
                    start_group(pos + 1)
                if g < 4:
                    sb_attention(g, pos % 2)
                else:
                    diff_attention(g, pos % 2)
                drain_inproj()
            closing[0] = True
            bg_drain()
            P.barrier()

            AR.reset()
            x1 = AR.alloc([8, 1024], F32)
            h2T = AR.alloc([8, TT], BF16)
            aT = AR.alloc([NFH, TT], BF16)
            wdb = AR.alloc([NFH, 1024], BF16)
            NRING = 3
            wgu = [AR.alloc([2, 8, 128], BF16) for _ in range(NRING)]
            xs2 = [AR.alloc([1024], F32) for _ in range(2)]
            xn2 = [AR.alloc([1024], BF16) for _ in range(3)]
            sg = [AR.alloc([2, 512], F32) for _ in range(2)]
            sqj2 = AR.alloc([1024], BF16)
            stat2 = AR.alloc([8, 4], F32)

            for tt in range(S // TT):
                tk = K
                trow = row0 + tt * TT
                def issue_gu(gf):
                    P.dma("pool", wgu[gf % NRING].rearrange("p a b c -> p (a b c)"), wgud[gf],
                          writes=[tk("wgu", gf % NRING)], slot="wgu%d" % (gf % NRING))

                def issue_wd(half):
                    for f in range(NFH):
                        lt = P.dma("pool", wdb[:, f, :], wdd[half * NFH + f], writes=[tk("wd", f)], slot="wd")
                    for f in range(NFH):
                        P.last_w[tk("wd", f)] = lt

                for f in range(NRING):
                    issue_gu(f)
                issue_wd(0)

                def c1_post(i):
                    for c in range(8):
                        P.op("pe", lambda t, c=c: t.transpose(ps_bf7[:, c * 128:(c + 1) * 128],
                                                              xn2[i % 3][:, c * 128:(c + 1) * 128], ident),
                             reads=[tk("xn2", i % 3), "cmat"], writes=[bank(7)], signal=(c == 7))
                    P.op("act", lambda a: a.activation(out=h2T[:, :, i * 128:(i + 1) * 128],
                                                       in_=ps_bf7.rearrange("p (c n) -> p c n", c=8), func=AF.Copy),
                         reads=[bank(7)], writes=[tk("h2T", i)])

                for i in range(8):
                    Jg = (tt * TT + i * 128) // 512
                    col = tt * TT + i * 128
                    yb = (i % 2) * 2
                    P.dma("sp", xs2[i % 2], xd[trow + i * 128: trow + (i + 1) * 128, :], writes=[tk("xs2", i % 2)],
                          slot="xs%d" % (i % 2))
                    for hh in range(2):
                        for kc in range(8):
                            P.op("pe", lambda t, hh=hh, kc=kc: t.matmul(
                                ps[:, yb + hh, :], mixT[:, kc, col:col + 128], wo[:, kc, hh * 512:(hh + 1) * 512],
                                start=(kc == 0), stop=(kc == 7)),
                                 reads=[K("mix", kc, Jg), "wo"], writes=[bank(yb + hh)],
                                 signal=(kc == 7 and hh == 1))
                    P.op("dve", lambda v: v.tensor_tensor(out=x1[:, i, :].rearrange("p (a n) -> p a n", a=2),
                                                          in0=ps[:, yb:yb + 2, :],
                                                          in1=xs2[i % 2].rearrange("p (a n) -> p a n", a=2),
                                                          op=ALU.add),
                         reads=[bank(yb), bank(yb + 1), tk("xs2", i % 2)], writes=[tk("x1", i)])
                    P.op("act", lambda a: a.activation(out=sqj2, in_=x1[:, i, :], func=AF.Square,
                                                       accum_out=stat2[:, i, 0:1]),
                         reads=[tk("x1", i)], writes=[tk("sqj2"), tk("st2", i)])
                    rstd_from_ss(stat2[:, i, 0:1], stat2[:, i, 1:2], D, EPS6, tk("st2", i), tk("rs2", i))
                    P.op("dve", lambda v: v.scalar_tensor_tensor(
                        out=xn2[i % 3], in0=x1[:, i, :], scalar=stat2[:, i, 1:2], in1=small[:, SP_FG:SP_FG + 1024],
                        op0=ALU.mult, op1=ALU.mult),
                         reads=[tk("x1", i), tk("rs2", i), "small"], writes=[tk("xn2", i % 3)])
                    if i >= 2:
                        c1_post(i - 2)
                c1_post(6)
                c1_post(7)

                for half in range(2):
                    for f in range(NFH):
                        gf = half * NFH + f
                        slot = gf % NRING
                        gb_ = (gf % 2) * 2
                        ub_ = 4 + (gf % 2) * 2
                        for gu, bb in ((0, gb_), (1, ub_)):
                            for hh in range(2):
                                for kc in range(8):
                                    P.op("pe", lambda t, gu=gu, bb=bb, hh=hh, kc=kc: t.matmul(
                                        ps[:, bb + hh, :], wgu[slot][:, gu, kc, :], h2T[:, kc, hh * 512:(hh + 1) * 512],
                                        start=(kc == 0), stop=(kc == 7)),
                                         reads=[tk("wgu", slot)] + [tk("h2T", 4 * hh + q) for q in range(4)],
                                         writes=[bank(bb + hh)], signal=(kc == 7 and hh == 1))
                        P.op("act", lambda a: a.activation(out=sg[gf % 2], in_=ps[:, gb_:gb_ + 2, :], func=AF.Silu),
                             reads=[bank(gb_), bank(gb_ + 1)], writes=[tk("sg", gf % 2)])
                        P.op("dve", lambda v: v.tensor_tensor(out=aT[:, f, :].rearrange("p (a n) -> p a n", a=2),
                                                              in0=ps[:, ub_:ub_ + 2, :], in1=sg[gf % 2], op=ALU.mult),
                             reads=[bank(ub_), bank(ub_ + 1), tk("sg", gf % 2)], writes=[tk("aT", f)])
                        if f + NRING < NFH:
                            issue_gu(gf + NRING)
                    for i in range(8):
                        ob = (i % 2) * 2
                        for hh in range(2):
                            for f in range(NFH):
                                P.op("pe", lambda t, hh=hh, f=f: t.matmul(
                                    ps[:, ob + hh, :], aT[:, f, i * 128:(i + 1) * 128],
                                    wdb[:, f, hh * 512:(hh + 1) * 512], start=(f == 0), stop=(f == NFH - 1)),
                                     reads=[tk("aT", f), tk("wd", f)], writes=[bank(ob + hh)],
                                     signal=(f == NFH - 1 and hh == 1))
                        P.op("dve", lambda v: v.tensor_tensor(out=x1[:, i, :].rearrange("p (a n) -> p a n", a=2),
                                                              in0=ps[:, ob:ob + 2, :],
                                                              in1=x1[:, i, :].rearrange("p (a n) -> p a n", a=2),
                                                              op=ALU.add),
                             reads=[bank(ob), bank(ob + 1), tk("x1", i)], writes=[tk("x1", i)])
                        if half == 1:
                            P.dma("sp", yd[trow + i * 128: trow + (i + 1) * 128, :], x1[:, i, :],
                                  reads=[tk("x1", i)], slot="out%d" % i, is_out=True)
                    if half == 0:
                        for f in range(NRING):
                            issue_gu(NFH + f)
                        issue_wd(1)
            P.barrier()
        P.barrier()
        P.finish()
    return nc


def _host_constants():
    j = np.arange(128)
    ident = np.eye(128, dtype=np.float32)
    L1 = (j[:, None] >= j[None, :]).astype(np.float32)
    L2 = (j[:, None] < j[None, :]).astype(np.float32)
    negs = np.where(j[:, None] >= j[None, :], NEG, 0.0).astype(np.float32)
    negi = np.where(j[:, None] > j[None, :], NEG, 0.0).astype(np.float32)
    ones = np.ones((128, 128), np.float32)
    m01 = (j[:, None] <= j[None, :]).astype(np.float32)
    cmat = np.concatenate([ident, L1, L2, negs, negi, ones, ones / 32.0, ones / 128.0, m01], axis=1)
    half = HD // 2
    inv_freq = (np.float32(10000.0) ** (-np.arange(half, dtype=np.float32) / np.float32(half))).astype(np.float32)
    pos = np.arange(S, dtype=np.float32)
    ang = (pos[:, None] * inv_freq[None, :]).astype(np.float32)
    cos, sin = np.cos(ang.astype(np.float64)).astype(np.float32), np.sin(ang.astype(np.float64)).astype(np.float32)
    CC = np.concatenate([cos, cos], axis=1).reshape(NT, 128, 64).transpose(1, 0, 2)
    SS = np.concatenate([-sin, sin], axis=1).reshape(NT, 128, 64).transpose(1, 0, 2)
    rope = np.stack([CC, SS], axis=1).reshape(128, 2 * NT * 64).astype(np.float32)
    return np.ascontiguousarray(cmat), np.ascontiguousarray(rope)


def _prep_weights(inp):
    w_in = np.asarray(inp["w_in"], np.float32)[0]
    groups = []
    for g in range(8):
        if g < 4:
            cols = np.r_[128 * g:128 * g + 128, 512 + 128 * g:512 + 128 * g + 128,
                         1024 + 128 * g:1024 + 128 * g + 128]
        else:
            h = g - 4
            cols = np.r_[1536 + 128 * h:1536 + 128 * h + 128, 2048 + 128 * h:2048 + 128 * h + 128,
                         2560 + 128 * h:2560 + 128 * h + 128]
        wg = w_in[:, cols].reshape(8, 128, 384).transpose(1, 0, 2).reshape(128, 8 * 384)
        groups.append(wg)
    wing = np.ascontiguousarray(np.stack(groups, 0))
    wo = np.asarray(inp["w_o"], np.float32)[0].reshape(8, 128, 1024).transpose(1, 0, 2).reshape(128, 8 * 1024)
    wg_ = np.asarray(inp["w_gate"], np.float32)[0].reshape(8, 128, NF, 128)
    wu_ = np.asarray(inp["w_up"], np.float32)[0].reshape(8, 128, NF, 128)
    wgu = np.stack([wg_, wu_], 0).transpose(3, 2, 0, 1, 4).reshape(NF, 128, 2 * 8 * 128)
    wd = np.asarray(inp["w_down"], np.float32)[0].reshape(NF, 128, 1024)
    gq = np.asarray(inp["diff_q_norm_g"], np.float32)[0]
    gk = np.asarray(inp["diff_k_norm_g"], np.float32)[0]
    sw = lambda v: np.concatenate([v[32:], v[:32]])
    small = np.concatenate([
        np.asarray(inp["attn_norm_g"], np.float32)[0], np.asarray(inp["ffn_norm_g"], np.float32)[0],
        gq, sw(gq), gk, sw(gk),
        np.asarray(inp["lambda_q1"], np.float32)[0], np.asarray(inp["lambda_k1"], np.float32)[0],
        np.asarray(inp["lambda_q2"], np.float32)[0], np.asarray(inp["lambda_k2"], np.float32)[0]])
    small = np.ascontiguousarray(np.broadcast_to(small[None, :], (128, SP_N)))
    subln = np.ascontiguousarray(np.asarray(inp["diff_subln_g"], np.float32)[0].reshape(128, 1))
    return dict(wing=wing, wo=np.ascontiguousarray(wo), wgu=np.ascontiguousarray(wgu), wd=np.ascontiguousarray(wd),
                small=small, subln=subln)


def kernel(**inputs):
    x = np.asarray(inputs["x"], np.float32)
    wmaps = _prep_weights(inputs)
    cmat, rope = _host_constants()
    nc = build_program(NSEQ)
    in_maps = []
    for c in range(NCORES):
        m = dict(wmaps)
        m["x"] = np.ascontiguousarray(x[c * NSEQ:(c + 1) * NSEQ].reshape(NSEQ * S, D))
        m["cmat"] = cmat
        m["rope"] = rope
        in_maps.append(m)
    res = run_bass_kernel_spmd(nc, in_maps, core_ids=list(range(NCORES)))
    out = np.concatenate([np.asarray(r["y"], np.float32).reshape(NSEQ, S, D) for r in res.results], axis=0)
    return out
```

```python
import math
from contextlib import ExitStack

import numpy as np
import concourse.bass as bass
import concourse.mybir as mybir
from concourse.bass_utils import run_bass_kernel_spmd

F32 = mybir.dt.float32
BF16 = mybir.dt.bfloat16
AF = mybir.ActivationFunctionType
ALU = mybir.AluOpType
AX = mybir.AxisListType

NCORES = 8
D = 1024
S = 2048
BATCH = 32
NSEQ = BATCH // NCORES
DFF = 2816
NF = DFF // 128
NFH = NF // 2
HD = 64
NT = S // 128
NJ = S // 512
TT = 1024
NEG = -30000.0
LAMBDA_INIT = 0.8 - 0.6 * math.exp(-0.3 * 0)

SP_AG, SP_FG, SP_GQ, SP_GQS, SP_GK, SP_GKS, SP_L = 0, 1024, 2048, 2112, 2176, 2240, 2304
SP_N = 2304 + 256


class Tok:
    __slots__ = ("sem", "val", "eng")

    def __init__(self, sem, val, eng):
        self.sem, self.val, self.eng = sem, val, eng


class Prog:
    def __init__(self, nc, es):
        self.nc = nc
        self.E = {"pe": nc.tensor, "act": nc.scalar, "dve": nc.vector, "pool": nc.gpsimd, "sp": nc.sync}
        self.sem = {e: es.enter_context(nc.semaphore("s_" + e)) for e in self.E}
        self.cnt = {e: 0 for e in self.E}
        self.waited = {e: {} for e in self.E}
        self.last_w = {}
        self.readers = {}
        self.pending = {e: [] for e in self.E}
        self.dma_sem = {}
        self.dma_cnt = {}
        self.es = es
        self.out_toks = []
        self.all_dma = []

    def _wait(self, e, toks):
        best = {}
        for t in toks:
            if t is None:
                continue
            assert t.val is not None, "dependency on unsignaled op"
            if t.val > best.get(t.sem, 0):
                best[t.sem] = t.val
        for sname, v in best.items():
            if v > self.waited[e].get(sname, 0):
                self.E[e].wait_ge(self._semh(sname), v)
                self.waited[e][sname] = v

    def _semh(self, sname):
        return self.sem[sname] if sname in self.sem else self.dma_sem[sname]

    def _deps(self, e, reads, writes, is_dma=False, slot=None):
        deps = []
        for k in reads:
            t = self.last_w.get(k)
            if t is not None:
                if not (t.eng == e and e == "pe"):
                    deps.append(t)
            if len(k) == 2 and k[0] == "b":
                for r in self.readers.get(k, ()):
                    if r.eng != e:
                        deps.append(r)
        for k in writes:
            t = self.last_w.get(k)
            same_ok = (not is_dma) and e != "pool"
            if t is not None and not (t.eng == e and same_ok) and not (is_dma and t.sem == slot):
                deps.append(t)
            for r in self.readers.get(k, ()):
                if r.eng == e and same_ok:
                    continue
                deps.append(r)
        return deps

    def op(self, e, fn, reads=(), writes=(), signal=True):
        self._wait(e, self._deps(e, reads, writes))
        inst = fn(self.E[e])
        tok = Tok(e, None, e)
        if signal:
            inst.then_inc(self.sem[e], 1)
            self.cnt[e] += 1
            tok.val = self.cnt[e]
            for p in self.pending[e]:
                p.val = self.cnt[e]
            self.pending[e] = []
        else:
            self.pending[e].append(tok)
        for k in reads:
            self.readers.setdefault(k, []).append(tok)
        for k in writes:
            self.last_w[k] = tok
            self.readers[k] = []
        return tok

    def dma(self, q, out, in_, reads=(), writes=(), slot="d", is_out=False):
        sname = "dma_" + slot
        if sname not in self.dma_sem:
            self.dma_sem[sname] = self.es.enter_context(self.nc.semaphore(sname))
            self.dma_cnt[sname] = 0
        self._wait(q, self._deps(q, reads, writes, is_dma=True, slot=sname))
        self.E[q].dma_start(out=out, in_=in_).then_inc(self.dma_sem[sname], 16)
        self.dma_cnt[sname] += 16
        tok = Tok(sname, self.dma_cnt[sname], None)
        for k in reads:
            self.readers.setdefault(k, []).append(tok)
        for k in writes:
            self.last_w[k] = tok
            self.readers[k] = []
        if is_out:
            self.out_toks.append(tok)
        self.all_dma.append(tok)
        return tok

    def barrier(self):
        for e in self.E:
            assert not self.pending[e], "unsignaled tail on " + e
        toks = [Tok(e, self.cnt[e], e) for e in self.E if self.cnt[e] > 0]
        toks += [Tok(s, c, None) for s, c in self.dma_cnt.items()]
        for e in self.E:
            self._wait(e, [t for t in toks if t.eng != e])

    def finish(self):
        self._wait("sp", self.out_toks)


class Arena:
    log = []

    def __init__(self, t, nelem):
        self.t, self.n, self.off = t, nelem, 0

    def reset(self, off=0):
        self.off = off

    def alloc(self, free_shape, dt):
        n = int(np.prod(free_shape))
        sz = n * (2 if dt == F32 else 1)
        self.off = (self.off + 15) // 16 * 16
        a = self.t[:, self.off:self.off + sz]
        Arena.log.append((self.off, sz, str(dt), tuple(free_shape)))
        self.off += sz
        assert self.off <= self.n, ("arena overflow", self.off, self.n)
        if dt == F32:
            a = a.bitcast(F32)
        if len(free_shape) == 2:
            a = a.rearrange("p (a b) -> p a b", a=free_shape[0])
        elif len(free_shape) == 3:
            a = a.rearrange("p (a b c) -> p a b c", a=free_shape[0], b=free_shape[1])
        elif len(free_shape) == 4:
            a = a.rearrange("p (a b c d) -> p a b c d", a=free_shape[0], b=free_shape[1], c=free_shape[2])
        return a


def build_program(nseq=NSEQ):
    nc = bass.Bass("TRN2", target_bir_lowering=False)
    ntok = nseq * S
    xd = nc.dram_tensor("x", [ntok, D], F32, kind="ExternalInput").ap()
    wing = nc.dram_tensor("wing", [8, 128, 8 * 384], F32, kind="ExternalInput").ap()
    wod = nc.dram_tensor("wo", [128, 8 * 1024], F32, kind="ExternalInput").ap()
    wgud = nc.dram_tensor("wgu", [NF, 128, 2 * 8 * 128], F32, kind="ExternalInput").ap()
    wdd = nc.dram_tensor("wd", [NF, 128, 1024], F32, kind="ExternalInput").ap()
    smalld = nc.dram_tensor("small", [128, SP_N], F32, kind="ExternalInput").ap()
    sublnd = nc.dram_tensor("subln", [128, 1], F32, kind="ExternalInput").ap()
    cmatd = nc.dram_tensor("cmat", [128, 9 * 128], F32, kind="ExternalInput").ap()
    roped = nc.dram_tensor("rope", [128, 2 * NT * 64], F32, kind="ExternalInput").ap()
    yd = nc.dram_tensor("y", [ntok, D], F32, kind="ExternalOutput").ap()
    scrd = nc.dram_tensor("bc_scratch", [2, 2, 512], F32, kind="Internal").ap()

    with ExitStack() as es:
        P = Prog(nc, es)
        sb = lambda name, shape, dt: es.enter_context(nc.sbuf_tensor(name, shape, dt))
        ps = es.enter_context(nc.psum_tensor("ps", [128, 8, 512], F32))
        wo = sb("wo_sb", [128, 8, 1024], BF16)
        mixT = sb("mixT", [128, 8, S], BF16)
        small = sb("small_sb", [128, 2048], F32)
        cmat = sb("cmat_sb", [128, 9, 128], BF16)
        rtab = sb("rtab", [128, NT, 2, 2, 64], F32)
        misc = sb("misc", [128, 16], F32)
        zero_bf = sb("zero_bf", [128, 512], BF16)
        subg = sb("subg", [128, 1], F32)
        ltmp = sb("ltmp", [128, 2, 64], F32)
        ARENA = (int(nc.sbuf_bytes_remaining) - 2048) // 64 * 32
        arena_t = sb("arena", [128, ARENA], BF16)
        AR = Arena(arena_t, ARENA)
        ropeT = AR.alloc([2, NT, 64], F32)
        small2 = AR.alloc([SP_N - 2048], F32)

        ident = cmat[:, 0, :]
        L1 = cmat[:, 1, :]
        L2 = cmat[:, 2, :]
        NEGS = cmat[:, 3, :]
        NEGI = cmat[:, 4, :]
        ONES = cmat[:, 5, :]
        C32 = cmat[:, 6, :]
        C128 = cmat[:, 7, :]
        M01 = cmat[:, 8, :]
        EPS6, EPS5, ONE, LAM, NLAM = (misc[:, i:i + 1] for i in range(5))
        ps_bf7 = ps[:, 7, :].bitcast(BF16)

        def bank(b):
            return "b%d" % b

        P.dma("pool", cmat[:].rearrange("p a b -> p (a b)"), cmatd, writes=["cmat"], slot="c0")
        P.dma("sp", small[:], smalld[:, 0:2048], writes=["small"], slot="c1")
        P.dma("sp", small2, smalld[:, 2048:SP_N], writes=["small2"], slot="c5")
        P.dma("sp", ropeT.rearrange("p a t d -> p (a t d)"), roped, writes=["rope"], slot="c2")
        P.dma("sp", subg[:], sublnd, writes=["subg"], slot="c3")
        P.dma("pool", wo[:].rearrange("p a b -> p (a b)"), wod, writes=["wo"], slot="c4")
        P.op("dve", lambda v: v.memset(misc[:, 0:1], 1e-6), writes=["misc"])
        P.op("dve", lambda v: v.memset(misc[:, 1:2], 1e-5), writes=["misc"])
        P.op("dve", lambda v: v.memset(misc[:, 2:3], 1.0), writes=["misc"])
        P.op("dve", lambda v: v.memset(zero_bf[:], 0.0), writes=["zero"])
        P.op("dve", lambda v: v.tensor_scalar(out=subg[:], in0=subg[:], scalar1=1.0 - LAMBDA_INIT, scalar2=None,
                                              op0=ALU.mult), reads=["subg"], writes=["subg"])
        lv = lambda i: small2[:, SP_L - 2048 + 64 * i: SP_L - 2048 + 64 * (i + 1)]
        P.op("dve", lambda v: v.tensor_tensor(out=ltmp[:, 0, :], in0=lv(0), in1=lv(1), op=ALU.mult),
             reads=["small2"], writes=["ltmp"])
        P.op("dve", lambda v: v.tensor_tensor(out=ltmp[:, 1, :], in0=lv(2), in1=lv(3), op=ALU.mult),
             reads=["small2"], writes=["ltmp"])
        P.op("dve", lambda v: v.tensor_reduce(out=misc[:, 8:10], in_=ltmp[:], axis=AX.X, op=ALU.add),
             reads=["ltmp"], writes=["misc"])
        P.op("act", lambda a: a.activation(out=misc[:, 10:12], in_=misc[:, 8:10], func=AF.Exp),
             reads=["misc"], writes=["misc2"])
        P.op("dve", lambda v: v.tensor_tensor(out=misc[:, 12:13], in0=misc[:, 10:11], in1=misc[:, 11:12],
                                              op=ALU.subtract), reads=["misc2"], writes=["misc3"])
        P.op("dve", lambda v: v.tensor_scalar(out=misc[:, 3:4], in0=misc[:, 12:13], scalar1=LAMBDA_INIT, scalar2=None,
                                              op0=ALU.add), reads=["misc3"], writes=["misc4"])
        P.op("dve", lambda v: v.tensor_scalar(out=misc[:, 4:5], in0=misc[:, 3:4], scalar1=-1.0, scalar2=None,
                                              op0=ALU.mult), reads=["misc4"], writes=["misc5"])
        for qk, (go, gso) in enumerate(((SP_GQ, SP_GQS), (SP_GK, SP_GKS))):
            gb = small2[:, go - 2048:go - 2048 + 64].unsqueeze(1).broadcast_to([128, NT, 64])
            gsb = small2[:, gso - 2048:gso - 2048 + 64].unsqueeze(1).broadcast_to([128, NT, 64])
            P.op("pool", lambda g, gb=gb, qk=qk: g.tensor_tensor(out=rtab[:, :, qk, 0, :], in0=ropeT[:, 0, :, :],
                                                                 in1=gb, op=ALU.mult),
                 reads=["small2", "rope"], writes=["rtab"])
            P.op("pool", lambda g, gsb=gsb, qk=qk: g.tensor_tensor(out=rtab[:, :, qk, 1, :], in0=ropeT[:, 1, :, :],
                                                                   in1=gsb, op=ALU.mult),
                 reads=["small2", "rope"], writes=["rtab"])

        P.barrier()

        def rstd_from_ss(ss_ap, out_ap, n, eps_ap, rk, wk):
            P.op("act", lambda a: a.activation(out=out_ap, in_=ss_ap, func=AF.Ln, bias=eps_ap, scale=1.0 / n),
                 reads=[rk, "misc"], writes=[wk])
            P.op("act", lambda a: a.activation(out=out_ap, in_=out_ap, func=AF.Exp, scale=-0.5),
                 reads=[wk], writes=[wk])

        for b in range(nseq):
            row0 = b * S
            AR.reset()
            hT = AR.alloc([8, S], BF16)
            stat = AR.alloc([NT, 4], F32)
            wgb = [AR.alloc([8, 384], BF16) for _ in range(2)]
            qkT = [AR.alloc([2, S], BF16) for _ in range(2)]
            vtok = [AR.alloc([NT, 128], BF16) for _ in range(2)]
            Eb = [AR.alloc([2, 512], F32) for _ in range(3)]
            xs = [e_.rearrange("p a b -> p (a b)") for e_ in Eb]
            SQb = [AR.alloc([2, 512], BF16) for _ in range(2)]
            xn = [q_.rearrange("p a b -> p (a b)") for q_ in SQb]
            Fb = [AR.alloc([2, 512], F32) for _ in range(2)]
            Wb = [AR.alloc([2, 512], BF16) for _ in range(3)]
            sqj = Wb[0].rearrange("p a b -> p (a b)")
            qk32 = [AR.alloc([256], F32) for _ in range(2)]
            sqs = [AR.alloc([256], F32) for _ in range(2)]
            rt1 = [AR.alloc([256], F32) for _ in range(4)]
            rt2 = AR.alloc([256], F32)
            qkb = [AR.alloc([256], BF16) for _ in range(4)]
            dstat = [AR.alloc([8], F32) for _ in range(4)]
            Os = [AR.alloc([2, 512], F32) for _ in range(2)]
            RSs = [AR.alloc([512], F32) for _ in range(2)]
            fin = [AR.alloc([512], F32) for _ in range(3)]
            hib = AR.alloc([512], BF16)
            lob = AR.alloc([512], BF16)
            pfx = "s%d_" % b
            K = lambda *a: pfx + "_".join(str(x) for x in a)

            def phaseA_evac(i):
                for c in range(8):
                    P.op("pe", lambda t, c=c, i=i: t.transpose(ps_bf7[:, c * 128:(c + 1) * 128],
                                                               xn[i % 2][:, c * 128:(c + 1) * 128], ident),
                         reads=[K("SQ", i % 2), "cmat"], writes=[bank(7)], signal=(c == 7))
                if i % 2 == 0:
                    P.op("act", lambda a, i=i: a.activation(
                        out=hT[:, :, i * 128:(i + 1) * 128],
                        in_=ps_bf7.rearrange("p (c n) -> p c n", c=8), func=AF.Copy),
                         reads=[bank(7)], writes=[K("hT", i)])
                else:
                    P.op("dve", lambda v, i=i: v.tensor_copy(
                        out=hT[:, :, i * 128:(i + 1) * 128],
                        in_=ps_bf7.rearrange("p (c n) -> p c n", c=8)),
                         reads=[bank(7)], writes=[K("hT", i)])

            for i in range(NT):
                P.dma("sp", xs[i % 2], xd[row0 + i * 128: row0 + (i + 1) * 128, :], writes=[K("E", i % 2)],
                      slot="xs%d" % (i % 2))
                P.op("act", lambda a, i=i: a.activation(out=sqj, in_=xs[i % 2], func=AF.Square,
                                                        accum_out=stat[:, i, 0:1]),
                     reads=[K("E", i % 2)], writes=[K("W", 0), K("stat", i)])
                rstd_from_ss(stat[:, i, 0:1], stat[:, i, 1:2], D, EPS6, K("stat", i), K("rstd", i))
                P.op("dve", lambda v, i=i: v.scalar_tensor_tensor(
                    out=xn[i % 2], in0=xs[i % 2], scalar=stat[:, i, 1:2], in1=small[:, SP_AG:SP_AG + 1024],
                    op0=ALU.mult, op1=ALU.mult),
                     reads=[K("E", i % 2), K("rstd", i), "small"], writes=[K("SQ", i % 2)])
                if i >= 1:
                    phaseA_evac(i - 1)
            phaseA_evac(NT - 1)

            def load_wg(g, par):
                P.dma("pool", wgb[par].rearrange("p a b -> p (a b)"), wing[g], writes=[K("wg", par)],
                      slot="wg%d" % par)

            b7 = {"req": False, "clean": True}

            def run_stages(n_items, stages):
                done = {st[0]: 0 for st in stages}
                b7w = [(st[0], st[3]) for st in stages if st[3]]
                while any(done[st[0]] < n_items for st in stages):
                    snap = dict(done)
                    for name, fn, prods, rd7, limits in stages:
                        i = done[name]
                        if i >= n_items or any(snap[p] <= i for p in prods):
                            continue
                        if any(i - done[c] >= d for c, d in limits):
                            continue
                        if rd7 and (b7["req"] or done[rd7] < i):
                            continue
                        fn(i)
                        done[name] += 1
                    b7["clean"] = all(done[r] == done[w] for w, r in b7w)
                    yield
                b7["clean"] = True

            def inproj_sb(g, par, atomic):
                wgk = K("wg", par)
                units = []
                for j in range(NJ):
                    units += [("qk", j, 0), ("qk", j, 1), ("v", j, 0)]

                fg = (g == ORDER[0])
                bk = (lambda u: 6 + (u % 2)) if fg else (lambda u: 7)

                def mm(u):
                    kind, j, which = units[u]
                    hk = [K("hT", 4 * j + t) for t in range(4)]
                    if kind == "qk":
                        for kc in range(8):
                            P.op("pe", lambda t: t.matmul(
                                ps[:, bk(u), :], wgb[par][:, kc, which * 128:(which + 1) * 128],
                                hT[:, kc, j * 512:(j + 1) * 512], start=(kc == 0), stop=(kc == 7)),
                                 reads=[wgk] + hk, writes=[bank(bk(u))], signal=(kc == 7))
                    else:
                        for t4 in range(4):
                            for kc in range(8):
                                P.op("pe", lambda t: t.matmul(
                                    ps[:, bk(u), t4 * 128:(t4 + 1) * 128],
                                    hT[:, kc, (4 * j + t4) * 128:(4 * j + t4 + 1) * 128],
                                    wgb[par][:, kc, 256:384], start=(kc == 0), stop=(kc == 7)),
                                     reads=[wgk] + hk, writes=[bank(bk(u))], signal=(kc == 7 and t4 == 3))

                def ev(u):
                    kind, j, which = units[u]
                    if kind == "qk":
                        P.op("dve", lambda v: v.tensor_copy(
                            out=qkT[par][:, which, j * 512:(j + 1) * 512], in_=ps[:, bk(u), :]),
                             reads=[bank(bk(u))], writes=[K("qk", par, j)])
                    else:
                        P.op("dve", lambda v: v.tensor_copy(
                            out=vtok[par][:, 4 * j:4 * j + 4, :],
                            in_=ps[:, bk(u), :].rearrange("p (a n) -> p a n", a=4)),
                             reads=[bank(bk(u))], writes=[K("v", par, j)])

                if fg:
                    for u in range(len(units)):
                        mm(u)
                        if u >= 1:
                            ev(u - 1)
                        yield
                    ev(len(units) - 1)
                    yield
                else:
                    yield from run_stages(len(units), [("ev", ev, ["mm"], None, []), ("mm", mm, [], "ev", [])])

            def inproj_diff(g, par, atomic):
                wgk = K("wg", par)

                def st_mm(i):
                    for kc in range(8):
                        P.op("pe", lambda t: t.matmul(
                            ps[:, 7, 0:384], hT[:, kc, i * 128:(i + 1) * 128], wgb[par][:, kc, :],
                            start=(kc == 0), stop=(kc == 7)),
                             reads=[wgk, K("hT", i)], writes=[bank(7)], signal=(kc == 7))

                def st_ev(i):
                    k2 = i % 2
                    P.op("dve", lambda v: v.tensor_copy(out=qk32[k2], in_=ps[:, 7, 0:256]),
                         reads=[bank(7)], writes=[K("qk32", k2)])
                    P.op("dve", lambda v: v.tensor_copy(out=vtok[par][:, i, :], in_=ps[:, 7, 256:384]),
                         reads=[bank(7)], writes=[K("v", par, i // 4)])
                    P.op("dve", lambda v: v.tensor_tensor(out=sqs[k2], in0=qk32[k2], in1=qk32[k2], op=ALU.mult),
                         reads=[K("qk32", k2)], writes=[K("sqs", k2)])
                    k4 = i % 4
                    x4 = qk32[k2].rearrange("p (q m d) -> p q m d", q=2, m=2)
                    o1 = rt1[k4].rearrange("p (q m d) -> p q m d", q=2, m=2)
                    o2 = rt2.rearrange("p (q m d) -> p q m d", q=2, m=2)
                    tabA = rtab[:, i, :, 0, :].unsqueeze(2).broadcast_to([128, 2, 2, 64])
                    tabB_lo = rtab[:, i, :, 1, 0:32].unsqueeze(2).broadcast_to([128, 2, 2, 32])
                    tabB_hi = rtab[:, i, :, 1, 32:64].unsqueeze(2).broadcast_to([128, 2, 2, 32])
                    P.op("dve", lambda v: v.tensor_tensor(out=o1, in0=x4, in1=tabA, op=ALU.mult),
                         reads=[K("qk32", k2), "rtab"], writes=[K("rt1", k4)])
                    P.op("pool", lambda g_: g_.tensor_tensor(out=o2[:, :, :, 0:32], in0=x4[:, :, :, 32:64],
                                                             in1=tabB_lo, op=ALU.mult),
                         reads=[K("qk32", k2), "rtab"], writes=[K("rt2")])
                    P.op("pool", lambda g_: g_.tensor_tensor(out=o2[:, :, :, 32:64], in0=x4[:, :, :, 0:32],
                                                             in1=tabB_hi, op=ALU.mult),
                         reads=[K("qk32", k2), "rtab"], writes=[K("rt2")])

                def st_add(i):
                    k4 = i % 4
                    P.op("pool", lambda g_: g_.tensor_tensor(out=rt1[k4], in0=rt1[k4], in1=rt2, op=ALU.add),
                         reads=[K("rt1", k4), K("rt2")], writes=[K("rt1", k4)])

                def st_red(i):
                    k2, k4 = i % 2, i % 4
                    P.op("dve", lambda v: v.tensor_reduce(out=dstat[k4][:, 0:4],
                                                          in_=sqs[k2].rearrange("p (g d) -> p g d", g=4),
                                                          axis=AX.X, op=ALU.add),
                         reads=[K("sqs", k2)], writes=[K("dss", k4)])

                def st_rstd(i):
                    k4 = i % 4
                    rstd_from_ss(dstat[k4][:, 0:4], dstat[k4][:, 4:8], HD, EPS6, K("dss", k4), K("drs", k4))

                def st_fin(i):
                    k4 = i % 4
                    rb = dstat[k4][:, 4:8].unsqueeze(2).broadcast_to([128, 4, 64])
                    P.op("pool", lambda v: v.tensor_tensor(out=qkb[k4].rearrange("p (g d) -> p g d", g=4),
                                                           in0=rt1[k4].rearrange("p (g d) -> p g d", g=4),
                                                           in1=rb, op=ALU.mult),
                         reads=[K("rt1", k4), K("drs", k4)], writes=[K("qkb", k4)])

                def st_tr(i):
                    k4 = i % 4
                    for which in range(2):
                        P.op("pe", lambda t: t.transpose(
                            ps_bf7[:, 768 + which * 128: 768 + (which + 1) * 128],
                            qkb[k4][:, which * 128:(which + 1) * 128], ident),
                             reads=[K("qkb", k4), "cmat"], writes=[bank(7)], signal=(which == 1))

                def st_tev(i):
                    P.op("dve", lambda v: v.tensor_copy(
                        out=qkT[par][:, :, i * 128:(i + 1) * 128],
                        in_=ps_bf7[:, 768:1024].rearrange("p (a n) -> p a n", a=2)),
                         reads=[bank(7)], writes=[K("qk", par, i // 4)])

                yield from run_stages(NT, [
                    ("add", st_add, ["ev"], None, []),
                    ("ev", st_ev, ["mm"], None, [("fin", 4), ("red", 2), ("add", 1)]),
                    ("tev", st_tev, ["tr"], None, []),
                    ("tr", st_tr, ["w"], "tev", []),
                    ("mm", st_mm, [], "ev", []),
                    ("w", lambda i: None, ["fin"], None, []),
                    ("fin", st_fin, ["rstd", "add"], None, [("tr", 4)]),
                    ("rstd", st_rstd, ["red"], None, []),
                    ("red", st_red, ["ev"], None, [("fin", 4)]),
                ])

            bg = []

            def bg_step(n=1):
                for _ in range(n):
                    for gen in list(bg):
                        try:
                            next(gen)
                        except StopIteration:
                            bg.remove(gen)

            def bg_drain():
                while bg:
                    bg_step()

            inproj_gen = [None]
            finals = []
            os_busy = {0: False, 1: False}
            closing = [False]

            def final_worker():
                while True:
                    if finals:
                        yield from diff_final(*finals.pop(0))
                    elif closing[0]:
                        return
                    else:
                        yield

            ORDER = [0, 4, 1, 5, 2, 6, 7, 3]

            def throttle(gen, every):
                k = 0
                for _ in gen:
                    yield
                    k += 1
                    if k % every == 0:
                        yield

            def start_group(pos):
                g, par = ORDER[pos], pos % 2
                load_wg(g, par)
                during_diff = pos >= 1 and ORDER[pos - 1] >= 4
                inproj_gen[0] = inproj_sb(g, par, during_diff) if g < 4 else throttle(inproj_diff(g, par, True), 2)
                bg.append(inproj_gen[0])

            def drain_inproj():
                while inproj_gen[0] in bg:
                    bg_step()

            def sb_attention(g, par):
                q_ = qkT[par][:, 0, :]
                k_ = qkT[par][:, 1, :]
                steps = []
                for J in range(NJ):
                    nkb = 4 * J + 4
                    for kb in range(nkb - 1, -1, -1):
                        steps.append((J, kb, kb == nkb - 1, kb == 0))
                N = len(steps)

                def c0_of(J, kb):
                    return max(0, kb * 128 - 512 * J)

                def S1(i):
                    J, kb, first, last = steps[i]
                    c0 = c0_of(J, kb)
                    zb = (i % 2) * 2
                    diag = kb >= 4 * J
                    rk = [K("qk", par, J), K("qk", par, kb // 4)]
                    for h in range(2):
                        lo, hi = h * 64, (h + 1) * 64
                        kT = k_[lo:hi, kb * 128:(kb + 1) * 128]
                        if diag:
                            P.op("pe", lambda t: t.matmul(ps[:, zb + h, c0:c0 + 128], kT,
                                                          q_[lo:hi, J * 512 + c0: J * 512 + c0 + 128],
                                                          start=True, stop=False),
                                 reads=rk, writes=[bank(zb + h)], signal=False)
                            P.op("pe", lambda t: t.matmul(ps[:, zb + h, c0:c0 + 128], ident, NEGS,
                                                          start=False, stop=True),
                                 reads=["cmat"], writes=[bank(zb + h)], signal=(h == 1 and c0 + 128 >= 512))
                            if c0 + 128 < 512:
                                P.op("pe", lambda t: t.matmul(ps[:, zb + h, c0 + 128:512], kT,
                                                              q_[lo:hi, J * 512 + c0 + 128: (J + 1) * 512],
                                                              start=True, stop=True),
                                     reads=rk, writes=[bank(zb + h)], signal=(h == 1))
                        else:
                            P.op("pe", lambda t: t.matmul(ps[:, zb + h, :], kT, q_[lo:hi, J * 512:(J + 1) * 512],
                                                          start=True, stop=True),
                                 reads=rk, writes=[bank(zb + h)], signal=(h == 1))

                def S2a(i):
                    J, kb, first, last = steps[i]
                    c0 = c0_of(J, kb)
                    zb = (i % 2) * 2
                    P.op("act", lambda a: a.activation(out=Eb[i % 3][:, :, c0:], in_=ps[:, zb:zb + 2, c0:],
                                                       func=AF.Exp, scale=0.125),
                         reads=[bank(zb), bank(zb + 1)], writes=[K("E", i % 3)])

                def S2b(i):
                    J, kb, first, last = steps[i]
                    c0 = c0_of(J, kb)
                    P.op("act", lambda a: a.activation(out=SQb[i % 2][:, :, c0:], in_=Eb[i % 3][:, :, c0:],
                                                       func=AF.Ln, bias=ONE, scale=1.0),
                         reads=[K("E", i % 3), "misc"], writes=[K("SQ", i % 2)])

                def S3(i):
                    J, kb, first, last = steps[i]
                    c0 = c0_of(J, kb)
                    if first:
                        for h in range(2):
                            P.op("pe", lambda t: t.matmul(ps[:, 4 + h, :], L1, zero_bf[:], start=True, stop=True),
                                 reads=["cmat", "zero"], writes=[bank(4 + h)], signal=False)
                    for h in range(2):
                        P.op("pe", lambda t: t.matmul(ps[:, 4 + h, c0:], L1, SQb[i % 2][:, h, c0:],
                                                      start=False, stop=True, skip_group_check=True),
                             reads=["cmat", K("SQ", i % 2)], writes=[bank(4 + h)], signal=(h == 1))

                def S4(i):
                    J, kb, first, last = steps[i]
                    c0 = c0_of(J, kb)
                    P.op("act", lambda a: a.activation(out=Fb[i % 2][:, :, c0:], in_=ps[:, 4:6, c0:], func=AF.Exp,
                                                       scale=-1.0),
                         reads=[bank(4), bank(5)], writes=[K("F", i % 2)])

                def S5(i):
                    J, kb, first, last = steps[i]
                    c0 = c0_of(J, kb)
                    P.op("dve", lambda v: v.tensor_tensor(out=Wb[i % 3][:, :, c0:], in0=Eb[i % 3][:, :, c0:],
                                                          in1=Fb[i % 2][:, :, c0:], op=ALU.mult),
                         reads=[K("E", i % 3), K("F", i % 2)], writes=[K("W", i % 3)])

                def S6(i):
                    J, kb, first, last = steps[i]
                    c0 = c0_of(J, kb)
                    if last:
                        return
                    for h in range(2):
                        P.op("pe", lambda t: t.matmul(ps[:, 4 + h, c0:], L2, SQb[i % 2][:, h, c0:],
                                                      start=False, stop=True, skip_group_check=True),
                             reads=["cmat", K("SQ", i % 2)], writes=[bank(4 + h)], signal=(h == 1))

                def S7(i):
                    J, kb, first, last = steps[i]
                    c0 = c0_of(J, kb)
                    if first:
                        for h in range(2):
                            P.op("pe", lambda t: t.matmul(ps[h * 64:(h + 1) * 64, 6, :],
                                                          vtok[par][:, 0, h * 64:(h + 1) * 64], zero_bf[:],
                                                          start=True, stop=False),
                                 reads=["zero", K("v", par, 0)], writes=[bank(6)], signal=False)
                    for h in range(2):
                        P.op("pe", lambda t: t.matmul(ps[h * 64:(h + 1) * 64, 6, c0:],
                                                      vtok[par][:, kb, h * 64:(h + 1) * 64],
                                                      Wb[i % 3][:, h, c0:], start=False, stop=last),
                             reads=[K("v", par, kb // 4), K("W", i % 3)], writes=[bank(6)], signal=(h == 1))
                    if last:
                        P.op("dve", lambda v: v.tensor_copy(out=mixT[:, g, J * 512:(J + 1) * 512], in_=ps[:, 6, :]),
                             reads=[bank(6)], writes=[K("mix", g, J)])

                for n in range(-2, N + 1):
                    if 0 <= n + 2 < N:
                        S1(n + 2)
                    if 0 <= n < N:
                        S4(n)
                        S5(n)
                    if 0 <= n + 1 < N:
                        S2b(n + 1)
                    if 0 <= n + 2 < N:
                        S2a(n + 2)
                    if 0 <= n < N:
                        S6(n)
                    if 0 <= n + 1 < N:
                        S3(n + 1)
                    if 0 <= n - 1 < N:
                        S7(n - 1)
                    bg_step()

            def diff_final(g, J, k2):
                o_ = Os[k2]
                rs = RSs[k2]
                f0, f1, f2 = fin
                kk = K("fin")
                hk, lk = K("hi"), K("lo")
                P.op("act", lambda a_: a_.activation(out=rs[0:64, :], in_=rs[0:64, :], func=AF.Ln),
                     reads=[K("RS", k2)], writes=[K("RS", k2)])
                P.op("act", lambda a_: a_.activation(out=rs[0:64, :], in_=rs[0:64, :], func=AF.Exp, scale=-1.0),
                     reads=[K("RS", k2)], writes=[K("RS", k2)])
                yield
                P.dma("sp", scrd[k2, 0:1, :], rs[0:1, :], reads=[K("RS", k2)], writes=["scr%d0" % k2], slot="bw0")
                P.dma("sp", scrd[k2, 1:2, :], rs[32:33, :], reads=[K("RS", k2)], writes=["scr%d1" % k2], slot="bw1")
                yield
                yield
                yield
                P.dma("sp", f1, scrd[k2, 0:1, :].broadcast_to([128, 512]), reads=["scr%d0" % k2],
                      writes=[kk + "1"], slot="bc0")
                P.dma("sp", f2, scrd[k2, 1:2, :].broadcast_to([128, 512]), reads=["scr%d1" % k2],
                      writes=[kk + "2"], slot="bc1")
                yield
                yield
                yield
                P.op("pool", lambda g_: g_.tensor_tensor(out=f1, in0=o_[:, 0, :], in1=f1, op=ALU.mult),
                     reads=[K("Os", k2), kk + "1"], writes=[kk + "1"])
                P.op("pool", lambda g_: g_.tensor_tensor(out=f2, in0=o_[:, 1, :], in1=f2, op=ALU.mult),
                     reads=[K("Os", k2), kk + "2"], writes=[kk + "2"])
                yield
                os_busy[k2] = False
                yield
                P.op("dve", lambda v: v.scalar_tensor_tensor(out=f0, in0=f2, scalar=NLAM, in1=f1, op0=ALU.mult,
                                                             op1=ALU.add),
                     reads=[kk + "1", kk + "2", "misc5"], writes=[kk + "0"])
                yield
                P.op("pool", lambda g_: g_.tensor_tensor(out=f1, in0=f0, in1=f0, op=ALU.mult),
                     reads=[kk + "0"], writes=[kk + "1"])
                P.op("pool", lambda g_: g_.tensor_copy(out=hib, in_=f1), reads=[kk + "1"], writes=[hk])
                P.op("pool", lambda g_: g_.tensor_tensor(out=lob, in0=f1, in1=hib, op=ALU.subtract),
                     reads=[kk + "1", hk], writes=[lk])
                yield
                yield
                yield
                b7["req"] = True
                while not b7["clean"]:
                    yield
                P.op("pe", lambda t: t.matmul(ps[:, 7, :], C128, hib, start=True, stop=False),
                     reads=["cmat", hk], writes=[bank(7)], signal=False)
                P.op("pe", lambda t: t.matmul(ps[:, 7, :], C128, lob, start=False, stop=True),
                     reads=["cmat", lk], writes=[bank(7)])
                P.op("dve", lambda v: v.tensor_copy(out=f2, in_=ps[:, 7, :]),
                     reads=[bank(7)], writes=[kk + "2"])
                b7["req"] = False
                yield
                yield
                P.op("act", lambda a: a.activation(out=f2, in_=f2, func=AF.Ln, bias=EPS5, scale=1.0),
                     reads=[kk + "2", "misc"], writes=[kk + "2"])
                P.op("act", lambda a: a.activation(out=f2, in_=f2, func=AF.Exp, scale=-0.5),
                     reads=[kk + "2"], writes=[kk + "2"])
                yield
                P.op("dve", lambda v: v.scalar_tensor_tensor(out=mixT[:, g, J * 512:(J + 1) * 512], in0=f0,
                                                             scalar=subg[:, 0:1], in1=f2, op0=ALU.mult, op1=ALU.mult),
                     reads=[kk + "0", kk + "2", "subg"], writes=[K("mix", g, J)])
                yield

            def diff_attention(g, par):
                q_ = qkT[par][:, 0, :]
                k_ = qkT[par][:, 1, :]
                steps = []
                for J in range(NJ):
                    nkb = 4 * J + 4
                    for kb in range(nkb):
                        steps.append((J, kb, kb == 0, kb == nkb - 1))
                N = len(steps)
                Pb = Wb

                def c0_of(J, kb):
                    return max(0, kb * 128 - 512 * J)

                def D1(i):
                    J, kb, first, last = steps[i]
                    c0 = c0_of(J, kb)
                    zb = (i % 2) * 2
                    rk = [K("qk", par, J), K("qk", par, kb // 4)]
                    for m in range(2):
                        lo, hi = m * 64, (m + 1) * 64
                        kT = k_[lo:hi, kb * 128:(kb + 1) * 128]
                        P.op("pe", lambda t: t.matmul(ps[:, zb + m, c0:], kT, q_[lo:hi, J * 512 + c0:(J + 1) * 512],
                                                      start=True, stop=True),
                             reads=rk, writes=[bank(zb + m)], signal=(m == 1))

                def D2(i):
                    J, kb, first, last = steps[i]
                    c0 = c0_of(J, kb)
                    zb = (i % 2) * 2
                    P.op("act", lambda a: a.activation(out=Pb[i % 3][:, :, c0:], in_=ps[:, zb:zb + 2, c0:],
                                                       func=AF.Exp, scale=0.125),
                         reads=[bank(zb), bank(zb + 1)], writes=[K("W", i % 3)])
                    if kb >= 4 * J:
                        pd = Pb[i % 3][:, :, c0:c0 + 128]
                        P.op("pool", lambda g_: g_.tensor_tensor(out=pd, in0=pd,
                                                                 in1=M01.unsqueeze(1).broadcast_to([128, 2, 128]),
                                                                 op=ALU.mult),
                             reads=[K("W", i % 3), "cmat"], writes=[K("W", i % 3)])

                def D3(i):
                    J, kb, first, last = steps[i]
                    c0 = c0_of(J, kb)
                    p_ = Pb[i % 3]
                    for m in range(2):
                        P.op("pe", lambda t: t.matmul(ps[:, 4 + m, c0:], vtok[par][:, kb, :], p_[:, m, c0:],
                                                      start=first, stop=last),
                             reads=[K("v", par, kb // 4), K("W", i % 3)], writes=[bank(4 + m)], signal=False)
                    for m in range(2):
                        P.op("pe", lambda t: t.matmul(ps[32 * m:32 * m + 32, 6, c0:], ONES[:, 0:32], p_[:, m, c0:],
                                                      start=first, stop=last),
                             reads=["cmat", K("W", i % 3)], writes=[bank(6)], signal=(m == 1))
                    if last:
                        k2 = J % 2
                        while os_busy[k2]:
                            bg_step()
                        P.op("act", lambda a: a.activation(out=Os[k2], in_=ps[:, 4:6, :], func=AF.Copy),
                             reads=[bank(4), bank(5)], writes=[K("Os", k2)])
                        P.op("dve", lambda v: v.tensor_copy(out=RSs[k2][0:64, :], in_=ps[0:64, 6, :]),
                             reads=[bank(6)], writes=[K("RS", k2)])
                        os_busy[k2] = True
                        finals.append((g, J, k2))

                for n in range(-1, N + 1):
                    if 0 <= n + 1 < N:
                        D1(n + 1)
                    if 0 <= n < N:
                        D2(n)
                    if 0 <= n - 1 < N:
                        D3(n - 1)
                    bg_step()

            start_group(0)
            bg_drain()
            closing[0] = False
            bg.append(final_worker())
            for pos, g in enumerate(ORDER):
                if pos + 1 < 8:
                    start_group(pos + 1)
                if g < 4:
                    sb_attention(g, pos % 2)
                else:
                    diff_attention(g, pos % 2)
                drain_inproj()
            closing[0] = True
            bg_drain()
            P.barrier()

            AR.reset()
            x1 = AR.alloc([8, 1024], F32)
            h2T = AR.alloc([8, TT], BF16)
            aT = AR.alloc([NFH, TT], BF16)
            wdb = AR.alloc([NFH, 1024], BF16)
            NRING = 3
            wgu = [AR.alloc([2, 8, 128], BF16) for _ in range(NRING)]
            xs2 = [AR.alloc([1024], F32) for _ in range(2)]
            xn2 = [AR.alloc([1024], BF16) for _ in range(3)]
            sg = [AR.alloc([2, 512], F32) for _ in range(2)]
            sqj2 = AR.alloc([1024], BF16)
            stat2 = AR.alloc([8, 4], F32)

            for tt in range(S // TT):
                tk = K
                trow = row0 + tt * TT
                def issue_gu(gf):
                    P.dma("pool", wgu[gf % NRING].rearrange("p a b c -> p (a b c)"), wgud[gf],
                          writes=[tk("wgu", gf % NRING)], slot="wgu%d" % (gf % NRING))

                def issue_wd(half):
                    for f in range(NFH):
                        lt = P.dma("pool", wdb[:, f, :], wdd[half * NFH + f], writes=[tk("wd", f)], slot="wd")
                    for f in range(NFH):
                        P.last_w[tk("wd", f)] = lt

                for f in range(NRING):
                    issue_gu(f)
                issue_wd(0)

                def c1_post(i):
                    for c in range(8):
                        P.op("pe", lambda t, c=c: t.transpose(ps_bf7[:, c * 128:(c + 1) * 128],
                                                              xn2[i % 3][:, c * 128:(c + 1) * 128], ident),
                             reads=[tk("xn2", i % 3), "cmat"], writes=[bank(7)], signal=(c == 7))
                    P.op("act", lambda a: a.activation(out=h2T[:, :, i * 128:(i + 1) * 128],
                                                       in_=ps_bf7.rearrange("p (c n) -> p c n", c=8), func=AF.Copy),
                         reads=[bank(7)], writes=[tk("h2T", i)])

                for i in range(8):
                    Jg = (tt * TT + i * 128) // 512
                    col = tt * TT + i * 128
                    yb = (i % 2) * 2
                    P.dma("sp", xs2[i % 2], xd[trow + i * 128: trow + (i + 1) * 128, :], writes=[tk("xs2", i % 2)],
                          slot="xs%d" % (i % 2))
                    for hh in range(2):
                        for kc in range(8):
                            P.op("pe", lambda t, hh=hh, kc=kc: t.matmul(
                                ps[:, yb + hh, :], mixT[:, kc, col:col + 128], wo[:, kc, hh * 512:(hh + 1) * 512],
                                start=(kc == 0), stop=(kc == 7)),
                                 reads=[K("mix", kc, Jg), "wo"], writes=[bank(yb + hh)],
                                 signal=(kc == 7 and hh == 1))
                    P.op("dve", lambda v: v.tensor_tensor(out=x1[:, i, :].rearrange("p (a n) -> p a n", a=2),
                                                          in0=ps[:, yb:yb + 2, :],
                                                          in1=xs2[i % 2].rearrange("p (a n) -> p a n", a=2),
                                                          op=ALU.add),
                         reads=[bank(yb), bank(yb + 1), tk("xs2", i % 2)], writes=[tk("x1", i)])
                    P.op("act", lambda a: a.activation(out=sqj2, in_=x1[:, i, :], func=AF.Square,
                                                       accum_out=stat2[:, i, 0:1]),
                         reads=[tk("x1", i)], writes=[tk("sqj2"), tk("st2", i)])
                    rstd_from_ss(stat2[:, i, 0:1], stat2[:, i, 1:2], D, EPS6, tk("st2", i), tk("rs2", i))
                    P.op("dve", lambda v: v.scalar_tensor_tensor(
                        out=xn2[i % 3], in0=x1[:, i, :], scalar=stat2[:, i, 1:2], in1=small[:, SP_FG:SP_FG + 1024],
                        op0=ALU.mult, op1=ALU.mult),
                         reads=[tk("x1", i), tk("rs2", i), "small"], writes=[tk("xn2", i % 3)])
                    if i >= 2:
                        c1_post(i - 2)
                c1_post(6)
                c1_post(7)

                for half in range(2):
                    for f in range(NFH):
                        gf = half * NFH + f
                        slot = gf % NRING
                        gb_ = (gf % 2) * 2
                        ub_ = 4 + (gf % 2) * 2
                        for gu, bb in ((0, gb_), (1, ub_)):
                            for hh in range(2):
                                for kc in range(8):
                                    P.op("pe", lambda t, gu=gu, bb=bb, hh=hh, kc=kc: t.matmul(
                                        ps[:, bb + hh, :], wgu[slot][:, gu, kc, :], h2T[:, kc, hh * 512:(hh + 1) * 512],
                                        start=(kc == 0), stop=(kc == 7)),
                                         reads=[tk("wgu", slot)] + [tk("h2T", 4 * hh + q) for q in range(4)],
                                         writes=[bank(bb + hh)], signal=(kc == 7 and hh == 1))
                        P.op("act", lambda a: a.activation(out=sg[gf % 2], in_=ps[:, gb_:gb_ + 2, :], func=AF.Silu),
                             reads=[bank(gb_), bank(gb_ + 1)], writes=[tk("sg", gf % 2)])
                        P.op("dve", lambda v: v.tensor_tensor(out=aT[:, f, :].rearrange("p (a n) -> p a n", a=2),
                                                              in0=ps[:, ub_:ub_ + 2, :], in1=sg[gf % 2], op=ALU.mult),
                             reads=[bank(ub_), bank(ub_ + 1), tk("sg", gf % 2)], writes=[tk("aT", f)])
                        if f + NRING < NFH:
                            issue_gu(gf + NRING)
                    for i in range(8):
                        ob = (i % 2) * 2
                        for hh in range(2):
                            for f in range(NFH):
                                P.op("pe", lambda t, hh=hh, f=f: t.matmul(
                                    ps[:, ob + hh, :], aT[:, f, i * 128:(i + 1) * 128],
                                    wdb[:, f, hh * 512:(hh + 1) * 512], start=(f == 0), stop=(f == NFH - 1)),
                                     reads=[tk("aT", f), tk("wd", f)], writes=[bank(ob + hh)],
                                     signal=(f == NFH - 1 and hh == 1))
                        P.op("dve", lambda v: v.tensor_tensor(out=x1[:, i, :].rearrange("p (a n) -> p a n", a=2),
                                                              in0=ps[:, ob:ob + 2, :],
                                                              in1=x1[:, i, :].rearrange("p (a n) -> p a n", a=2),
                                                              op=ALU.add),
                             reads=[bank(ob), bank(ob + 1), tk("x1", i)], writes=[tk("x1", i)])
                        if half == 1:
                            P.dma("act", yd[trow + i * 128: trow + (i + 1) * 128, :], x1[:, i, :],
                                  reads=[tk("x1", i)], slot="out%d" % i, is_out=True)
                    if half == 0:
                        for f in range(NRING):
                            issue_gu(NFH + f)
                        issue_wd(1)
            P.barrier()
        P.barrier()
        P.finish()
    return nc


def _host_constants():
    j = np.arange(128)
    ident = np.eye(128, dtype=np.float32)
    L1 = (j[:, None] >= j[None, :]).astype(np.float32)
    L2 = (j[:, None] < j[None, :]).astype(np.float32)
    negs = np.where(j[:, None] >= j[None, :], NEG, 0.0).astype(np.float32)
    negi = np.where(j[:, None] > j[None, :], NEG, 0.0).astype(np.float32)
    ones = np.ones((128, 128), np.float32)
    m01 = (j[:, None] <= j[None, :]).astype(np.float32)
    cmat = np.concatenate([ident, L1, L2, negs, negi, ones, ones / 32.0, ones / 128.0, m01], axis=1)
    half = HD // 2
    inv_freq = (np.float32(10000.0) ** (-np.arange(half, dtype=np.float32) / np.float32(half))).astype(np.float32)
    pos = np.arange(S, dtype=np.float32)
    ang = (pos[:, None] * inv_freq[None, :]).astype(np.float32)
    cos, sin = np.cos(ang.astype(np.float64)).astype(np.float32), np.sin(ang.astype(np.float64)).astype(np.float32)
    CC = np.concatenate([cos, cos], axis=1).reshape(NT, 128, 64).transpose(1, 0, 2)
    SS = np.concatenate([-sin, sin], axis=1).reshape(NT, 128, 64).transpose(1, 0, 2)
    rope = np.stack([CC, SS], axis=1).reshape(128, 2 * NT * 64).astype(np.float32)
    return np.ascontiguousarray(cmat), np.ascontiguousarray(rope)


def _prep_weights(inp):
    w_in = np.asarray(inp["w_in"], np.float32)[0]
    groups = []
    for g in range(8):
        if g < 4:
            cols = np.r_[128 * g:128 * g + 128, 512 + 128 * g:512 + 128 * g + 128,
                         1024 + 128 * g:1024 + 128 * g + 128]
        else:
            h = g - 4
            cols = np.r_[1536 + 128 * h:1536 + 128 * h + 128, 2048 + 128 * h:2048 + 128 * h + 128,
                         2560 + 128 * h:2560 + 128 * h + 128]
        wg = w_in[:, cols].reshape(8, 128, 384).transpose(1, 0, 2).reshape(128, 8 * 384)
        groups.append(wg)
    wing = np.ascontiguousarray(np.stack(groups, 0))
    wo = np.asarray(inp["w_o"], np.float32)[0].reshape(8, 128, 1024).transpose(1, 0, 2).reshape(128, 8 * 1024)
    wg_ = np.asarray(inp["w_gate"], np.float32)[0].reshape(8, 128, NF, 128)
    wu_ = np.asarray(inp["w_up"], np.float32)[0].reshape(8, 128, NF, 128)
    wgu = np.stack([wg_, wu_], 0).transpose(3, 2, 0, 1, 4).reshape(NF, 128, 2 * 8 * 128)
    wd = np.asarray(inp["w_down"], np.float32)[0].reshape(NF, 128, 1024)
    gq = np.asarray(inp["diff_q_norm_g"], np.float32)[0]
    gk = np.asarray(inp["diff_k_norm_g"], np.float32)[0]
    sw = lambda v: np.concatenate([v[32:], v[:32]])
    small = np.concatenate([
        np.asarray(inp["attn_norm_g"], np.float32)[0], np.asarray(inp["ffn_norm_g"], np.float32)[0],
        gq, sw(gq), gk, sw(gk),
        np.asarray(inp["lambda_q1"], np.float32)[0], np.asarray(inp["lambda_k1"], np.float32)[0],
        np.asarray(inp["lambda_q2"], np.float32)[0], np.asarray(inp["lambda_k2"], np.float32)[0]])
    small = np.ascontiguousarray(np.broadcast_to(small[None, :], (128, SP_N)))
    subln = np.ascontiguousarray(np.asarray(inp["diff_subln_g"], np.float32)[0].reshape(128, 1))
    return dict(wing=wing, wo=np.ascontiguousarray(wo), wgu=np.ascontiguousarray(wgu), wd=np.ascontiguousarray(wd),
                small=small, subln=subln)


def kernel(**inputs):
    x = np.asarray(inputs["x"], np.float32)
    wmaps = _prep_weights(inputs)
    cmat, rope = _host_constants()
    nc = build_program(NSEQ)
    in_maps = []
    for c in range(NCORES):
        m = dict(wmaps)
        m["x"] = np.ascontiguousarray(x[c * NSEQ:(c + 1) * NSEQ].reshape(NSEQ * S, D))
        m["cmat"] = cmat
        m["rope"] = rope
        in_maps.append(m)
    res = run_bass_kernel_spmd(nc, in_maps, core_ids=list(range(NCORES)))
    out = np.concatenate([np.asarray(r["y"], np.float32).reshape(NSEQ, S, D) for r in res.results], axis=0)
    return out
```
